# Optimizing a Trainium2 kernel written in Bass

```python
import math
import jax
import jax.numpy as jnp
from jax import lax
import numpy as np

D_MODEL = 2048
BATCH = 4
SEQ = 2048
DEPTH = 2
DEC_BATCH = 128
DEC_SEQ = 1
PAST_LEN = 2048
PAGE_SIZE = 128

M_WIDTH = 3 * D_MODEL // 8
M_HEADS = 4
M_DV = M_WIDTH // M_HEADS
M_DK = M_DV // 2
M_QK = M_HEADS * M_DK
M_CHUNK = 64
CONV_WIDTH = D_MODEL // 4
CONV_K = 3
A_GROUPS = ((128, 1), (512, 4), (2048, 16))
A_HPG = 4
A_HEAD_DIM = 64
A_HEADS = A_HPG * len(A_GROUPS)
A_WIDTH = A_HEADS * A_HEAD_DIM
A_OUT = A_HPG * A_HEAD_DIM
A_BLOCK = 128
ROPE_THETA = 10000.0
N_BRANCH = 3
D_FF = ((8 * D_MODEL // 3 + 255) // 256) * 256
LN_EPS = 1e-5
ALPHA = (2 * DEPTH) ** 0.25
BETA = (8 * DEPTH) ** -0.25
IN_SIZES = (M_QK, M_QK, M_WIDTH, M_HEADS, M_HEADS, M_WIDTH,
            CONV_WIDTH, CONV_WIDTH, CONV_WIDTH,
            A_WIDTH, A_WIDTH, A_WIDTH,
            N_BRANCH * D_MODEL)
IN_WIDTH = sum(IN_SIZES)
IN_OFFSETS = tuple(int(o) for o in np.cumsum(IN_SIZES)[:-1])

kernel_name = 'hybrid_mlstm_shortconv_dilattn_decoder_step'

F32 = jnp.float32


def layer_norm(x, g, b):
    xf = x.astype(F32)
    mu = jnp.mean(xf, axis=-1, keepdims=True)
    var = jnp.mean(jnp.square(xf - mu), axis=-1, keepdims=True)
    return ((xf - mu) * lax.rsqrt(var + LN_EPS) * g + b).astype(x.dtype)


def head_norm(h, g):
    mu = jnp.mean(h, axis=-1, keepdims=True)
    var = jnp.mean(jnp.square(h - mu), axis=-1, keepdims=True)
    return (h - mu) * lax.rsqrt(var + LN_EPS) * g.reshape(h.shape[-2], h.shape[-1])


def swiglu(x, w_in, w_out):
    a, b = jnp.split(x @ w_in, 2, axis=-1)
    return (jax.nn.silu(a) * b) @ w_out


def rope(x, pos):
    dh = x.shape[-1]
    half = dh // 2
    inv = ROPE_THETA ** (-(2.0 * jnp.arange(half, dtype=F32)) / dh)
    ang = pos.astype(F32)[:, None] * inv[None, :]
    cos = jnp.cos(ang)[None, :, None, :]
    sin = jnp.sin(ang)[None, :, None, :]
    xf = x.astype(F32)
    x1, x2 = xf[..., :half], xf[..., half:]
    return jnp.concatenate([x1 * cos - x2 * sin, x2 * cos + x1 * sin], axis=-1)


def mlstm(q, k, v, i_pre, f_pre, C0, n0, m0):
    Bn, S, H, DK = q.shape
    L = math.gcd(S, M_CHUNK)
    nc = S // L
    k = k * DK ** -0.5
    log_f = jax.nn.log_sigmoid(f_pre)

    def chunks(a):
        a = a.reshape((Bn, nc, L, H) + a.shape[3:])
        return jnp.moveaxis(a, (1, 3), (0, 2))

    causal = jnp.tril(jnp.ones((L, L), dtype=bool))

    def step(carry, inp):
        C, n, m = carry
        qc, kc, vc, ic, fc = inp
        b = jnp.cumsum(fc, axis=-1)
        d_log = jnp.where(causal, b[..., :, None] - b[..., None, :] + ic[..., None, :], -jnp.inf)
        inter = b + m[..., None]
        m_t = jnp.maximum(jnp.max(d_log, axis=-1), inter)
        s = jnp.einsum('bhtd,bhsd->bhts', qc, kc) * jnp.exp(d_log - m_t[..., None])
        w_inter = jnp.exp(inter - m_t)
        num = jnp.einsum('bhts,bhsv->bhtv', s, vc) + w_inter[..., None] * jnp.einsum('bhtd,bhdv->bhtv', qc, C)
        den = jnp.sum(s, axis=-1) + w_inter * jnp.einsum('bhtd,bhd->bht', qc, n)
        h = num / jnp.maximum(jnp.abs(den), jnp.exp(-m_t))[..., None]
        b_last = b[..., -1]
        g = b_last[..., None] - b + ic
        m_new = jnp.maximum(b_last + m, jnp.max(g, axis=-1))
        w_k = jnp.exp(g - m_new[..., None])
        decay = jnp.exp(b_last + m - m_new)
        C_new = decay[..., None, None] * C + jnp.einsum('bhs,bhsd,bhsv->bhdv', w_k, kc, vc)
        n_new = decay[..., None] * n + jnp.einsum('bhs,bhsd->bhd', w_k, kc)
        return (C_new, n_new, m_new), h

    carry0 = (C0.astype(F32), n0.astype(F32), m0.astype(F32))
    (C1, n1, m1), h = lax.scan(step, carry0, (chunks(q), chunks(k), chunks(v), chunks(i_pre), chunks(log_f)))
    h = jnp.moveaxis(h, (0, 2), (1, 3)).reshape(Bn, S, H, v.shape[-1])
    return h, (C1, n1, m1)


def dilated_attn_prompt(q, kv, window, dil):
    Bn, S, H, Dh = q.shape
    L = S // dil
    w = window // dil
    Q = math.gcd(L, A_BLOCK)
    nb = L // Q

    def streams(a):
        return a.astype(F32).reshape(Bn, L, dil, H, Dh).transpose(0, 2, 3, 1, 4)

    qs = streams(q).reshape(Bn, dil, H, nb, Q, Dh)
    pad = ((0, 0), (0, 0), (0, 0), (w, 0), (0, 0))
    idx = jnp.arange(nb)[:, None] * Q + jnp.arange(Q + w)[None, :]
    kb = jnp.pad(streams(kv[:, :, 0]), pad)[:, :, :, idx]
    vb = jnp.pad(streams(kv[:, :, 1]), pad)[:, :, :, idx]
    s = jnp.einsum('brhnqd,brhnkd->brhnqk', qs, kb) * Dh ** -0.5
    qi = jnp.arange(Q)[:, None]
    kj = jnp.arange(Q + w)[None, :]
    kpos = jnp.arange(nb)[:, None, None] * Q + kj[None] - w
    valid = (kj >= qi)[None] & (kj <= qi + w)[None] & (kpos >= 0)
    s = jnp.where(valid, s, -jnp.inf)
    lse = jax.nn.logsumexp(s, axis=-1)
    o = jnp.einsum('brhnqk,brhnkd->brhnqd', jnp.exp(s - lse[..., None]), vb)
    o = o.reshape(Bn, dil, H, L, Dh).transpose(0, 3, 1, 2, 4).reshape(Bn, S, H, Dh)
    lse = lse.reshape(Bn, dil, H, L).transpose(0, 3, 1, 2).reshape(Bn, S, H)
    return o, lse


def dilated_attn_step(q, kv_new, kv_buf, window, dil):
    Bn, T, H, Dh = q.shape
    Wb = kv_buf.shape[1]
    kv = jnp.concatenate([kv_buf.astype(F32), kv_new.astype(F32)], axis=1)
    nk = window // dil + 1
    idx = Wb + jnp.arange(T)[:, None] - dil * jnp.arange(nk)[None, :]
    valid = idx >= 0
    kvg = kv[:, jnp.maximum(idx, 0)]
    s = jnp.einsum('bthd,btkhd->bthk', q, kvg[:, :, :, 0]) * Dh ** -0.5
    s = jnp.where(valid[None, :, None, :], s, -jnp.inf)
    lse = jax.nn.logsumexp(s, axis=-1)
    o = jnp.einsum('bthk,btkhd->bthd', jnp.exp(s - lse[..., None]), kvg[:, :, :, 1])
    return o, lse


def token_mix(h, pos, conv_prev, C0, n0, m0, kv_bufs, w_in, b_if, m_norm_g, conv_w,
              w_up_m, w_up_c, w_up_a, w_o):
    Bn, S, _ = h.shape
    (mq, mk, mv, mi, mf, mo, cb, cc, ch, aq, ak, av, gt) = jnp.split(h @ w_in, list(IN_OFFSETS), axis=-1)

    gif = jnp.concatenate([mi, mf], axis=-1).astype(F32) + b_if
    hm, (C1, n1, m1) = mlstm(mq.reshape(Bn, S, M_HEADS, M_DK).astype(F32),
                             mk.reshape(Bn, S, M_HEADS, M_DK).astype(F32),
                             mv.reshape(Bn, S, M_HEADS, M_DV).astype(F32),
                             gif[..., :M_HEADS], gif[..., M_HEADS:], C0, n0, m0)
    o_gate = jax.nn.sigmoid(mo.astype(F32)).reshape(Bn, S, M_HEADS, M_DV)
    hm = (o_gate * head_norm(hm, m_norm_g)).reshape(Bn, S, M_WIDTH).astype(h.dtype)

    u = cc * ch
    u_ext = jnp.concatenate([conv_prev.astype(u.dtype), u], axis=1)
    taps = jnp.stack([u_ext[:, j:j + S] for j in range(CONV_K)], axis=2)
    yc = cb * jnp.einsum('bskc,kc->bsc', taps, conv_w)
    conv_new = u_ext[:, S:]

    qa = rope(aq.reshape(Bn, S, A_HEADS, A_HEAD_DIM), pos)
    ka = rope(ak.reshape(Bn, S, A_HEADS, A_HEAD_DIM), pos)
    va = av.reshape(Bn, S, A_HEADS, A_HEAD_DIM).astype(F32)
    outs, lses, kv_out = [], [], []
    for g, (win, dil) in enumerate(A_GROUPS):
        sl = slice(g * A_HPG, (g + 1) * A_HPG)
        kv_g = jnp.stack([ka[:, :, sl], va[:, :, sl]], axis=2).astype(h.dtype)
        if kv_bufs is None:
            o, l = dilated_attn_prompt(qa[:, :, sl], kv_g, win, dil)
            kv_out.append(kv_g[:, -min(win, S):])
        else:
            o, l = dilated_attn_step(qa[:, :, sl], kv_g, kv_bufs[g], win, dil)
            kv_out.append(kv_g)
        outs.append(o)
        lses.append(l)
    alpha = jax.nn.softmax(jnp.stack(lses), axis=0)
    oa = jnp.einsum('gbsh,gbshd->bshd', alpha, jnp.stack(outs)).reshape(Bn, S, A_OUT).astype(h.dtype)

    gates = jax.nn.sigmoid(gt.astype(F32)).reshape(Bn, S, N_BRANCH, D_MODEL).astype(h.dtype)
    merged = (gates[:, :, 0] * (hm @ w_up_m) + gates[:, :, 1] * (yc @ w_up_c)
              + gates[:, :, 2] * (oa @ w_up_a))
    return merged @ w_o, (C1, n1, m1, conv_new, kv_out[0], kv_out[1], kv_out[2])


def decoder_layer(x, pos, conv_prev, C0, n0, m0, kv_bufs, w_in, b_if, m_norm_g, conv_w,
                  w_up_m, w_up_c, w_up_a, w_o, w_ffn_in, w_ffn_out, ln_g, ln_b):
    x = layer_norm(ALPHA * x + 0.5 * swiglu(x, w_ffn_in[0], w_ffn_out[0]), ln_g[0], ln_b[0])
    mix, st = token_mix(x, pos, conv_prev, C0, n0, m0, kv_bufs, w_in, b_if, m_norm_g, conv_w,
                        w_up_m, w_up_c, w_up_a, w_o)
    x = layer_norm(ALPHA * x + mix, ln_g[1], ln_b[1])
    x = layer_norm(ALPHA * x + 0.5 * swiglu(x, w_ffn_in[1], w_ffn_out[1]), ln_g[2], ln_b[2])
    return x, st


def setup_inputs(seed: int = 0) -> dict:
    key = jax.random.key(seed)
    ks = jax.random.split(key, 24)
    nrm = jax.random.normal

    def kv_cache(k, win):
        return nrm(k, (DEPTH, DEC_BATCH, min(win, PAST_LEN), 2, A_HPG, A_HEAD_DIM), F32)

    b_i = 0.1 * nrm(ks[9], (DEPTH, M_HEADS), F32)
    b_f = jnp.linspace(3.0, 6.0, M_HEADS, dtype=F32)[None, :] + 0.1 * nrm(ks[10], (DEPTH, M_HEADS), F32)
    return {
        'x_prompt': nrm(ks[0], (BATCH, SEQ, D_MODEL), F32),
        'x_sample': nrm(ks[1], (DEC_BATCH, DEC_SEQ, D_MODEL), F32),
        'state_mlstm_C': 0.2 * nrm(ks[2], (DEPTH, DEC_BATCH, M_HEADS, M_DK, M_DV), F32),
        'state_mlstm_n': 0.2 * nrm(ks[3], (DEPTH, DEC_BATCH, M_HEADS, M_DK), F32),
        'state_mlstm_m': 0.5 * nrm(ks[4], (DEPTH, DEC_BATCH, M_HEADS), F32),
        'state_conv': nrm(ks[5], (DEPTH, DEC_BATCH, CONV_K - 1, CONV_WIDTH), F32),
        'cache_attn_kv_w128': kv_cache(ks[6], A_GROUPS[0][0]),
        'cache_attn_kv_w512': kv_cache(ks[7], A_GROUPS[1][0]),
        'cache_attn_kv_w2048': kv_cache(ks[8], A_GROUPS[2][0]),
        'w_in': nrm(ks[11], (DEPTH, D_MODEL, IN_WIDTH), F32) * D_MODEL ** -0.5,
        'b_gate_if': jnp.concatenate([b_i, b_f], axis=-1),
        'mlstm_norm_g': 1.0 + 0.02 * nrm(ks[12], (DEPTH, M_WIDTH), F32),
        'conv_w': nrm(ks[13], (DEPTH, CONV_K, CONV_WIDTH), F32) * CONV_K ** -0.5,
        'w_up_mlstm': nrm(ks[14], (DEPTH, M_WIDTH, D_MODEL), F32) * M_WIDTH ** -0.5,
        'w_up_conv': nrm(ks[15], (DEPTH, CONV_WIDTH, D_MODEL), F32) * CONV_WIDTH ** -0.5,
        'w_up_attn': nrm(ks[16], (DEPTH, A_OUT, D_MODEL), F32) * A_OUT ** -0.5,
        'w_o': nrm(ks[17], (DEPTH, D_MODEL, D_MODEL), F32) * (D_MODEL ** -0.5 * BETA),
        'w_ffn_in': nrm(ks[18], (DEPTH, 2, D_MODEL, 2 * D_FF), F32) * D_MODEL ** -0.5,
        'w_ffn_out': nrm(ks[19], (DEPTH, 2, D_FF, D_MODEL), F32) * (D_FF ** -0.5 * BETA),
        'ln_g': 1.0 + 0.02 * nrm(ks[20], (DEPTH, 3, D_MODEL), F32),
        'ln_b': 0.02 * nrm(ks[21], (DEPTH, 3, D_MODEL), F32),
    }


def reference(x_prompt, x_sample, state_mlstm_C, state_mlstm_n, state_mlstm_m, state_conv,
              cache_attn_kv_w128, cache_attn_kv_w512, cache_attn_kv_w2048,
              w_in, b_gate_if, mlstm_norm_g, conv_w, w_up_mlstm, w_up_conv, w_up_attn, w_o,
              w_ffn_in, w_ffn_out, ln_g, ln_b):
    Bp, Sp, _ = x_prompt.shape
    Ts = x_sample.shape[1]
    pos_p = jnp.arange(Sp)
    pos_s = PAST_LEN + jnp.arange(Ts)
    y_prompt, y_sample = x_prompt, x_sample
    prompt_states, sample_states = [], []
    for l in range(DEPTH):
        weights = (w_in[l], b_gate_if[l], mlstm_norm_g[l], conv_w[l], w_up_mlstm[l], w_up_conv[l],
                   w_up_attn[l], w_o[l], w_ffn_in[l], w_ffn_out[l], ln_g[l], ln_b[l])
        conv0 = jnp.zeros((Bp, CONV_K - 1, CONV_WIDTH), x_prompt.dtype)
        C0 = jnp.zeros((Bp, M_HEADS, M_DK, M_DV), F32)
        n0 = jnp.zeros((Bp, M_HEADS, M_DK), F32)
        m0 = jnp.zeros((Bp, M_HEADS), F32)
        y_prompt, st_p = decoder_layer(y_prompt, pos_p, conv0, C0, n0, m0, None, *weights)
        bufs = (cache_attn_kv_w128[l], cache_attn_kv_w512[l], cache_attn_kv_w2048[l])
        y_sample, st_s = decoder_layer(y_sample, pos_s, state_conv[l], state_mlstm_C[l], state_mlstm_n[l],
                                       state_mlstm_m[l], bufs, *weights)
        prompt_states.append(st_p)
        sample_states.append(st_s)
    p_C, p_n, p_m, p_conv, p_kv128, p_kv512, p_kv2048 = [jnp.stack(z) for z in zip(*prompt_states)]
    s_C, s_n, s_m, s_conv, s_kv128, s_kv512, s_kv2048 = [jnp.stack(z) for z in zip(*sample_states)]
    return (y_prompt, y_sample, p_C, p_n, p_m, p_conv, p_kv128, p_kv512, p_kv2048,
            s_C, s_n, s_m, s_conv, s_kv128, s_kv512, s_kv2048)
```

```python
import contextlib
import numpy as np
import concourse.bass as bass
import concourse.mybir as mybir

F32 = mybir.dt.float32
BF16 = mybir.dt.bfloat16
AF = mybir.ActivationFunctionType
ALU = mybir.AluOpType
AX = mybir.AxisListType

ENGS = ("pe", "act", "dve", "pool", "sp")
NDMA_SEMS = 8


class Op:
    __slots__ = ("eng", "fn", "deps", "is_dma", "inc", "sem", "val", "qwait")

    def __init__(self, eng, fn, is_dma):
        self.eng = eng
        self.fn = fn
        self.is_dma = is_dma
        self.deps = []
        self.inc = False
        self.sem = None
        self.val = 0
        self.qwait = None


class Prog:
    def __init__(self, nc):
        self.nc = nc
        self.ops = []
        self.last_w = {}
        self.readers = {}
        self.stack = contextlib.ExitStack()
        self.n_sb = 0

    def sb(self, shape, dt, name=None):
        self.n_sb += 1
        return self.stack.enter_context(self.nc.sbuf_tensor(name or f"sb{self.n_sb}", list(shape), dt))

    def ps(self, shape, dt=F32, name=None):
        self.n_sb += 1
        return self.stack.enter_context(self.nc.psum_tensor(name or f"ps{self.n_sb}", list(shape), dt))

    def _add(self, eng, fn, reads, writes, is_dma):
        op = Op(eng, fn, is_dma)
        idx = len(self.ops)
        writes = list(writes) + [k for k in reads if isinstance(k, tuple) and k and k[0] == "pb"]
        deps = set()
        for k in reads:
            w = self.last_w.get(k)
            if w is not None:
                deps.add(w)
        for k in writes:
            w = self.last_w.get(k)
            if w is not None:
                deps.add(w)
            for r in self.readers.get(k, ()):
                deps.add(r)
        for k in reads:
            self.readers.setdefault(k, []).append(idx)
        for k in writes:
            self.last_w[k] = idx
            self.readers[k] = []
        deps.discard(idx)
        op.deps = sorted(deps)
        self.ops.append(op)
        return idx

    def pe(self, fn, reads=(), writes=()):
        return self._add("pe", fn, reads, writes, False)

    def act(self, fn, reads=(), writes=()):
        return self._add("act", fn, reads, writes, False)

    def dve(self, fn, reads=(), writes=()):
        return self._add("dve", fn, reads, writes, False)

    def pool(self, fn, reads=(), writes=()):
        return self._add("pool", fn, reads, writes, False)

    def dma(self, fn, reads=(), writes=(), q="sp"):
        return self._add(q, fn, reads, writes, True)

    def barrier(self):
        last = {}
        dmas = {"sp": [], "pool": []}
        for i, op in enumerate(self.ops):
            if op.fn is None:
                continue
            if op.is_dma:
                dmas[op.eng].append(i)
            else:
                last[op.eng] = i
        deps = sorted(set(list(last.values()) + dmas["sp"][-NDMA_SEMS:] + dmas["pool"][-NDMA_SEMS:]))
        for e in ENGS:
            op = Op(e, None, False)
            op.deps = list(deps)
            self.ops.append(op)
        self.last_w = {}
        self.readers = {}

    def emit(self):
        nc = self.nc
        ops = self.ops
        for op in ops:
            for d in op.deps:
                p = ops[d]
                if p.eng == "pe" and op.eng == "pe" and not p.is_dma and op.fn is not None:
                    continue
                p.inc = True
        sems = {}
        for e in ("pe", "act", "dve", "pool"):
            sems[e] = self.stack.enter_context(nc.semaphore(f"s_{e}"))
        dsems = {}
        for q in ("sp", "pool"):
            dsems[q] = [self.stack.enter_context(nc.semaphore(f"d_{q}{i}")) for i in range(NDMA_SEMS)]
        cnt = {e: 0 for e in sems}
        dcnt = {"sp": 0, "pool": 0}
        for op in ops:
            if op.is_dma:
                i = dcnt[op.eng]
                dcnt[op.eng] += 1
                op.sem = dsems[op.eng][i % NDMA_SEMS]
                op.val = 16 * (i // NDMA_SEMS + 1)
                op.qwait = 16 * (i // NDMA_SEMS)
                op.inc = True
            elif op.inc:
                cnt[op.eng] += 1
                op.sem = sems[op.eng]
                op.val = cnt[op.eng]
        self.final_d = {q: [(dsems[q][i], 16 * ((dcnt[q] - 1 - i) // NDMA_SEMS + 1)) for i in range(min(NDMA_SEMS, dcnt[q]))]
                        for q in dsems}
        by_eng = {e: [] for e in ENGS}
        for op in ops:
            by_eng[op.eng].append(op)
        self.n_wait = 0

        def run(eng_name, eng):
            waited = {}
            for op in by_eng[eng_name]:
                need = {}
                for d in op.deps:
                    p = ops[d]
                    if not p.inc:
                        continue
                    if p.eng == "pe" and eng_name == "pe" and not p.is_dma and op.fn is not None:
                        continue
                    key = id(p.sem)
                    if waited.get(key, 0) >= p.val:
                        continue
                    if key not in need or need[key][1] < p.val:
                        need[key] = (p.sem, p.val)
                if op.is_dma and op.qwait > 0:
                    key = id(op.sem)
                    if waited.get(key, 0) < op.qwait:
                        if key not in need or need[key][1] < op.qwait:
                            need[key] = (op.sem, op.qwait)
                for key, (s, v) in need.items():
                    eng.wait_ge(s, v)
                    waited[key] = v
                    self.n_wait += 1
                if op.fn is None:
                    continue
                ins = op.fn(eng)
                if op.inc:
                    ins.then_inc(op.sem, 16 if op.is_dma else 1)
            if eng_name == "sp":
                for q in ("sp", "pool"):
                    for s, v in self.final_d[q]:
                        eng.wait_ge(s, v)

        with nc.Block() as block:
            @block.tensor
            def _(e):
                run("pe", e)

            @block.scalar
            def _(e):
                run("act", e)

            @block.vector
            def _(e):
                run("dve", e)

            @block.gpsimd
            def _(e):
                run("pool", e)

            @block.sync
            def _(e):
                run("sp", e)
        self.stack.close()

from concourse.bass_utils import run_bass_kernel_spmd

D = 2048; KC = 16; NL = 2; TP = 512; NPASS = 4; NS = 16; DFF = 5632; SEQ = 2048
ALPHA = float((2 * NL) ** 0.25)
EPS = 1e-5
OFF = dict(mq=0, mk=384, mv=768, mi=1536, mf=1540, mo=1544, cb=2312, cc=2824, ch=3336,
           aq=3848, ak=4616, av=5384, gt=6152)
GD = (1, 4, 16)
SLOT_ELEMS = 4096
NSLOT = 4


def layer_plan():
    plan = []

    def add(name, src, k0, kc, cols):
        cols = np.asarray(cols)
        assert kc * len(cols) <= SLOT_ELEMS
        plan.append((name, src, k0, kc, cols))

    def ffn(f):
        for qd in range(4):
            for j in range(11 * qd, 11 * qd + 11):
                add(f"up{f}", f"ffi{f}", 0, 16, np.r_[j * 128:(j + 1) * 128, DFF + j * 128:DFF + (j + 1) * 128])
            for mg in range(8):
                add(f"dn{f}", f"ffo{f}", 11 * qd * 128, 11, np.r_[mg * 256:(mg + 1) * 256])

    ffn(0)
    for hp in range(2):
        add("mq", "win", 0, 16, OFF["mq"] + np.r_[hp * 192:(hp + 1) * 192])
    for hp in range(2):
        add("mk", "win", 0, 16, OFF["mk"] + np.r_[hp * 192:(hp + 1) * 192])
    for h in range(4):
        add("mv", "win", 0, 16, OFF["mv"] + np.r_[h * 192:(h + 1) * 192])
    add("mg", "win", 0, 16, OFF["mi"] + np.r_[0:8])
    for h in range(4):
        add("mo", "win", 0, 16, OFF["mo"] + np.r_[h * 192:(h + 1) * 192])
    for cp in range(2):
        for nm in ("cb", "cc", "ch"):
            add(nm, "win", 0, 16, OFF[nm] + np.r_[cp * 256:(cp + 1) * 256])
    for nm in ("aq", "ak"):
        for t in range(3):
            c = np.r_[t * 256:(t + 1) * 256]
            add(nm, "win", 0, 16, OFF[nm] + c)
            add(nm + "s", "win", 0, 16, OFF[nm] + (c // 64) * 64 + (c % 64 + 32) % 64)
    for t in range(3):
        add("av", "win", 0, 16, OFF["av"] + np.r_[t * 256:(t + 1) * 256])
    for m in range(16):
        g0 = OFF["gt"] + m * 128
        add("gA", "win", 0, 16, np.r_[g0:g0 + 128, g0 + 2048:g0 + 2048 + 128])
        add("gB", "win", 0, 16, np.r_[g0 + 4096:g0 + 4096 + 128])
        add("up", "wup", 0, 14, np.r_[m * 128:(m + 1) * 128])
    for t in range(8):
        add("wo", "wo", 0, 16, np.r_[t * 256:(t + 1) * 256])
    ffn(1)
    offs = []
    o = 0
    for (_, _, _, kc, cols) in plan:
        offs.append(o)
        o += 128 * kc * len(cols)
    return plan, offs, o


PLAN, POFF, PTOT = layer_plan()


def host_weights(l, w_in, w_up_mlstm, w_up_conv, w_up_attn, w_o, w_ffn_in, w_ffn_out):
    wa = np.zeros((512, D), np.float32)
    for s in range(4):
        wa[s * 128:s * 128 + 64] = w_up_attn[l][s * 64:(s + 1) * 64]
    src = dict(ffi0=w_ffn_in[l, 0], ffi1=w_ffn_in[l, 1], ffo0=w_ffn_out[l, 0], ffo1=w_ffn_out[l, 1],
               win=w_in[l], wo=w_o[l], wup=np.concatenate([w_up_mlstm[l], w_up_conv[l], wa], axis=0))
    out = np.empty((PTOT,), np.float32)
    for (name, s, k0, kc, cols), o in zip(PLAN, POFF):
        W = src[s]
        n = len(cols)
        if n > 1 and np.all(np.diff(cols) == 1):
            blk = W[k0:k0 + kc * 128, cols[0]:cols[0] + n]
        else:
            blk = W[k0:k0 + kc * 128][:, cols]
        out[o:o + 128 * kc * n] = blk.reshape(kc, 128, n).transpose(1, 0, 2).reshape(-1)
    return out


def host_consts():
    c = {}
    half = 32
    inv = (10000.0 ** (-(2.0 * np.arange(half, dtype=np.float32)) / 64)).astype(np.float32)
    pos = np.arange(SEQ + 1, dtype=np.float32)
    ang = (pos[None, :] * np.tile(inv, 4)[:, None]).astype(np.float32)
    sgn = np.where((np.arange(128) % 64) < 32, -1.0, 1.0).astype(np.float32)[:, None]
    c["cosT"] = np.cos(ang).astype(np.float32)
    c["sinT"] = (np.sin(ang) * sgn).astype(np.float32)
    k = np.arange(128)
    UT = (k[:, None] <= k[None, :]).astype(np.float32)
    LT = (k[:, None] >= k[None, :]).astype(np.float32)
    c["UT"] = UT
    c["ident"] = np.eye(128, dtype=np.float32)
    c["UT4"] = np.tile(UT, (1, 4)).astype(np.float32)
    c["LT4"] = np.tile(LT, (1, 4)).astype(np.float32)
    m2 = np.zeros((NPASS, 128, 128), np.float32)
    for p in range(NPASS):
        blk = (k[:, None] <= (np.arange(32)[None, :] + 32 * p)).astype(np.float32)
        m2[p] = np.tile(blk, (1, 4))
    c["M2"] = m2
    return c


class K:
    def __init__(self, nc, npass=NPASS, nl=NL):
        self.nc = nc
        self.P = Prog(nc)
        self.npass = npass
        self.nl = nl
        self.bank_i = 0
        self.wnext = 0
        self.wissued = 0

    def declare(self):
        nc = self.nc
        di = lambda n, s: nc.dram_tensor(n, list(s), F32, kind="ExternalInput").ap()
        do = lambda n, s: nc.dram_tensor(n, list(s), F32, kind="ExternalOutput").ap()
        self.wts = di("wts", [self.nl, PTOT])
        self.xT = di("xT", [D, SEQ])
        self.xsT = di("xsT", [D, NS])
        self.cosT = di("cosT", [128, SEQ + 1]); self.sinT = di("sinT", [128, SEQ + 1])
        self.c_UT = di("UT", [128, 128]); self.c_id = di("ident", [128, 128])
        self.c_UT4 = di("UT4", [128, 512]); self.c_LT4 = di("LT4", [128, 512]); self.c_M2 = di("M2", [NPASS, 128, 128])
        self.lnp = di("lnp", [128, 2 * NL * 3 * KC])
        self.convw = di("convw", [128, NL * 3 * 4])
        self.gain = di("gain", [128, NL * 768])
        self.bif = di("bif", [128, NL * 8])
        self.sC = di("sC", [NL, NS, 4, 96, 192])
        self.snT = di("snT", [NL, 4, 96, NS])
        self.sm = di("sm", [NL, NS, 4])
        self.scT = di("scT", [NL, 2, 512, NS])
        self.ck = [di(f"ck{g}", [NL, NS, (128, 512, 2048)[g], 512]) for g in range(3)]
        self.o_yT = do("o_yT", [D, SEQ]); self.o_ysT = do("o_ysT", [D, NS])
        self.o_pC = do("o_pC", [NL, 4, 96, 192]); self.o_pn = do("o_pn", [NL, 4, 96]); self.o_pm = do("o_pm", [NL, 4])
        self.o_pconvT = do("o_pconvT", [NL, 512, 2])
        self.o_pk = [do(f"o_pk{g}", [NL, 256, (128, 512, 2048)[g]]) for g in range(3)]
        self.o_pv = [do(f"o_pv{g}", [NL, (128, 512, 2048)[g], 256]) for g in range(3)]
        self.o_sC = do("o_sC", [NL, NS, 4, 96, 192]); self.o_snT = do("o_snT", [NL, 4, 96, NS]); self.o_sm = do("o_sm", [NL, NS, 4])
        self.o_sconvT = do("o_sconvT", [NL, 2, 512, NS])
        self.o_skT = do("o_skT", [NL, 768, NS]); self.o_svT = do("o_svT", [NL, 768, NS])
        import os
        self.dbg = do("o_dbg", [14 * 128, NS]) if "KDBG" in os.environ else None
        self.hK = [[nc.dram_tensor(f"hK{l}_{g}", [256, SEQ], BF16, kind="Internal").ap() for g in range(3)] for l in range(NL)]
        self.hV = [[nc.dram_tensor(f"hV{l}_{g}", [SEQ, 256], BF16, kind="Internal").ap() for g in range(3)] for l in range(NL)]

    def alloc(self):
        P = self.P
        TT = TP + NS
        self.TT = TT
        self.xf = P.sb([128, KC, TT], F32, "xf")
        self.xb = P.sb([128, KC, TT], BF16, "xb")
        self.br = P.sb([128, 14, TT], BF16, "br")
        self.mg = P.sb([128, KC, TT], BF16, "mg")
        self.ws = [P.sb([128, SLOT_ELEMS], BF16, f"ws{i}") for i in range(NSLOT)]
        self.pb = [P.ps([128, 512], F32, f"pb{i}") for i in range(8)]
        self.AF = P.sb([128, 9800], F32, "arenaF")
        self.AB = P.sb([128, 13600], BF16, "arenaB")
        self.UT = P.sb([128, 128], F32, "cUT"); self.ident = P.sb([128, 128], F32, "cid")
        self.UT4 = P.sb([128, 512], F32, "cUT4"); self.LT4 = P.sb([128, 512], F32, "cLT4"); self.M2 = P.sb([128, NPASS, 128], F32, "cM2")
        self.onesD = P.sb([128, 128], F32, "onesD"); self.ones = P.sb([128, 128], F32, "ones1"); self.onesb = P.sb([128, 64], BF16, "onesb")
        self.lnp_sb = P.sb([128, 2 * NL * 3 * KC], F32, "lnp_s")
        self.convw_sb = P.sb([128, NL * 12], F32, "convw_s")
        self.gain_sb = P.sb([128, NL * 768], F32, "gain_s")
        self.bif_sb = P.sb([128, NL * 8], F32, "bif_s")
        self.Cst = [[P.sb([96, 194], F32, f"Cst{l}_{h}") for h in range(4)] for l in range(NL)]
        self.Cbf = [[P.sb([96, 194], BF16, f"Cbf{l}_{h}") for h in range(4)] for l in range(NL)]
        self.mst = [P.sb([4, 1], F32, f"mst{l}") for l in range(NL)]
        self.ctail = [P.sb([128, 4, 2], F32, f"ctail{l}") for l in range(NL)]
        self.fo = 0; self.bo = 0

    def phase(self):
        self.P.barrier()
        self.fo = 0; self.bo = 0

    def fa(self, n):
        a = self.AF[:, self.fo:self.fo + n]; self.fo += n
        assert self.fo <= 9800, self.fo
        return a

    def ba(self, n):
        a = self.AB[:, self.bo:self.bo + n]; self.bo += n
        assert self.bo <= 13600, self.bo
        return a

    def bank(self):
        i = self.bank_i % 8
        self.bank_i += 1
        return i, self.pb[i]

    def mm(self, out, lhsT, rhs, start, stop, r, w):
        self.P.pe(lambda e: e.matmul(out, lhsT=lhsT, rhs=rhs, start=start, stop=stop), r, w)

    def tr(self, out, in_, ident, r, w):
        self.P.pe(lambda e: e.transpose(out, in_, ident), r, w)

    def act(self, out, in_, func, r, w, bias=None, scale=None):
        kw = {}
        if bias is not None: kw["bias"] = bias
        if scale is not None: kw["scale"] = scale
        self.P.act(lambda e: e.activation(out=out, in_=in_, func=func, **kw), r, w)

    def tt(self, out, in0, in1, op, r, w, eng="dve"):
        getattr(self.P, eng)(lambda e: e.tensor_tensor(out=out, in0=in0, in1=in1, op=op), r, w)

    def ts(self, out, in0, s1, s2, op0, op1, r, w, eng="dve"):
        if s2 is None:
            getattr(self.P, eng)(lambda e: e.tensor_scalar(out=out, in0=in0, scalar1=s1, scalar2=None, op0=op0), r, w)
        else:
            getattr(self.P, eng)(lambda e: e.tensor_scalar(out=out, in0=in0, scalar1=s1, scalar2=s2, op0=op0, op1=op1), r, w)

    def stt(self, out, in0, scalar, in1, op0, op1, r, w, eng="dve"):
        getattr(self.P, eng)(lambda e: e.scalar_tensor_tensor(out=out, in0=in0, scalar=scalar, in1=in1, op0=op0, op1=op1), r, w)

    def cp(self, out, in_, r, w, eng="dve"):
        if eng == "act":
            self.P.act(lambda e: e.copy(out=out, in_=in_), r, w)
        else:
            getattr(self.P, eng)(lambda e: e.tensor_copy(out=out, in_=in_), r, w)

    def memset(self, ap, v, w, eng="dve"):
        getattr(self.P, eng)(lambda e: e.memset(ap, v), (), w)

    def dma(self, out, in_, r, w, q="sp"):
        self.P.dma(lambda e: e.dma_start(out=out, in_=in_, allow_slow_non_contiguous=True), r, w, q=q)

    def _issue(self, g):
        per = len(PLAN)
        li = (g // per) % self.nl
        idx = g % per
        name, src, k0, kc, cols = PLAN[idx]
        n = kc * len(cols)
        s = g % NSLOT
        src_ap = self.wts[li, POFF[idx]:POFF[idx] + 128 * n].rearrange("(p n) -> p n", p=128)
        self.dma(self.ws[s][:, 0:n], src_ap, (), [("ws", s)], q="pool")

    def next_w(self, expect):
        g = self.wnext
        import os
        total = self.npass * self.nl * len(PLAN)
        lim = {0: 76, 1: 76, 2: 76 + 13, 3: 76 + 19, 4: 76 + 34}.get(int(os.environ.get("KSTOP", "9")))
        if lim is not None:
            total = lim
        if "KPROJ" in os.environ:
            total = 76 + int(os.environ["KPROJ"])
        if os.environ.get("KATT") == "1":
            total = 76 + 19 + 12
        while self.wissued < min(total, g + NSLOT - 1):
            self._issue(self.wissued)
            self.wissued += 1
        name = PLAN[g % len(PLAN)][0]
        assert name == expect, (name, expect, g)
        self.wnext += 1
        s = g % NSLOT
        return self.ws[s], ("ws", s)

    def fm(self, slot, skey, stride, c0, M, nk, src, srckey, evac):
        for pi, (p0, pn) in enumerate(self.pieces):
            bi, b = self.bank()
            for kc in range(nk):
                self.mm(b[0:M, 0:pn], slot[:, kc * stride + c0:kc * stride + c0 + M], src[:, kc, p0:p0 + pn],
                        kc == 0, kc == nk - 1, [skey, srckey], [("pb", bi)])
            evac(pi, p0, pn, ("pb", bi), b[0:M, 0:pn])

    def tmj(self, slot, skey, stride, c0, N, evac):
        for ti, (t0, tn) in enumerate(self.ttiles):
            bi, b = self.bank()
            for kc in range(KC):
                self.mm(b[0:tn, 0:N], self.xb[:, kc, t0:t0 + tn], slot[:, kc * stride + c0:kc * stride + c0 + N],
                        kc == 0, kc == KC - 1, [skey, "xb"], [("pb", bi)])
            evac(ti, t0, tn, ("pb", bi), b[0:tn, 0:N])

    def layernorm(self, l, i):
        self.phase()
        TT = self.TTc
        sq = [self.fa(512) for _ in range(2)]
        mean = self.fa(TT); rstd = self.fa(TT); tmp = [self.fa(TT) for _ in range(2)]
        gcol = lambda kc: self.lnp_sb[:, ((0 * NL + l) * 3 + i) * KC + kc:((0 * NL + l) * 3 + i) * KC + kc + 1]
        bcol = lambda kc: self.lnp_sb[:, ((1 * NL + l) * 3 + i) * KC + kc:((1 * NL + l) * 3 + i) * KC + kc + 1]
        for (p0, pn) in self.pieces:
            b1i, b1 = self.bank(); b2i, b2 = self.bank()
            for kc in range(KC):
                s = sq[kc % 2]
                self.act(s[:, 0:pn], self.xf[:, kc, p0:p0 + pn], AF.Square, ["xf"], [("sq", kc % 2)])
                self.mm(b1[:, 0:pn], self.onesD[:], self.xf[:, kc, p0:p0 + pn], kc == 0, kc == KC - 1, ["xf", "c"], [("pb", b1i)])
                self.mm(b2[:, 0:pn], self.onesD[:], s[:, 0:pn], kc == 0, kc == KC - 1, [("sq", kc % 2), "c"], [("pb", b2i)])
            self.cp(mean[:, p0:p0 + pn], b1[:, 0:pn], [("pb", b1i)], ["mean"])
            self.tt(tmp[0][:, p0:p0 + pn], mean[:, p0:p0 + pn], mean[:, p0:p0 + pn], ALU.mult, ["mean"], [("lt", 0)])
            self.tt(tmp[1][:, p0:p0 + pn], b2[:, 0:pn], tmp[0][:, p0:p0 + pn], ALU.subtract, [("pb", b2i), ("lt", 0)], [("lt", 1)])
            self.ts(tmp[1][:, p0:p0 + pn], tmp[1][:, p0:p0 + pn], EPS, None, ALU.add, None, [("lt", 1)], [("lt", 1)])
            self.act(tmp[0][:, p0:p0 + pn], tmp[1][:, p0:p0 + pn], AF.Sqrt, [("lt", 1)], [("lt", 0)])
            self.P.dve(lambda e, o=rstd[:, p0:p0 + pn], a=tmp[0][:, p0:p0 + pn]: e.reciprocal(out=o, in_=a), [("lt", 0)], ["rstd"])
        for kc in range(KC):
            t = tmp[kc % 2]
            self.tt(t[:, 0:TT], self.xf[:, kc, 0:TT], mean[:, 0:TT], ALU.subtract, ["xf", "mean"], [("lt", kc % 2)])
            self.tt(t[:, 0:TT], t[:, 0:TT], rstd[:, 0:TT], ALU.mult, [("lt", kc % 2), "rstd"], [("lt", kc % 2)], eng="pool")
            self.act(self.xf[:, kc, 0:TT], t[:, 0:TT], AF.Identity, [("lt", kc % 2)], [("xfo", kc)], bias=bcol(kc), scale=gcol(kc))
            self.act(self.xb[:, kc, 0:TT], t[:, 0:TT], AF.Identity, [("lt", kc % 2)], [("xbo", kc)], bias=bcol(kc), scale=gcol(kc))
        self.phase()

    def ffn(self, l, f):
        TT = self.TTc
        self.phase()
        sa = [self.fa(512) for _ in range(2)]
        for qd in range(4):
            for jl in range(11):
                slot, sk = self.next_w(f"up{f}")
                for pi, (p0, pn) in enumerate(self.pieces):
                    ai, a = self.bank(); bi, b = self.bank()
                    for kc in range(KC):
                        self.mm(a[:, 0:pn], slot[:, kc * 256:kc * 256 + 128], self.xb[:, kc, p0:p0 + pn], kc == 0, kc == KC - 1, [sk, "xb"], [("pb", ai)])
                    for kc in range(KC):
                        self.mm(b[:, 0:pn], slot[:, kc * 256 + 128:kc * 256 + 256], self.xb[:, kc, p0:p0 + pn], kc == 0, kc == KC - 1, [sk, "xb"], [("pb", bi)])
                    s = sa[(jl + pi) % 2]; skk = ("sa", (jl + pi) % 2)
                    self.act(s[:, 0:pn], a[:, 0:pn], AF.Silu, [("pb", ai)], [skk])
                    self.stt(self.br[:, jl, p0:p0 + pn], s[:, 0:pn], 0.5, b[:, 0:pn], ALU.mult, ALU.mult, [skk, ("pb", bi)], [("br", jl)])
            for mgi in range(8):
                slot, sk = self.next_w(f"dn{f}")
                for mc in range(2):
                    m = mgi * 2 + mc
                    for (p0, pn) in self.pieces:
                        bi, b = self.bank()
                        for jl in range(11):
                            self.mm(b[:, 0:pn], slot[:, jl * 256 + mc * 128:jl * 256 + mc * 128 + 128], self.br[:, jl, p0:p0 + pn],
                                    jl == 0, jl == 10, [sk, ("br", jl)], [("pb", bi)])
                        if qd == 0:
                            self.stt(self.xf[:, m, p0:p0 + pn], self.xf[:, m, p0:p0 + pn], ALPHA, b[:, 0:pn], ALU.mult, ALU.add,
                                     [("pb", bi), ("xf", m)], [("xf", m)])
                        else:
                            self.tt(self.xf[:, m, p0:p0 + pn], self.xf[:, m, p0:p0 + pn], b[:, 0:pn], ALU.add, [("pb", bi), ("xf", m)], [("xf", m)])

    def mlstm(self, l, ps):
        self.phase()
        TT = self.TTc; nt = len(self.ttiles); DKS = 96 ** -0.5
        qs32 = [self.fa(NS) for _ in range(4)]; ks32 = [self.fa(NS) for _ in range(4)]
        v32s = self.fa(772)
        ktm_s = self.fa(384); g8_s = self.fa(8); og_s = self.ba(768); vaug_s = self.ba(784)
        mark_f, mark_b = self.fo, self.bo
        qT = [self.ba(TT) for _ in range(4)]; kT = [self.ba(TT) for _ in range(4)]
        ktm = [self.fa(384) for _ in range(4)] + [ktm_s]
        vaug = [self.ba(784) for _ in range(4)] + [vaug_s]
        og = [self.ba(768) for _ in range(4)] + [og_s]
        g8 = [self.fa(8) for _ in range(4)] + [g8_s]
        for ti in range(nt):
            self.memset(vaug[ti], 1.0, [("vaug", ti)])
        self.memset(v32s, 1.0, ["v32s"])
        for nm, dst, d32 in (("mq", qT, qs32), ("mk", kT, ks32)):
            for hp in range(2):
                slot, sk = self.next_w(nm)
                for hh in range(2):
                    h = hp * 2 + hh
                    def ev(pi, p0, pn, bk, psap, h=h, dst=dst, d32=d32, nm=nm):
                        if pi == 0:
                            self.cp(dst[h][0:96, 0:pn], psap[0:96, :], [bk], [(nm, h)], eng="act")
                        else:
                            self.cp(d32[h][0:96, 0:NS], psap[0:96, :], [bk], [(nm + "s", h)])
                    self.fm(slot, sk, 192, hh * 96, 96, KC, self.xb, "xb", ev)
                if nm == "mk":
                    def ev2(ti, t0, tn, bk, psap, hp=hp):
                        self.cp(ktm[ti][0:tn, hp * 192:(hp + 1) * 192], psap, [bk], [("ktm", ti, hp)])
                    self.tmj(slot, sk, 192, 0, 192, ev2)
        import os
        KPROJ = int(os.environ.get("KPROJ", "99"))
        if KPROJ <= 4: return
        for h in range(4):
            slot, sk = self.next_w("mv")
            def ev(ti, t0, tn, bk, psap, h=h):
                self.cp(vaug[ti][0:tn, h * 196:h * 196 + 192], psap, [bk], [("vaug", ti)], eng=os.environ.get("KENG", "act"))
                if tn == NS and "KNOV32" not in os.environ:
                    self.cp(v32s[0:NS, h * 193:h * 193 + 192], psap, [bk], ["v32s"])
            self.tmj(slot, sk, 192, 0, 192, ev)
            if KPROJ <= 5 + h: return
        if KPROJ <= 8: return
        slot, sk = self.next_w("mg")
        def ev(ti, t0, tn, bk, psap):
            self.tt(g8[ti][0:tn, :], psap, self.bif_sb[0:tn, l * 8:(l + 1) * 8], ALU.add, [bk], [("g8", ti)])
        self.tmj(slot, sk, 8, 0, 8, ev)
        if KPROJ <= 9: return
        for h in range(4):
            slot, sk = self.next_w("mo")
            def ev(ti, t0, tn, bk, psap, h=h):
                self.act(og[ti][0:tn, h * 192:(h + 1) * 192], psap, AF.Sigmoid, [bk], [("og", ti)])
            self.tmj(slot, sk, 192, 0, 192, ev)
        import os
        KSUB = int(os.environ.get("KSUB", "9"))
        if KSUB <= 1: return
        sm = [self.fa(40) for _ in range(2)]
        hp_ = [self.fa(192) for _ in range(2)]
        hm = [self.fa(768) for _ in range(2)]
        PT = [self.ba(128) for _ in range(2)]
        ktil = [self.ba(384) for _ in range(2)]
        bn = [self.fa(8) for _ in range(2)]
        flb = self.fa(4)
        WB = dict(hp=hp_, hm=hm, bn=bn)
        gain = self.gain_sb[:, l * 768:(l + 1) * 768]
        def headnorm(src_ps, sc_col, tn, h, ti, par, rk):
            hpp = WB['hp'][par]; b6 = WB['bn'][par]; hm = WB['hm']
            self.act(hpp[0:tn, :], src_ps, AF.Copy, rk, [("hp", par)], scale=sc_col)
            self.P.dve(lambda e: e.bn_stats(out=b6[0:tn, 0:6], in_=hpp[0:tn, :]), [("hp", par)], [("bn", par)])
            self.P.dve(lambda e: e.bn_aggr(out=b6[0:tn, 6:8], in_=b6[0:tn, 0:6]), [("bn", par)], [("bn", par)])
            self.ts(b6[0:tn, 7:8], b6[0:tn, 7:8], EPS, None, ALU.add, None, [("bn", par)], [("bn", par)])
            self.act(b6[0:tn, 7:8], b6[0:tn, 7:8], AF.Sqrt, [("bn", par)], [("bn", par)])
            self.P.dve(lambda e: e.reciprocal(out=b6[0:tn, 7:8], in_=b6[0:tn, 7:8]), [("bn", par)], [("bn", par)])
            self.ts(hpp[0:tn, :], hpp[0:tn, :], b6[0:tn, 6:7], b6[0:tn, 7:8], ALU.subtract, ALU.mult, [("bn", par), ("hp", par)], [("hp", par)])
            self.tt(hpp[0:tn, :], hpp[0:tn, :], gain[0:tn, h * 192:(h + 1) * 192], ALU.mult, [("hp", par)], [("hp", par)], eng="pool")
            self.tt(hm[par][0:tn, h * 192:(h + 1) * 192], hpp[0:tn, :], og[ti][0:tn, h * 192:(h + 1) * 192], ALU.mult,
                    [("hp", par), ("og", ti)], [("hm", par)])
        def to_fm(par, t0, tn):
            hm = WB['hm']
            for c in range(6):
                bi, b = self.bank()
                self.tr(b[:, 0:tn], hm[par][0:tn, c * 128:(c + 1) * 128], self.ident[0:tn, 0:tn], [("hm", par)], [("pb", bi)])
                self.cp(self.br[:, c, t0:t0 + tn], b[:, 0:tn], [("pb", bi)], [("br", c)], eng="act")
        if ps == 0:
            for h in range(4):
                self.memset(self.Cst[l][h][:], 0.0, [("Cst", h)])
                self.memset(self.Cbf[l][h][:], 0.0, [("Cbf", h)])
            self.memset(self.mst[l][:], 0.0, ["mst"])
        for ti in range(4):
            t0 = ti * 128; par = ti % 2; s = sm[par]; sk_ = ("sm", par)
            self.act(s[:, 0:4], g8[ti][:, 4:8], AF.Exp, [("g8", ti)], [sk_], scale=-1.0)
            self.act(s[:, 0:4], s[:, 0:4], AF.Ln, [sk_], [sk_], bias=1.0)
            ci, cb_ = self.bank(); toi, tob = self.bank()
            self.mm(cb_[:, 0:4], self.UT[:], s[:, 0:4], True, True, [sk_, "c"], [("pb", ci)])
            self.mm(tob[:, 0:4], self.ones[:], s[:, 0:4], True, True, [sk_, "c"], [("pb", toi)])
            self.act(s[:, 4:8], cb_[:, 0:4], AF.Exp, [("pb", ci)], [sk_], scale=-1.0)
            self.tt(s[:, 16:20], g8[ti][:, 0:4], cb_[:, 0:4], ALU.add, [("g8", ti), ("pb", ci)], [sk_])
            self.act(s[:, 8:12], s[:, 16:20], AF.Exp, [sk_], [sk_])
            self.ts(s[:, 8:12], s[:, 8:12], DKS, None, ALU.mult, None, [sk_], [sk_])
            self.tt(s[:, 24:28], s[:, 16:20], tob[:, 0:4], ALU.subtract, [sk_, ("pb", toi)], [sk_])
            self.act(s[:, 12:16], s[:, 24:28], AF.Exp, [sk_], [sk_])
            self.cp(s[:, 20:24], tob[:, 0:4], [("pb", toi)], [sk_])
            fi, fb = self.bank()
            self.mm(fb[0:96, 0:4], self.onesD[:, 0:96], s[:, 20:24], True, True, [sk_, "c"], [("pb", fi)])
            self.act(flb[0:96, 0:4], fb[0:96, 0:4], AF.Exp, [("pb", fi)], ["flb"], scale=-16.0)
            gi, gb = self.bank(); tti, ttb = self.bank()
            self.tr(gb[0:4, 0:128], s[:, 24:28], self.ident[:], [sk_], [("pb", gi)])
            self.tr(ttb[0:4, 0:128], s[:, 20:24], self.ident[:], [sk_], [("pb", tti)])
            self.P.dve(lambda e, o=s[0:4, 32:33], a=gb[0:4, 0:128]: e.reduce_max(out=o, in_=a, axis=AX.X), [("pb", gi)], [("smx", par)])
            self.cp(s[0:4, 33:34], ttb[0:4, 0:1], [("pb", tti)], [("smx", par)])
            self.stt(self.mst[l][:], self.mst[l][:], s[0:4, 33:34], s[0:4, 32:33], ALU.subtract, ALU.max, [("smx", par), "mst"], ["mst"])
            for h in range(4):
                self.ts(ktil[par][:, h * 96:(h + 1) * 96], ktm[ti][:, h * 96:(h + 1) * 96], s[:, 12 + h:13 + h], DKS, ALU.mult, ALU.mult,
                        [sk_, ("ktm", ti, h // 2)], [("ktil", par)], eng="pool")
            for h in range(4):
                ai, ab = self.bank()
                self.mm(ab[:, 0:128], kT[h][0:96, t0:t0 + 128], qT[h][0:96, t0:t0 + 128], True, True, [("mk", h), ("mq", h)], [("pb", ai)])
                self.stt(PT[h % 2][:, :], ab[:, 0:128], s[:, 8 + h:9 + h], self.UT[:], ALU.mult, ALU.mult, [("pb", ai), sk_], [("PT", h % 2)])
                ni, nb = self.bank()
                self.mm(nb[:, 0:194], PT[h % 2][:, :], vaug[ti][:, h * 196:h * 196 + 194], True, False, [("PT", h % 2), ("vaug", ti)], [("pb", ni)])
                self.mm(nb[:, 0:194], qT[h][0:96, t0:t0 + 128], self.Cbf[l][h][:], False, True, [("mq", h), ("Cbf", h)], [("pb", ni)])
                dcol = s[:, 28 + (h % 2) * 2:29 + (h % 2) * 2]
                self.ts(dcol, nb[:, 192:193], s[:, 4 + h:5 + h], None, ALU.mult, None, [("pb", ni), sk_], [("dn", par, h % 2)])
                self.stt(dcol, dcol, -1.0, dcol, ALU.mult, ALU.max, [("dn", par, h % 2)], [("dn", par, h % 2)])
                self.ts(dcol, dcol, 1.0, None, ALU.max, None, [("dn", par, h % 2)], [("dn", par, h % 2)])
                self.P.dve(lambda e, o=s[:, 29 + (h % 2) * 2:30 + (h % 2) * 2], a=s[:, 28 + (h % 2) * 2:29 + (h % 2) * 2]: e.reciprocal(out=o, in_=a),
                           [("dn", par, h % 2)], [("dn", par, h % 2)])
                self.tt(s[:, 29 + (h % 2) * 2:30 + (h % 2) * 2], s[:, 29 + (h % 2) * 2:30 + (h % 2) * 2], s[:, 4 + h:5 + h], ALU.mult,
                        [("dn", par, h % 2), sk_], [("dn", par, h % 2)])
                headnorm(nb[:, 0:192], s[:, 29 + (h % 2) * 2:30 + (h % 2) * 2], 128, h, ti, par, [("pb", ni), ("dn", par, h % 2)])
                di, db = self.bank()
                self.mm(db[0:96, 0:194], ktil[par][:, h * 96:(h + 1) * 96], vaug[ti][:, h * 196:h * 196 + 194], True, True,
                        [("ktil", par), ("vaug", ti)], [("pb", di)])
                self.stt(self.Cst[l][h][:], self.Cst[l][h][:], flb[0:96, h:h + 1], db[0:96, 0:194], ALU.mult, ALU.add,
                         [("pb", di), "flb", ("Cst", h)], [("Cst", h)])
                self.cp(self.Cbf[l][h][:], self.Cst[l][h][:], [("Cst", h)], [("Cbf", h)], eng="act")
            to_fm(par, t0, 128)
        if KSUB <= 2: return
        if ps == self.npass - 1:
            s = sm[0]
            di4 = self.fa(4); emb = self.fa(4)
            self.ts(di4[0:4, 0:4], self.ident[0:4, 0:4], self.mst[l][:, 0:1], None, ALU.mult, None, ["mst"], ["di4"])
            bi, b = self.bank()
            self.mm(b[0:96, 0:4], self.ones[0:4, 0:96], di4[0:4, 0:4], True, True, ["di4", "c"], [("pb", bi)])
            self.act(emb[0:96, 0:4], b[0:96, 0:4], AF.Exp, [("pb", bi)], ["emb"], scale=-1.0)
            self.dma(self.o_pm[l].rearrange("(h o) -> h o", o=1), self.mst[l][:, 0:1], ["mst"], ["o_pm"])
            for h in range(4):
                co = self.fa(193)
                self.ts(co[0:96, :], self.Cst[l][h][:, 0:193], emb[0:96, h:h + 1], None, ALU.mult, None, [("Cst", h), "emb"], [("co", h)])
                self.dma(self.o_pC[l, h], co[0:96, 0:192], [("co", h)], ["o_pC"])
                self.dma(self.o_pn[l, h].rearrange("(k o) -> k o", o=1), co[0:96, 192:193], [("co", h)], ["o_pn"])
        if ps != 0 or KSUB <= 3:
            return
        self.P.barrier()
        self.fo, self.bo = mark_f, mark_b
        sm = [self.fa(40)]; WB['hp'] = [self.fa(192)]; WB['hm'] = [self.fa(768)]; WB['bn'] = [self.fa(8)]
        ti = 4; t0 = TP; s = sm[0]; sk_ = ("sms",)
        m0 = self.fa(4)
        self.dma(m0[0:NS, :], self.sm[l], (), ["m0"])
        self.act(s[0:NS, 0:4], g8[ti][0:NS, 4:8], AF.Exp, [("g8", ti)], [sk_], scale=-1.0)
        self.act(s[0:NS, 0:4], s[0:NS, 0:4], AF.Ln, [sk_], [sk_], bias=1.0)
        self.tt(s[0:NS, 4:8], m0[0:NS, :], s[0:NS, 0:4], ALU.subtract, ["m0", sk_], [sk_])
        self.tt(s[0:NS, 8:12], s[0:NS, 4:8], g8[ti][0:NS, 0:4], ALU.max, [sk_, ("g8", ti)], [sk_])
        self.dma(self.o_sm[l], s[0:NS, 8:12], [sk_], ["o_sm"])
        self.tt(s[0:NS, 12:16], s[0:NS, 4:8], s[0:NS, 8:12], ALU.subtract, [sk_], [sk_])
        self.act(s[0:NS, 12:16], s[0:NS, 12:16], AF.Exp, [sk_], [sk_])
        self.tt(s[0:NS, 16:20], g8[ti][0:NS, 0:4], s[0:NS, 8:12], ALU.subtract, [sk_, ("g8", ti)], [sk_])
        self.act(s[0:NS, 16:20], s[0:NS, 16:20], AF.Exp, [sk_], [sk_])
        self.act(s[0:NS, 20:24], g8[ti][0:NS, 0:4], AF.Exp, [("g8", ti)], [sk_])
        self.act(s[0:NS, 24:28], s[0:NS, 4:8], AF.Exp, [sk_], [sk_])
        Dm = self.fa(64); dbc = self.fa(64)
        for b_ in range(NS):
            self.ts(Dm[0:NS, b_ * 4:(b_ + 1) * 4], s[0:NS, 12:16], self.ident[0:NS, b_:b_ + 1], None, ALU.mult, None, [sk_], ["Dm"])
        bi, b = self.bank()
        self.mm(b[0:96, 0:64], self.ones[0:NS, 0:96], Dm[0:NS, 0:64], True, True, ["Dm", "c"], [("pb", bi)])
        self.cp(dbc[0:96, 0:64], b[0:96, 0:64], [("pb", bi)], ["dbc"])
        qk = self.fa(4); pr = self.fa(NS)
        for h in range(4):
            self.tt(pr[0:96, 0:NS], qs32[h][0:96, 0:NS], ks32[h][0:96, 0:NS], ALU.mult, [("mqs", h), ("mks", h)], ["pr"])
            bi, b = self.bank()
            self.mm(b[0:NS, 0:1], pr[0:96, 0:NS], self.ones[0:96, 0:1], True, True, ["pr", "c"], [("pb", bi)])
            self.ts(qk[0:NS, h:h + 1], b[0:NS, 0:1], DKS, None, ALU.mult, None, [("pb", bi)], [("qk", h)])
        Cin = self.fa(NS * 193); nin = self.fa(NS); QE = self.fa(NS * NS); KE = self.fa(NS * 96); vt = self.fa(193)
        Cn = [self.fa(193) for _ in range(2)]; nout = self.fa(NS); hsrc = self.fa(193)
        Cin3 = Cin.rearrange("p (b v) -> p b v", b=NS)
        self.memset(QE[0:96, :], 0.0, ["QE"])
        for h in range(4):
            self.dma(Cin3[0:96, :, 0:192], self.sC[l, :, h].rearrange("b k v -> k b v"), (), [("Cin", h)])
            self.dma(nin[0:96, 0:NS], self.snT[l, h], (), [("nin", h)])
            self.cp(Cin3[0:96, :, 192], nin[0:96, 0:NS], [("nin", h)], [("Cin", h)])
            self.cp(QE[0:96, 0:NS * NS:NS + 1], qs32[h][0:96, 0:NS], [("mqs", h)], ["QE"])
            QE3 = QE.rearrange("p (a b) -> p a b", a=NS)
            qi, qb = self.bank()
            for b_ in range(NS):
                self.mm(qb[0:NS, 0:193], QE3[0:96, b_, :], Cin3[0:96, b_, :], b_ == 0, b_ == NS - 1, ["QE", ("Cin", h)], [("pb", qi)])
            qC = self.fa(193)
            self.cp(qC[0:NS, :], qb[0:NS, 0:193], [("pb", qi)], [("qC", h)])
            self.ts(vt[0:NS, :], v32s[0:NS, h * 193:(h + 1) * 193], s[0:NS, 16 + h:17 + h], None, ALU.mult, None, ["v32s", sk_], ["vt"])
            KE3 = KE.rearrange("p (b k) -> p b k", b=NS)
            for b_ in range(NS):
                self.ts(KE3[0:NS, b_, :], ktm[ti][0:NS, h * 96:(h + 1) * 96], self.ident[0:NS, b_:b_ + 1], DKS, ALU.mult, ALU.mult,
                        [("ktm", ti, h // 2)], ["KE"], eng="pool")
            for b_ in range(NS):
                oi, ob = self.bank()
                self.mm(ob[0:96, 0:193], KE3[0:NS, b_, :], vt[0:NS, :], True, True, ["KE", "vt"], [("pb", oi)])
                cn = Cn[b_ % 2]
                self.stt(cn[0:96, :], Cin3[0:96, b_, :], dbc[0:96, b_ * 4 + h:b_ * 4 + h + 1], ob[0:96, 0:193], ALU.mult, ALU.add,
                         [("pb", oi), ("Cin", h), "dbc"], [("Cn", b_ % 2)])
                self.dma(self.o_sC[l, b_, h], cn[0:96, 0:192], [("Cn", b_ % 2)], ["o_sC"])
                self.cp(nout[0:96, b_:b_ + 1], cn[0:96, 192:193], [("Cn", b_ % 2)], ["nout"], eng="act")
            self.dma(self.o_snT[l, h], nout[0:96, 0:NS], ["nout"], ["o_snT"])
            self.tt(s[0:NS, 28:29], s[0:NS, 20 + h:21 + h], qk[0:NS, h:h + 1], ALU.mult, [sk_, ("qk", h)], [("s1",)])
            self.ts(hsrc[0:NS, :], v32s[0:NS, h * 193:(h + 1) * 193], s[0:NS, 28:29], None, ALU.mult, None, ["v32s", ("s1",)], ["hsrc"])
            self.stt(hsrc[0:NS, :], qC[0:NS, :], s[0:NS, 24 + h:25 + h], hsrc[0:NS, :], ALU.mult, ALU.add, [("qC", h), sk_, "hsrc"], ["hsrc"])
            self.stt(s[0:NS, 29:30], hsrc[0:NS, 192:193], -1.0, hsrc[0:NS, 192:193], ALU.mult, ALU.max, ["hsrc"], [("s2",)])
            self.ts(s[0:NS, 29:30], s[0:NS, 29:30], 1.0, None, ALU.max, None, [("s2",)], [("s2",)])
            self.P.dve(lambda e, o=s[0:NS, 30:31], a=s[0:NS, 29:30]: e.reciprocal(out=o, in_=a), [("s2",)], [("s3",)])
            headnorm(hsrc[0:NS, 0:192], s[0:NS, 30:31], NS, h, ti, 0, ["hsrc", ("s3",)])
        to_fm(0, t0, NS)

    def conv(self, l, ps):
        self.phase()
        TT = self.TTc
        W = TT + 2
        cw = lambda j, c: self.convw_sb[:, (l * 3 + j) * 4 + c:(l * 3 + j) * 4 + c + 1]
        if ps == 0:
            self.memset(self.ctail[l][:], 0.0, ["ctail"])
        for cp in range(2):
            buf = {}
            for nm in ("cb", "cc", "ch"):
                slot, sk = self.next_w(nm)
                for mc in range(2):
                    t = self.fa(W); buf[(nm, mc)] = t
                    def ev(pi, p0, pn, bk, psap, t=t, nm=nm, mc=mc):
                        self.cp(t[:, 2 + p0:2 + p0 + pn], psap, [bk], [(nm, cp, mc)], eng="act" if pi == 0 else "dve")
                    self.fm(slot, sk, 256, mc * 128, 128, KC, self.xb, "xb", ev)
            for mc in range(2):
                c = cp * 2 + mc
                cb, cc, ch = buf[("cb", mc)], buf[("cc", mc)], buf[("ch", mc)]
                kk = ("u", cp, mc)
                self.tt(cc[:, 2:2 + TT], cc[:, 2:2 + TT], ch[:, 2:2 + TT], ALU.mult, [("cc", cp, mc), ("ch", cp, mc)], [kk])
                self.cp(cc[:, 0:2], self.ctail[l][:, c, :], ["ctail"], [kk])
                acc = ch
                self.ts(acc[:, 2:2 + TP], cc[:, 0:TP], cw(0, c), None, ALU.mult, None, [kk], [("acc", cp, mc)])
                self.stt(acc[:, 2:2 + TP], cc[:, 1:1 + TP], cw(1, c), acc[:, 2:2 + TP], ALU.mult, ALU.add, [kk, ("acc", cp, mc)], [("acc", cp, mc)])
                self.stt(acc[:, 2:2 + TP], cc[:, 2:2 + TP], cw(2, c), acc[:, 2:2 + TP], ALU.mult, ALU.add, [kk, ("acc", cp, mc)], [("acc", cp, mc)])
                self.tt(self.br[:, 6 + c, 0:TP], acc[:, 2:2 + TP], cb[:, 2:2 + TP], ALU.mult, [("acc", cp, mc), ("cb", cp, mc)], [("br", 6 + c)])
                if ps == 0:
                    st = self.fa(2 * NS)
                    self.dma(st[:, 0:2 * NS].rearrange("p (j b) -> p j b", j=2), self.scT[l, :, c * 128:(c + 1) * 128, :].rearrange("j p b -> p j b"), (), [("st", c)])
                    us = cc[:, 2 + TP:2 + TP + NS]
                    a2 = acc[:, 2 + TP:2 + TP + NS]
                    self.ts(a2, st[:, 0:NS], cw(0, c), None, ALU.mult, None, [("st", c)], [("acc", cp, mc)])
                    self.stt(a2, st[:, NS:2 * NS], cw(1, c), a2, ALU.mult, ALU.add, [("st", c), ("acc", cp, mc)], [("acc", cp, mc)])
                    self.stt(a2, us, cw(2, c), a2, ALU.mult, ALU.add, [kk, ("acc", cp, mc)], [("acc", cp, mc)])
                    self.tt(self.br[:, 6 + c, TP:TP + NS], a2, cb[:, 2 + TP:2 + TP + NS], ALU.mult, [("acc", cp, mc), ("cb", cp, mc)], [("br", 6 + c)])
                    self.dma(self.o_sconvT[l, 0, c * 128:(c + 1) * 128, :], st[:, NS:2 * NS], [("st", c)], ["o_sconv"])
                    self.dma(self.o_sconvT[l, 1, c * 128:(c + 1) * 128, :], us, [kk], ["o_sconv"])
                self.cp(self.ctail[l][:, c, :], cc[:, TP:TP + 2], [kk], ["ctail"], eng="pool")
                if ps == self.npass - 1:
                    self.dma(self.o_pconvT[l, c * 128:(c + 1) * 128, :], cc[:, TP:TP + 2], [kk], ["o_pconv"])

    def attn(self, l, ps):
        self.phase()
        TT = self.TTc
        base = ps * TP
        qst = self.ba(6 * TP).rearrange("p (c t) -> p c t", c=6)
        qs32 = self.fa(6 * NS).rearrange("p (c b) -> p c b", c=6)
        ks32 = self.fa(6 * NS).rearrange("p (c b) -> p c b", c=6)
        vs32 = self.fa(6 * NS).rearrange("p (c b) -> p c b", c=6)
        mark_f, mark_b = self.fo, self.bo
        cosb = self.fa(TT); sinb = self.fa(TT)
        self.dma(cosb[:, 0:TP], self.cosT[:, base:base + TP], (), ["cos"])
        self.dma(sinb[:, 0:TP], self.sinT[:, base:base + TP], (), ["sin"])
        if ps == 0:
            for b_ in range(NS):
                pass
            self.dma(cosb[:, TP:TP + 1], self.cosT[:, SEQ:SEQ + 1], (), ["cos"])
            self.dma(sinb[:, TP:TP + 1], self.sinT[:, SEQ:SEQ + 1], (), ["sin"])
        kst = self.ba(6 * TP).rearrange("p (c t) -> p c t", c=6)
        t1 = [self.fa(TT) for _ in range(2)]; t2 = [self.fa(TT) for _ in range(2)]; kro = [self.fa(TP) for _ in range(2)]
        it = 0
        for nm, st, s32 in (("aq", qst, qs32), ("ak", kst, ks32)):
            for t in range(3):
                slot, sk = self.next_w(nm)
                slot2, sk2 = self.next_w(nm + "s")
                g = t; d = GD[g]; ni = TP // d
                for mc in range(2):
                    c = t * 2 + mc
                    par = it % 2; it += 1
                    banks = {}
                    def ev(pi, p0, pn, bk, psap, par=par):
                        self.tt(t1[par][:, p0:p0 + pn], psap, cosb[:, p0:p0 + pn] if pi == 0 else cosb[:, TP:TP + 1].to_broadcast([128, NS]), ALU.mult,
                                [bk, "cos"], [("t1", par)])
                    def ev2(pi, p0, pn, bk, psap, par=par):
                        self.tt(t2[par][:, p0:p0 + pn], psap, sinb[:, p0:p0 + pn] if pi == 0 else sinb[:, TP:TP + 1].to_broadcast([128, NS]), ALU.mult,
                                [bk, "sin"], [("t2", par)])
                    self.fm(slot, sk, 256, mc * 128, 128, KC, self.xb, "xb", ev)
                    self.fm(slot2, sk2, 256, mc * 128, 128, KC, self.xb, "xb", ev2)
                    stv = st[:, c, :].rearrange("p (r i) -> p i r", r=d)
                    if nm == "ak":
                        self.tt(kro[par][:, 0:TP], t1[par][:, 0:TP], t2[par][:, 0:TP], ALU.add, [("t1", par), ("t2", par)], [("kro", par)], eng="pool")
                        self.cp(stv, kro[par][:, 0:TP].rearrange("p (i r) -> p i r", r=d), [("kro", par)], [("kst", c)], eng="act")
                        W_ = (128, 512, 2048)[g]
                        lo = SEQ - W_
                        a = max(base, lo)
                        if a < base + TP:
                            self.dma(self.o_pk[g][l, mc * 128:(mc + 1) * 128, a - lo:base + TP - lo], kro[par][:, a - base:TP], [("kro", par)], ["o_pk"])
                        self.dma(self.hK[l][g][mc * 128:(mc + 1) * 128, :].rearrange("p (r x) -> p r x", r=d)[:, :, ni * ps:ni * (ps + 1)],
                                 st[:, c, :].rearrange("p (r i) -> p r i", r=d), [("kst", c)], [("hK", g)])
                    else:
                        self.tt(stv, t1[par][:, 0:TP].rearrange("p (i r) -> p i r", r=d), t2[par][:, 0:TP].rearrange("p (i r) -> p i r", r=d), ALU.add,
                                [("t1", par), ("t2", par)], [("qst", c)], eng="pool")
                    if ps == 0:
                        self.tt(s32[:, c, :], t1[par][:, TP:TP + NS], t2[par][:, TP:TP + NS], ALU.add, [("t1", par), ("t2", par)], [(nm + "32", c)])
                        if nm == "ak":
                            self.dma(self.o_skT[l, c * 128:(c + 1) * 128, :], s32[:, c, :], [(nm + "32", c)], ["o_sk"])
        import os
        KATT = int(os.environ.get("KATT", "9"))
        if KATT <= 1: return
        vb = [self.ba(256) for _ in range(2)]; vf = [self.fa(256) for _ in range(2)]
        it = 0
        for t in range(3):
            slot, sk = self.next_w("av")
            g = t
            def ev(ti, t0, tn, bk, psap, g=g):
                nonlocal it
                if tn != 128:
                    return
                par = it % 2; it += 1
                self.cp(vb[par][:, :], psap, [bk], [("vb", par)], eng="act")
                self.dma(self.hV[l][g][base + t0:base + t0 + 128, :], vb[par][:, :], [("vb", par)], [("hV", g)])
                W_ = (128, 512, 2048)[g]; lo = SEQ - W_
                if base + t0 >= lo:
                    self.cp(vf[par][:, :], psap, [bk], [("vf", par)])
                    self.dma(self.o_pv[g][l, base + t0 - lo:base + t0 - lo + 128, :], vf[par][:, :], [("vf", par)], ["o_pv"])
            self.tmj(slot, sk, 256, 0, 256, ev)
            if ps == 0:
                for mc in range(2):
                    c = t * 2 + mc
                    bi, b = self.bank()
                    for kc in range(KC):
                        self.mm(b[:, 0:NS], slot[:, kc * 256 + mc * 128:kc * 256 + mc * 128 + 128], self.xb[:, kc, TP:TP + NS], kc == 0, kc == KC - 1,
                                [sk, "xb"], [("pb", bi)])
                    self.cp(vs32[:, c, :], b[:, 0:NS], [("pb", bi)], [("vs32", c)])
                    self.dma(self.o_svT[l, c * 128:(c + 1) * 128, :], vs32[:, c, :], [("vs32", c)], ["o_sv"])
        if KATT <= 2: return
        self.P.barrier()
        self.fo, self.bo = mark_f, mark_b
        accN = self.fa(4 * TT).rearrange("p (s t) -> p s t", s=4)
        accD = self.fa(4 * TT).rearrange("p (s t) -> p s t", s=4)
        kt = [self.ba(512).rearrange("p (c k) -> p c k", c=2) for _ in range(2)]
        vt = [self.ba(512).rearrange("p (n f) -> p n f", n=2) for _ in range(2)]
        PT = [self.ba(512) for _ in range(4)]
        ex = [self.fa(512) for _ in range(2)]
        it = 0; pti = 0
        for g in range(int(os.environ.get("KAG", "3"))):
            d = GD[g]; ni = TP // d; L_ = SEQ // d
            qn = 128 if g < 2 else 32
            for r in range(d):
                for qb in range(ni // qn):
                    par = it % 2; it += 1
                    i0 = ni * ps + qb * qn
                    if g < 2:
                        k0 = max(0, i0 - 128); nk = i0 + 128 - k0
                    else:
                        k0 = 0; nk = i0 + 32
                    nblk = (nk + 127) // 128
                    self.dma(kt[par][:, :, 0:nk], self.hK[l][g].rearrange("(c p) x -> p c x", c=2)[:, :, r * L_ + k0:r * L_ + k0 + nk], [("hK", g)], [("kt", par)])
                    for n in range(nblk):
                        kk0 = k0 + n * 128; kn = min(128, nk - n * 128)
                        rows = self.hV[l][g].rearrange("(i r) f -> r i f", r=d)[r, kk0:kk0 + kn, :]
                        self.dma(vt[par][0:kn, n, :], rows, [("hV", g)], [("vt", par)])
                    qcol = r * ni + qb * qn
                    pts = []
                    for n in range(nblk):
                        kn = min(128, nk - n * 128)
                        e = ex[pti % 2]; ek = ("ex", pti % 2)
                        for u in range(2):
                            si, sb_ = self.bank()
                            for c2 in range(2):
                                self.mm(sb_[0:kn, c2 * qn:(c2 + 1) * qn], kt[par][64 * u:64 * u + 64, c2, n * 128:n * 128 + kn],
                                        qst[64 * u:64 * u + 64, 2 * g + c2, qcol:qcol + qn], True, True, [("kt", par), ("qst", 2 * g + c2)], [("pb", si)])
                            self.act(e[0:kn, u * 2 * qn:(u + 1) * 2 * qn], sb_[0:kn, 0:2 * qn], AF.Exp, [("pb", si)], [ek], scale=0.125)
                        if g < 2:
                            diag = (n == nblk - 1)
                            mask = (self.UT4 if diag else self.LT4)[0:kn, :]
                        else:
                            mask = self.M2[0:kn, ps, :]
                        p_ = PT[pti % 4]; pk = ("PT", pti % 4); pti += 1
                        self.tt(p_[0:kn, 0:4 * qn], e[0:kn, 0:4 * qn], mask, ALU.mult, [ek], [pk])
                        pts.append((p_, pk, kn, n))
                    ni_, nbk = self.bank(); di_, dbk = self.bank()
                    for hh in range(4):
                        ph = (hh % 2) * 2 + hh // 2
                        for j, (p_, pk, kn, n) in enumerate(pts):
                            self.mm(nbk[0:64, hh * qn:(hh + 1) * qn], vt[par][0:kn, n, hh * 64:(hh + 1) * 64], p_[0:kn, ph * qn:(ph + 1) * qn],
                                    j == 0, j == len(pts) - 1, [("vt", par), pk], [("pb", ni_)])
                        for j, (p_, pk, kn, n) in enumerate(pts):
                            self.mm(dbk[0:64, hh * qn:(hh + 1) * qn], self.onesb[0:kn, 0:64], p_[0:kn, ph * qn:(ph + 1) * qn],
                                    j == 0, j == len(pts) - 1, ["c", pk], [("pb", di_)])
                    tsl = slice(qb * qn * d + r, qb * qn * d + r + (qn - 1) * d + 1, d)
                    nv = nbk[0:64, 0:4 * qn].rearrange("p (s q) -> p s q", s=4)
                    dv = dbk[0:64, 0:4 * qn].rearrange("p (s q) -> p s q", s=4)
                    if g == 0:
                        self.cp(accN[0:64, :, tsl], nv, [("pb", ni_)], ["accN"])
                        self.cp(accD[0:64, :, tsl], dv, [("pb", di_)], ["accD"], eng="act")
                    else:
                        self.tt(accN[0:64, :, tsl], accN[0:64, :, tsl], nv, ALU.add, [("pb", ni_), "accN"], ["accN"])
                        self.tt(accD[0:64, :, tsl], accD[0:64, :, tsl], dv, ALU.add, [("pb", di_), "accD"], ["accD"], eng="pool" if False else "dve")
        if KATT <= 3: return
        if ps == 0:
            ones64 = self.ones
            kc_ = [self.fa(512) for _ in range(2)]
            kT_ = [self.fa(256).rearrange("p (c k) -> p c k", c=2) for _ in range(2)]
            Pm = [self.fa(4) for _ in range(2)]
            it = 0
            sN = self.fa(4 * NS).rearrange("p (s b) -> p s b", s=4); sD = self.fa(4 * NS).rearrange("p (s b) -> p s b", s=4)
            pr = self.fa(NS)
            first = True
            for g in range(3):
                for hh in range(4):
                    c = 2 * g + hh // 2; u = hh % 2
                    self.tt(pr[64 * u:64 * u + 64, 0:NS], qs32[64 * u:64 * u + 64, c, :], ks32[64 * u:64 * u + 64, c, :], ALU.mult, [("aq32", c), ("ak32", c)], ["pr"])
                    bi, b = self.bank()
                    self.mm(b[0:64, 0:NS], ones64[64 * u:64 * u + 64, 0:64], pr[64 * u:64 * u + 64, 0:NS], True, True, ["pr", "c"], [("pb", bi)])
                    pe_ = self.fa(NS)
                    self.act(pe_[0:64, 0:NS], b[0:64, 0:NS], AF.Exp, [("pb", bi)], [("pe", g, hh)], scale=0.125)
                    vi, vbk = self.bank()
                    self.mm(vbk[0:64, 0:NS], self.ident[64 * u:64 * u + 64, 64 * u:64 * u + 64], vs32[64 * u:64 * u + 64, c, :], True, True,
                            [("vs32", c), "c"], [("pb", vi)])
                    if g == 0:
                        self.tt(sN[0:64, hh, :], pe_[0:64, 0:NS], vbk[0:64, 0:NS], ALU.mult, [("pe", g, hh), ("pb", vi)], [("sN", hh)])
                        self.cp(sD[0:64, hh, :], pe_[0:64, 0:NS], [("pe", g, hh)], [("sD", hh)])
                    else:
                        tmpv = self.fa(NS)
                        self.tt(tmpv[0:64, 0:NS], pe_[0:64, 0:NS], vbk[0:64, 0:NS], ALU.mult, [("pe", g, hh), ("pb", vi)], [("tmpv", g, hh)])
                        self.tt(sN[0:64, hh, :], sN[0:64, hh, :], tmpv[0:64, 0:NS], ALU.add, [("tmpv", g, hh), ("sN", hh)], [("sN", hh)])
                        self.tt(sD[0:64, hh, :], sD[0:64, hh, :], pe_[0:64, 0:NS], ALU.add, [("pe", g, hh), ("sD", hh)], [("sD", hh)])
            for g in range(3):
                d = GD[g]
                for b_ in range(NS):
                    par = it % 2; it += 1
                    self.dma(kc_[par][:, :], self.ck[g][l, b_].rearrange("(j r) f -> r j f", r=d)[0, :, :], (), [("kc", par)])
                    for c2 in range(2):
                        ti_, tb = self.bank()
                        self.tr(tb[:, 0:128], kc_[par][:, c2 * 128:(c2 + 1) * 128], self.ident[:], [("kc", par)], [("pb", ti_)])
                        self.cp(kT_[par][:, c2, :], tb[:, 0:128], [("pb", ti_)], [("kT_", par, c2)], eng="act")
                    for u in range(2):
                        si, sb_ = self.bank()
                        for c2 in range(2):
                            self.mm(sb_[:, c2:c2 + 1], kT_[par][64 * u:64 * u + 64, c2, :], qs32[64 * u:64 * u + 64, 2 * g + c2, b_:b_ + 1], True, True,
                                    [("kT_", par, c2), ("aq32", 2 * g + c2)], [("pb", si)])
                        self.act(Pm[par][:, 2 * u:2 * u + 2], sb_[:, 0:2], AF.Exp, [("pb", si)], [("Pm", par)], scale=0.125)
                    ni_, nbk = self.bank()
                    for hh in range(4):
                        ph = (hh % 2) * 2 + hh // 2
                        self.mm(nbk[0:64, hh:hh + 1], kc_[par][:, 256 + hh * 64:256 + (hh + 1) * 64], Pm[par][:, ph:ph + 1], True, True,
                                [("kc", par), ("Pm", par)], [("pb", ni_)])
                        self.mm(nbk[0:64, 4 + hh:5 + hh], self.ones[:, 0:64], Pm[par][:, ph:ph + 1], True, True, ["c", ("Pm", par)], [("pb", ni_)])
                    self.tt(sN[0:64, :, b_], sN[0:64, :, b_], nbk[0:64, 0:4], ALU.add, [("pb", ni_), ("sN", 0), ("sN", 1), ("sN", 2), ("sN", 3)],
                            [("sN", 0), ("sN", 1), ("sN", 2), ("sN", 3)])
                    self.tt(sD[0:64, :, b_], sD[0:64, :, b_], nbk[0:64, 4:8], ALU.add, [("pb", ni_), ("sD", 0), ("sD", 1), ("sD", 2), ("sD", 3)],
                            [("sD", 0), ("sD", 1), ("sD", 2), ("sD", 3)])
            for hh in range(4):
                self.cp(accN[0:64, hh, TP:TP + NS], sN[0:64, hh, :], [("sN", hh)], ["accN"])
                self.cp(accD[0:64, hh, TP:TP + NS], sD[0:64, hh, :], [("sD", hh)], ["accD"])
        for hh in range(4):
            self.P.dve(lambda e, o=accD[0:64, hh, 0:TT], a=accD[0:64, hh, 0:TT]: e.reciprocal(out=o, in_=a), ["accD"], [("rD", hh)])
            self.tt(self.br[0:64, 10 + hh, 0:TT], accN[0:64, hh, 0:TT], accD[0:64, hh, 0:TT], ALU.mult, ["accN", ("rD", hh)], [("br", 10 + hh)])

    def merge(self, l, ps):
        self.phase()
        TT = self.TTc
        if self.dbg is not None and l == 0 and ps == 0:
            dbt = self.fa(14 * NS).rearrange("p (c b) -> p c b", c=14)
            self.cp(dbt, self.br[:, :, TP:TP + NS], [("br", c) for c in range(14)], ["dbt"])
            self.dma(self.dbg.rearrange("(c p) b -> p c b", p=128), dbt, ["dbt"], ["o_dbg"])
        gt = [self.ba(TT) for _ in range(3)]
        acc = [self.fa(TT) for _ in range(2)]
        KB = ((0, 6), (6, 4), (10, 4))
        for m in range(16):
            slotA, skA = self.next_w("gA")
            for b_ in range(2):
                def ev(pi, p0, pn, bk, psap, b_=b_):
                    self.act(gt[b_][:, p0:p0 + pn], psap, AF.Sigmoid, [bk], [("gt", b_)])
                self.fm(slotA, skA, 256, b_ * 128, 128, KC, self.xb, "xb", ev)
            slotB, skB = self.next_w("gB")
            def ev(pi, p0, pn, bk, psap):
                self.act(gt[2][:, p0:p0 + pn], psap, AF.Sigmoid, [bk], [("gt", 2)])
            self.fm(slotB, skB, 128, 0, 128, KC, self.xb, "xb", ev)
            slotU, skU = self.next_w("up")
            a = acc[m % 2]; ak = ("macc", m % 2)
            for bi_, (k0, nk) in enumerate(KB):
                for (p0, pn) in self.pieces:
                    bi, b = self.bank()
                    for j in range(nk):
                        self.mm(b[:, 0:pn], slotU[:, (k0 + j) * 128:(k0 + j + 1) * 128], self.br[:, k0 + j, p0:p0 + pn], j == 0, j == nk - 1,
                                [skU, ("br", k0 + j)], [("pb", bi)])
                    if bi_ == 0:
                        self.tt(a[:, p0:p0 + pn], b[:, 0:pn], gt[0][:, p0:p0 + pn], ALU.mult, [("pb", bi), ("gt", 0)], [ak])
                    else:
                        t_ = gt[bi_]
                        self.tt(t_[:, p0:p0 + pn], b[:, 0:pn], t_[:, p0:p0 + pn], ALU.mult, [("pb", bi), ("gt", bi_)], [("gt", bi_)])
                        if bi_ == 1:
                            self.tt(a[:, p0:p0 + pn], a[:, p0:p0 + pn], t_[:, p0:p0 + pn], ALU.add, [ak, ("gt", bi_)], [ak], eng="pool")
                        else:
                            self.tt(self.mg[:, m, p0:p0 + pn], a[:, p0:p0 + pn], t_[:, p0:p0 + pn], ALU.add, [ak, ("gt", bi_)], [("mg", m)], eng="pool")
        for t in range(8):
            slot, sk = self.next_w("wo")
            for mc in range(2):
                m = t * 2 + mc
                for (p0, pn) in self.pieces:
                    bi, b = self.bank()
                    for kc in range(KC):
                        self.mm(b[:, 0:pn], slot[:, kc * 256 + mc * 128:kc * 256 + mc * 128 + 128], self.mg[:, kc, p0:p0 + pn], kc == 0, kc == KC - 1,
                                [sk, ("mg", kc)], [("pb", bi)])
                    self.stt(self.xf[:, m, p0:p0 + pn], self.xf[:, m, p0:p0 + pn], ALPHA, b[:, 0:pn], ALU.mult, ALU.add, [("pb", bi), ("xf", m)], [("xf", m)])

    def build(self):
        self.declare(); self.alloc()
        P = self.P
        for dst, src in ((self.UT, self.c_UT), (self.ident, self.c_id), (self.UT4, self.c_UT4), (self.LT4, self.c_LT4),
                         (self.lnp_sb, self.lnp), (self.convw_sb, self.convw), (self.gain_sb, self.gain), (self.bif_sb, self.bif)):
            self.dma(dst[:], src, (), ["c"])
        self.dma(self.M2[:], self.c_M2.rearrange("n p x -> p n x"), (), ["c"])
        self.memset(self.onesD[:], 1.0 / D, ["c"]); self.memset(self.ones[:], 1.0, ["c"]); self.memset(self.onesb[:], 1.0, ["c"])
        for ps in range(self.npass):
            self.phase()
            base = ps * TP
            self.TTc = TP + NS if ps == 0 else TP
            self.pieces = [(0, TP)] + ([(TP, NS)] if ps == 0 else [])
            self.ttiles = [(i * 128, 128) for i in range(4)] + ([(TP, NS)] if ps == 0 else [])
            TT = self.TTc
            self.dma(self.xf[:, :, 0:TP], self.xT.rearrange("(c p) t -> p c t", p=128)[:, :, base:base + TP], (), ["xf"])
            if ps == 0:
                self.dma(self.xf[:, :, TP:TP + NS], self.xsT.rearrange("(c p) t -> p c t", p=128), (), ["xf"])
            for kc in range(KC):
                self.cp(self.xb[:, kc, 0:TT], self.xf[:, kc, 0:TT], ["xf"], [("xb", kc)], eng="act" if kc % 2 else "dve")
            import os
            STOP = int(os.environ.get("KSTOP", "9"))
            for l in range(self.nl):
                self.ffn(l, 0)
                if STOP <= 0: break
                self.layernorm(l, 0)
                if STOP <= 1: break
                self.mlstm(l, ps)
                if STOP <= 2: break
                self.conv(l, ps)
                if STOP <= 3: break
                self.P.dve(lambda e: e.memset(self.br[64:128, 10:14, :], 0.0), (), [("br", 10), ("br", 11), ("br", 12), ("br", 13)])
                self.attn(l, ps)
                if STOP <= 4: break
                self.merge(l, ps)
                self.layernorm(l, 1)
                self.ffn(l, 1)
                self.layernorm(l, 2)
            self.phase()
            self.dma(self.o_yT.rearrange("(c p) t -> p c t", p=128)[:, :, base:base + TP], self.xf[:, :, 0:TP], ["xf"], ["o_y"])
            if ps == 0:
                self.dma(self.o_ysT.rearrange("(c p) t -> p c t", p=128), self.xf[:, :, TP:TP + NS], ["xf"], ["o_ys"])
        self.P.barrier()
        P.emit()


_CACHE = {}


def _prep_core(c, inp, wts, consts):
    b = c % 4
    s0, s1 = c * NS, (c + 1) * NS
    m = dict(consts)
    m["wts"] = wts
    m["xT"] = np.ascontiguousarray(inp["x_prompt"][b].T)
    m["xsT"] = np.ascontiguousarray(inp["x_sample"][s0:s1, 0].T)
    g = inp["ln_g"]; bb = inp["ln_b"]
    lnp = np.stack([g, bb]).reshape(2, NL, 3, KC, 128)
    m["lnp"] = np.ascontiguousarray(lnp.transpose(4, 0, 1, 2, 3).reshape(128, -1))
    cw = inp["conv_w"].reshape(NL, 3, 4, 128)
    m["convw"] = np.ascontiguousarray(cw.transpose(3, 0, 1, 2).reshape(128, -1))
    m["gain"] = np.ascontiguousarray(np.broadcast_to(inp["mlstm_norm_g"].reshape(1, -1), (128, NL * 768)))
    m["bif"] = np.ascontiguousarray(np.broadcast_to(inp["b_gate_if"].reshape(1, -1), (128, NL * 8)))
    m["sC"] = np.ascontiguousarray(inp["state_mlstm_C"][:, s0:s1])
    m["snT"] = np.ascontiguousarray(inp["state_mlstm_n"][:, s0:s1].transpose(0, 2, 3, 1))
    m["sm"] = np.ascontiguousarray(inp["state_mlstm_m"][:, s0:s1])
    m["scT"] = np.ascontiguousarray(inp["state_conv"][:, s0:s1].transpose(0, 2, 3, 1))
    for gi, nm in enumerate(("cache_attn_kv_w128", "cache_attn_kv_w512", "cache_attn_kv_w2048")):
        a = inp[nm][:, s0:s1]
        m[f"ck{gi}"] = np.ascontiguousarray(a.reshape(NL, NS, a.shape[2], 512))
    return m


def build_program(npass=NPASS, nl=NL):
    nc = bass.Bass("TRN2", target_bir_lowering=False)
    k = K(nc, npass, nl)
    k.build()
    return nc, k


def assemble(res, ncores=8):
    f = np.float32
    y_prompt = np.zeros((4, SEQ, D), f); y_sample = np.zeros((128, 1, D), f)
    p_C = np.zeros((NL, 4, 4, 96, 192), f); p_n = np.zeros((NL, 4, 4, 96), f); p_m = np.zeros((NL, 4, 4), f)
    p_conv = np.zeros((NL, 4, 2, 512), f)
    p_kv = [np.zeros((NL, 4, w, 2, 4, 64), f) for w in (128, 512, 2048)]
    s_C = np.zeros((NL, 128, 4, 96, 192), f); s_n = np.zeros((NL, 128, 4, 96), f); s_m = np.zeros((NL, 128, 4), f)
    s_conv = np.zeros((NL, 128, 2, 512), f)
    s_kv = [np.zeros((NL, 128, 1, 2, 4, 64), f) for _ in range(3)]
    for c in range(ncores):
        r = res[c]
        s0, s1 = c * NS, (c + 1) * NS
        if c < 4:
            b = c
            y_prompt[b] = r["o_yT"].T
            p_C[:, b] = r["o_pC"]; p_n[:, b] = r["o_pn"]; p_m[:, b] = r["o_pm"]
            p_conv[:, b] = r["o_pconvT"].transpose(0, 2, 1)
            for g in range(3):
                w = (128, 512, 2048)[g]
                p_kv[g][:, b, :, 0] = r[f"o_pk{g}"].transpose(0, 2, 1).reshape(NL, w, 4, 64)
                p_kv[g][:, b, :, 1] = r[f"o_pv{g}"].reshape(NL, w, 4, 64)
        y_sample[s0:s1, 0] = r["o_ysT"].T
        s_C[:, s0:s1] = r["o_sC"]; s_n[:, s0:s1] = r["o_snT"].transpose(0, 3, 1, 2); s_m[:, s0:s1] = r["o_sm"]
        s_conv[:, s0:s1] = r["o_sconvT"].transpose(0, 3, 1, 2)
        kT = r["o_skT"]; vT = r["o_svT"]
        for g in range(3):
            s_kv[g][:, s0:s1, 0, 0] = kT[:, g * 256:(g + 1) * 256].transpose(0, 2, 1).reshape(NL, NS, 4, 64)
            s_kv[g][:, s0:s1, 0, 1] = vT[:, g * 256:(g + 1) * 256].transpose(0, 2, 1).reshape(NL, NS, 4, 64)
    return (y_prompt, y_sample, p_C, p_n, p_m, p_conv, p_kv[0], p_kv[1], p_kv[2],
            s_C, s_n, s_m, s_conv, s_kv[0], s_kv[1], s_kv[2])


def kernel(**inp):
    inp = {k: np.asarray(v) for k, v in inp.items()}
    wts = np.stack([host_weights(l, inp["w_in"], inp["w_up_mlstm"], inp["w_up_conv"], inp["w_up_attn"], inp["w_o"],
                                 inp["w_ffn_in"], inp["w_ffn_out"]) for l in range(NL)])
    consts = host_consts()
    nc, _ = build_program()
    in_maps = [_prep_core(c, inp, wts, consts) for c in range(8)]
    res = run_bass_kernel_spmd(nc, in_maps, core_ids=list(range(8)))
    return assemble(res.results, 8)
```

```python
import contextlib
import numpy as np
import concourse.bass as bass
import concourse.mybir as mybir

F32 = mybir.dt.float32
BF16 = mybir.dt.bfloat16
AF = mybir.ActivationFunctionType
ALU = mybir.AluOpType
AX = mybir.AxisListType

ENGS = ("pe", "act", "dve", "pool", "sp")
NDMA_SEMS = 8


class Op:
    __slots__ = ("eng", "fn", "deps", "is_dma", "inc", "sem", "val", "qwait")

    def __init__(self, eng, fn, is_dma):
        self.eng = eng
        self.fn = fn
        self.is_dma = is_dma
        self.deps = []
        self.inc = False
        self.sem = None
        self.val = 0
        self.qwait = None


class Prog:
    def __init__(self, nc):
        self.nc = nc
        self.ops = []
        self.last_w = {}
        self.readers = {}
        self.stack = contextlib.ExitStack()
        self.n_sb = 0

    def sb(self, shape, dt, name=None):
        self.n_sb += 1
        return self.stack.enter_context(self.nc.sbuf_tensor(name or f"sb{self.n_sb}", list(shape), dt))

    def ps(self, shape, dt=F32, name=None):
        self.n_sb += 1
        return self.stack.enter_context(self.nc.psum_tensor(name or f"ps{self.n_sb}", list(shape), dt))

    def _add(self, eng, fn, reads, writes, is_dma):
        op = Op(eng, fn, is_dma)
        idx = len(self.ops)
        writes = list(writes) + [k for k in reads if isinstance(k, tuple) and k and k[0] == "pb"]
        deps = set()
        for k in reads:
            w = self.last_w.get(k)
            if w is not None:
                deps.add(w)
        for k in writes:
            w = self.last_w.get(k)
            if w is not None:
                deps.add(w)
            for r in self.readers.get(k, ()):
                deps.add(r)
        for k in reads:
            self.readers.setdefault(k, []).append(idx)
        for k in writes:
            self.last_w[k] = idx
            self.readers[k] = []
        deps.discard(idx)
        op.deps = sorted(deps)
        self.ops.append(op)
        return idx

    def pe(self, fn, reads=(), writes=()):
        return self._add("pe", fn, reads, writes, False)

    def act(self, fn, reads=(), writes=()):
        return self._add("act", fn, reads, writes, False)

    def dve(self, fn, reads=(), writes=()):
        return self._add("dve", fn, reads, writes, False)

    def pool(self, fn, reads=(), writes=()):
        return self._add("pool", fn, reads, writes, False)

    def dma(self, fn, reads=(), writes=(), q="sp"):
        return self._add(q, fn, reads, writes, True)

    def barrier(self):
        last = {}
        dmas = {"sp": [], "pool": []}
        for i, op in enumerate(self.ops):
            if op.fn is None:
                continue
            if op.is_dma:
                dmas[op.eng].append(i)
            else:
                last[op.eng] = i
        deps = sorted(set(list(last.values()) + dmas["sp"][-NDMA_SEMS:] + dmas["pool"][-NDMA_SEMS:]))
        for e in ENGS:
            op = Op(e, None, False)
            op.deps = list(deps)
            self.ops.append(op)
        self.last_w = {}
        self.readers = {}

    def emit(self):
        nc = self.nc
        ops = self.ops
        for op in ops:
            for d in op.deps:
                p = ops[d]
                if p.eng == "pe" and op.eng == "pe" and not p.is_dma and op.fn is not None:
                    continue
                p.inc = True
        sems = {}
        for e in ("pe", "act", "dve", "pool"):
            sems[e] = self.stack.enter_context(nc.semaphore(f"s_{e}"))
        dsems = {}
        for q in ("sp", "pool"):
            dsems[q] = [self.stack.enter_context(nc.semaphore(f"d_{q}{i}")) for i in range(NDMA_SEMS)]
        cnt = {e: 0 for e in sems}
        dcnt = {"sp": 0, "pool": 0}
        for op in ops:
            if op.is_dma:
                i = dcnt[op.eng]
                dcnt[op.eng] += 1
                op.sem = dsems[op.eng][i % NDMA_SEMS]
                op.val = 16 * (i // NDMA_SEMS + 1)
                op.qwait = 16 * (i // NDMA_SEMS)
                op.inc = True
            elif op.inc:
                cnt[op.eng] += 1
                op.sem = sems[op.eng]
                op.val = cnt[op.eng]
        self.final_d = {q: [(dsems[q][i], 16 * ((dcnt[q] - 1 - i) // NDMA_SEMS + 1)) for i in range(min(NDMA_SEMS, dcnt[q]))]
                        for q in dsems}
        by_eng = {e: [] for e in ENGS}
        for op in ops:
            by_eng[op.eng].append(op)
        self.n_wait = 0

        def run(eng_name, eng):
            waited = {}
            for op in by_eng[eng_name]:
                need = {}
                for d in op.deps:
                    p = ops[d]
                    if not p.inc:
                        continue
                    if p.eng == "pe" and eng_name == "pe" and not p.is_dma and op.fn is not None:
                        continue
                    key = id(p.sem)
                    if waited.get(key, 0) >= p.val:
                        continue
                    if key not in need or need[key][1] < p.val:
                        need[key] = (p.sem, p.val)
                if op.is_dma and op.qwait > 0:
                    key = id(op.sem)
                    if waited.get(key, 0) < op.qwait:
                        if key not in need or need[key][1] < op.qwait:
                            need[key] = (op.sem, op.qwait)
                for key, (s, v) in need.items():
                    eng.wait_ge(s, v)
                    waited[key] = v
                    self.n_wait += 1
                if op.fn is None:
                    continue
                ins = op.fn(eng)
                if op.inc:
                    ins.then_inc(op.sem, 16 if op.is_dma else 1)
            if eng_name == "sp":
                for q in ("sp", "pool"):
                    for s, v in self.final_d[q]:
                        eng.wait_ge(s, v)

        with nc.Block() as block:
            @block.tensor
            def _(e):
                run("pe", e)

            @block.scalar
            def _(e):
                run("act", e)

            @block.vector
            def _(e):
                run("dve", e)

            @block.gpsimd
            def _(e):
                run("pool", e)

            @block.sync
            def _(e):
                run("sp", e)
        self.stack.close()

from concourse.bass_utils import run_bass_kernel_spmd

D = 2048; KC = 16; NL = 2; TP = 512; NPASS = 4; NS = 16; DFF = 5632; SEQ = 2048
ALPHA = float((2 * NL) ** 0.25)
EPS = 1e-5
OFF = dict(mq=0, mk=384, mv=768, mi=1536, mf=1540, mo=1544, cb=2312, cc=2824, ch=3336,
           aq=3848, ak=4616, av=5384, gt=6152)
GD = (1, 4, 16)
SLOT_ELEMS = 4096
NSLOT = 5


def layer_plan():
    plan = []

    def add(name, src, k0, kc, cols):
        cols = np.asarray(cols)
        assert kc * len(cols) <= SLOT_ELEMS
        plan.append((name, src, k0, kc, cols))

    def ffn(f):
        for qd in range(4):
            for j in range(11 * qd, 11 * qd + 11):
                add(f"up{f}", f"ffi{f}", 0, 16, np.r_[j * 128:(j + 1) * 128, DFF + j * 128:DFF + (j + 1) * 128])
            for mg in range(8):
                add(f"dn{f}", f"ffo{f}", 11 * qd * 128, 11, np.r_[mg * 256:(mg + 1) * 256])

    ffn(0)
    for hp in range(2):
        add("mq", "win", 0, 16, OFF["mq"] + np.r_[hp * 192:(hp + 1) * 192])
    for hp in range(2):
        add("mk", "win", 0, 16, OFF["mk"] + np.r_[hp * 192:(hp + 1) * 192])
    for h in range(4):
        add("mv", "win", 0, 16, OFF["mv"] + np.r_[h * 192:(h + 1) * 192])
    add("mg", "win", 0, 16, OFF["mi"] + np.r_[0:8])
    for h in range(4):
        add("mo", "win", 0, 16, OFF["mo"] + np.r_[h * 192:(h + 1) * 192])
    for cp in range(2):
        for nm in ("cb", "cc", "ch"):
            add(nm, "win", 0, 16, OFF[nm] + np.r_[cp * 256:(cp + 1) * 256])
    for nm in ("aq", "ak"):
        for t in range(3):
            c = np.r_[t * 256:(t + 1) * 256]
            add(nm, "win", 0, 16, OFF[nm] + c)
            add(nm + "s", "win", 0, 16, OFF[nm] + (c // 64) * 64 + (c % 64 + 32) % 64)
    for t in range(3):
        add("av", "win", 0, 16, OFF["av"] + np.r_[t * 256:(t + 1) * 256])
    for m in range(16):
        g0 = OFF["gt"] + m * 128
        add("gA", "win", 0, 16, np.r_[g0:g0 + 128, g0 + 2048:g0 + 2048 + 128])
        add("gB", "win", 0, 16, np.r_[g0 + 4096:g0 + 4096 + 128])
        add("up", "wup", 0, 14, np.r_[m * 128:(m + 1) * 128])
    for t in range(8):
        add("wo", "wo", 0, 16, np.r_[t * 256:(t + 1) * 256])
    ffn(1)
    offs = []
    o = 0
    for (_, _, _, kc, cols) in plan:
        offs.append(o)
        o += 128 * kc * len(cols)
    return plan, offs, o


PLAN, POFF, PTOT = layer_plan()


def host_weights(l, w_in, w_up_mlstm, w_up_conv, w_up_attn, w_o, w_ffn_in, w_ffn_out):
    wa = np.zeros((512, D), np.float32)
    for s in range(4):
        wa[s * 128:s * 128 + 64] = w_up_attn[l][s * 64:(s + 1) * 64]
    src = dict(ffi0=w_ffn_in[l, 0], ffi1=w_ffn_in[l, 1], ffo0=w_ffn_out[l, 0], ffo1=w_ffn_out[l, 1],
               win=w_in[l], wo=w_o[l], wup=np.concatenate([w_up_mlstm[l], w_up_conv[l], wa], axis=0))
    out = np.empty((PTOT,), np.float32)
    for (name, s, k0, kc, cols), o in zip(PLAN, POFF):
        W = src[s]
        n = len(cols)
        if n > 1 and np.all(np.diff(cols) == 1):
            blk = W[k0:k0 + kc * 128, cols[0]:cols[0] + n]
        else:
            blk = W[k0:k0 + kc * 128][:, cols]
        out[o:o + 128 * kc * n] = blk.reshape(kc, 128, n).transpose(1, 0, 2).reshape(-1)
    return out


def host_consts():
    c = {}
    half = 32
    inv = (10000.0 ** (-(2.0 * np.arange(half, dtype=np.float32)) / 64)).astype(np.float32)
    pos = np.arange(SEQ + 1, dtype=np.float32)
    ang = (pos[None, :] * np.tile(inv, 4)[:, None]).astype(np.float32)
    sgn = np.where((np.arange(128) % 64) < 32, -1.0, 1.0).astype(np.float32)[:, None]
    c["cosT"] = np.cos(ang).astype(np.float32)
    c["sinT"] = (np.sin(ang) * sgn).astype(np.float32)
    k = np.arange(128)
    UT = (k[:, None] <= k[None, :]).astype(np.float32)
    LT = (k[:, None] >= k[None, :]).astype(np.float32)
    c["UT"] = UT
    c["ident"] = np.eye(128, dtype=np.float32)
    c["UT4"] = np.tile(UT, (1, 4)).astype(np.float32)
    c["LT4"] = np.tile(LT, (1, 4)).astype(np.float32)
    m2 = np.zeros((NPASS, 128, 128), np.float32)
    for p in range(NPASS):
        blk = (k[:, None] <= (np.arange(32)[None, :] + 32 * p)).astype(np.float32)
        m2[p] = np.tile(blk, (1, 4))
    c["M2"] = m2
    return c


class K:
    def __init__(self, nc, npass=NPASS, nl=NL):
        self.nc = nc
        self.P = Prog(nc)
        self.npass = npass
        self.nl = nl
        self.bank_i = 0
        self.wnext = 0
        self.wissued = 0

    def declare(self):
        nc = self.nc
        di = lambda n, s: nc.dram_tensor(n, list(s), F32, kind="ExternalInput").ap()
        do = lambda n, s: nc.dram_tensor(n, list(s), F32, kind="ExternalOutput").ap()
        self.wts = di("wts", [self.nl, PTOT])
        self.xT = di("xT", [D, SEQ])
        self.xsT = di("xsT", [D, NS])
        self.cosT = di("cosT", [128, SEQ + 1]); self.sinT = di("sinT", [128, SEQ + 1])
        self.c_UT = di("UT", [128, 128]); self.c_id = di("ident", [128, 128])
        self.c_UT4 = di("UT4", [128, 512]); self.c_LT4 = di("LT4", [128, 512]); self.c_M2 = di("M2", [NPASS, 128, 128])
        self.lnp = di("lnp", [128, 2 * NL * 3 * KC])
        self.convw = di("convw", [128, NL * 3 * 4])
        self.gain = di("gain", [128, NL * 768])
        self.bif = di("bif", [128, NL * 8])
        self.sC = di("sC", [NL, NS, 4, 96, 192])
        self.snT = di("snT", [NL, 4, 96, NS])
        self.sm = di("sm", [NL, NS, 4])
        self.scT = di("scT", [NL, 2, 512, NS])
        self.ck = [di(f"ck{g}", [NL, NS, (128, 512, 2048)[g], 512]) for g in range(3)]
        self.o_yT = do("o_yT", [D, SEQ]); self.o_ysT = do("o_ysT", [D, NS])
        self.o_pC = do("o_pC", [NL, 4, 96, 192]); self.o_pn = do("o_pn", [NL, 4, 96]); self.o_pm = do("o_pm", [NL, 4])
        self.o_pconvT = do("o_pconvT", [NL, 512, 2])
        self.o_pk = [do(f"o_pk{g}", [NL, 256, (128, 512, 2048)[g]]) for g in range(3)]
        self.o_pv = [do(f"o_pv{g}", [NL, (128, 512, 2048)[g], 256]) for g in range(3)]
        self.o_sC = do("o_sC", [NL, NS, 4, 96, 192]); self.o_snT = do("o_snT", [NL, 4, 96, NS]); self.o_sm = do("o_sm", [NL, NS, 4])
        self.o_sconvT = do("o_sconvT", [NL, 2, 512, NS])
        self.o_skT = do("o_skT", [NL, 768, NS]); self.o_svT = do("o_svT", [NL, 768, NS])
        import os
        self.dbg = do("o_dbg", [14 * 128, NS]) if "KDBG" in os.environ else None
        self.hK = [[nc.dram_tensor(f"hK{l}_{g}", [256, SEQ], BF16, kind="Internal").ap() for g in range(3)] for l in range(NL)]
        self.hV = [[nc.dram_tensor(f"hV{l}_{g}", [SEQ, 256], BF16, kind="Internal").ap() for g in range(3)] for l in range(NL)]

    def alloc(self):
        P = self.P
        TT = TP + NS
        self.TT = TT
        self.xf = P.sb([128, KC, TT], F32, "xf")
        self.xb = P.sb([128, KC, TT], BF16, "xb")
        self.br = P.sb([128, 14, TT], BF16, "br")
        self.mg = P.sb([128, KC, TT], BF16, "mg")
        self.ws = [P.sb([128, SLOT_ELEMS], BF16, f"ws{i}") for i in range(NSLOT)]
        self.pb = [P.ps([128, 512], F32, f"pb{i}") for i in range(8)]
        self.AF = P.sb([128, 8912], F32, "arenaF")
        self.AB = P.sb([128, 13264], BF16, "arenaB")
        self.UT = P.sb([128, 128], F32, "cUT"); self.ident = P.sb([128, 128], F32, "cid")
        self.UT4 = P.sb([128, 512], F32, "cUT4"); self.LT4 = P.sb([128, 512], F32, "cLT4"); self.M2 = P.sb([128, NPASS, 128], F32, "cM2")
        self.onesD = P.sb([128, 128], F32, "onesD"); self.ones = P.sb([128, 128], F32, "ones1"); self.onesb = P.sb([128, 64], BF16, "onesb")
        self.lnp_sb = P.sb([128, 2 * NL * 3 * KC], F32, "lnp_s")
        self.convw_sb = P.sb([128, NL * 12], F32, "convw_s")
        self.gain_sb = P.sb([128, NL * 768], F32, "gain_s")
        self.bif_sb = P.sb([128, NL * 8], F32, "bif_s")
        self.Cst = [[P.sb([96, 194], F32, f"Cst{l}_{h}") for h in range(4)] for l in range(NL)]
        self.Cbf = [[P.sb([96, 194], BF16, f"Cbf{l}_{h}") for h in range(4)] for l in range(NL)]
        self.mst = [P.sb([4, 1], F32, f"mst{l}") for l in range(NL)]
        self.ctail = [P.sb([128, 4, 2], F32, f"ctail{l}") for l in range(NL)]
        self.fo = 0; self.bo = 0

    def phase(self):
        self.P.barrier()
        self.fo = 0; self.bo = 0

    def fa(self, n):
        a = self.AF[:, self.fo:self.fo + n]; self.fo += n; self.fmax = max(getattr(self, 'fmax', 0), self.fo)
        assert self.fo <= 8912, self.fo
        return a

    def ba(self, n):
        a = self.AB[:, self.bo:self.bo + n]; self.bo += n; self.bmax = max(getattr(self, 'bmax', 0), self.bo)
        assert self.bo <= 13264, self.bo
        return a

    def bank(self):
        i = self.bank_i % 8
        self.bank_i += 1
        return i, self.pb[i]

    def mm(self, out, lhsT, rhs, start, stop, r, w):
        self.P.pe(lambda e: e.matmul(out, lhsT=lhsT, rhs=rhs, start=start, stop=stop), r, w)

    def tr(self, out, in_, ident, r, w):
        self.P.pe(lambda e: e.transpose(out, in_, ident), r, w)

    def act(self, out, in_, func, r, w, bias=None, scale=None):
        kw = {}
        if bias is not None: kw["bias"] = bias
        if scale is not None: kw["scale"] = scale
        self.P.act(lambda e: e.activation(out=out, in_=in_, func=func, **kw), r, w)

    def tt(self, out, in0, in1, op, r, w, eng="dve"):
        getattr(self.P, eng)(lambda e: e.tensor_tensor(out=out, in0=in0, in1=in1, op=op), r, w)

    def ts(self, out, in0, s1, s2, op0, op1, r, w, eng="dve"):
        if s2 is None:
            getattr(self.P, eng)(lambda e: e.tensor_scalar(out=out, in0=in0, scalar1=s1, scalar2=None, op0=op0), r, w)
        else:
            getattr(self.P, eng)(lambda e: e.tensor_scalar(out=out, in0=in0, scalar1=s1, scalar2=s2, op0=op0, op1=op1), r, w)

    def stt(self, out, in0, scalar, in1, op0, op1, r, w, eng="dve"):
        getattr(self.P, eng)(lambda e: e.scalar_tensor_tensor(out=out, in0=in0, scalar=scalar, in1=in1, op0=op0, op1=op1), r, w)

    def cp(self, out, in_, r, w, eng="dve"):
        if eng == "act":
            self.P.act(lambda e: e.copy(out=out, in_=in_), r, w)
        else:
            getattr(self.P, eng)(lambda e: e.tensor_copy(out=out, in_=in_), r, w)

    def memset(self, ap, v, w, eng="dve"):
        getattr(self.P, eng)(lambda e: e.memset(ap, v), (), w)

    def dma(self, out, in_, r, w, q="sp"):
        self.P.dma(lambda e: e.dma_start(out=out, in_=in_, allow_slow_non_contiguous=True), r, w, q=q)

    def _issue(self, g):
        per = len(PLAN)
        li = (g // per) % self.nl
        idx = g % per
        name, src, k0, kc, cols = PLAN[idx]
        n = kc * len(cols)
        s = g % NSLOT
        src_ap = self.wts[li, POFF[idx]:POFF[idx] + 128 * n].rearrange("(p n) -> p n", p=128)
        self.dma(self.ws[s][:, 0:n], src_ap, (), [("ws", s)], q="pool")

    def next_w(self, expect):
        g = self.wnext
        import os
        total = self.npass * self.nl * len(PLAN)
        lim = {0: 76, 1: 76, 2: 76 + 13, 3: 76 + 19, 4: 76 + 34}.get(int(os.environ.get("KSTOP", "9")))
        if lim is not None:
            total = lim
        if "KPROJ" in os.environ:
            total = 76 + int(os.environ["KPROJ"])
        if os.environ.get("KATT") == "1":
            total = 76 + 19 + 12
        while self.wissued < min(total, g + NSLOT - 1):
            self._issue(self.wissued)
            self.wissued += 1
        name = PLAN[g % len(PLAN)][0]
        assert name == expect, (name, expect, g)
        self.wnext += 1
        s = g % NSLOT
        return self.ws[s], ("ws", s)

    def fm(self, slot, skey, stride, c0, M, nk, src, srckey, evac):
        for pi, (p0, pn) in enumerate(self.pieces):
            bi, b = self.bank()
            for kc in range(nk):
                self.mm(b[0:M, 0:pn], slot[:, kc * stride + c0:kc * stride + c0 + M], src[:, kc, p0:p0 + pn],
                        kc == 0, kc == nk - 1, [skey, srckey], [("pb", bi)])
            evac(pi, p0, pn, ("pb", bi), b[0:M, 0:pn])

    def tmj(self, slot, skey, stride, c0, N, evac):
        for ti, (t0, tn) in enumerate(self.ttiles):
            bi, b = self.bank()
            for kc in range(KC):
                self.mm(b[0:tn, 0:N], self.xb[:, kc, t0:t0 + tn], slot[:, kc * stride + c0:kc * stride + c0 + N],
                        kc == 0, kc == KC - 1, [skey, "xb"], [("pb", bi)])
            evac(ti, t0, tn, ("pb", bi), b[0:tn, 0:N])

    def layernorm(self, l, i):
        self.phase()
        TT = self.TTc
        sq = [self.fa(512) for _ in range(2)]
        mean = self.fa(TT); rstd = self.fa(TT); tmp = [self.fa(TT) for _ in range(2)]
        gcol = lambda kc: self.lnp_sb[:, ((0 * NL + l) * 3 + i) * KC + kc:((0 * NL + l) * 3 + i) * KC + kc + 1]
        bcol = lambda kc: self.lnp_sb[:, ((1 * NL + l) * 3 + i) * KC + kc:((1 * NL + l) * 3 + i) * KC + kc + 1]
        for (p0, pn) in self.pieces:
            b1i, b1 = self.bank(); b2i, b2 = self.bank()
            for kc in range(KC):
                s = sq[kc % 2]
                self.act(s[:, 0:pn], self.xf[:, kc, p0:p0 + pn], AF.Square, ["xf"], [("sq", kc % 2)])
                self.mm(b1[:, 0:pn], self.onesD[:], self.xf[:, kc, p0:p0 + pn], kc == 0, kc == KC - 1, ["xf", "c"], [("pb", b1i)])
                self.mm(b2[:, 0:pn], self.onesD[:], s[:, 0:pn], kc == 0, kc == KC - 1, [("sq", kc % 2), "c"], [("pb", b2i)])
            self.cp(mean[:, p0:p0 + pn], b1[:, 0:pn], [("pb", b1i)], ["mean"])
            self.tt(tmp[0][:, p0:p0 + pn], mean[:, p0:p0 + pn], mean[:, p0:p0 + pn], ALU.mult, ["mean"], [("lt", 0)])
            self.tt(tmp[1][:, p0:p0 + pn], b2[:, 0:pn], tmp[0][:, p0:p0 + pn], ALU.subtract, [("pb", b2i), ("lt", 0)], [("lt", 1)])
            self.ts(tmp[1][:, p0:p0 + pn], tmp[1][:, p0:p0 + pn], EPS, None, ALU.add, None, [("lt", 1)], [("lt", 1)])
            self.act(tmp[0][:, p0:p0 + pn], tmp[1][:, p0:p0 + pn], AF.Sqrt, [("lt", 1)], [("lt", 0)])
            self.P.dve(lambda e, o=rstd[:, p0:p0 + pn], a=tmp[0][:, p0:p0 + pn]: e.reciprocal(out=o, in_=a), [("lt", 0)], ["rstd"])
        for kc in range(KC):
            t = tmp[kc % 2]
            self.tt(t[:, 0:TT], self.xf[:, kc, 0:TT], mean[:, 0:TT], ALU.subtract, ["xf", "mean"], [("lt", kc % 2)])
            self.tt(t[:, 0:TT], t[:, 0:TT], rstd[:, 0:TT], ALU.mult, [("lt", kc % 2), "rstd"], [("lt", kc % 2)], eng="pool")
            self.act(self.xf[:, kc, 0:TT], t[:, 0:TT], AF.Identity, [("lt", kc % 2)], [("xfo", kc)], bias=bcol(kc), scale=gcol(kc))
            self.act(self.xb[:, kc, 0:TT], t[:, 0:TT], AF.Identity, [("lt", kc % 2)], [("xbo", kc)], bias=bcol(kc), scale=gcol(kc))
        self.phase()

    def ffn(self, l, f):
        TT = self.TTc
        self.phase()
        sa = [self.fa(512) for _ in range(2)]
        for qd in range(4):
            for jl in range(11):
                slot, sk = self.next_w(f"up{f}")
                for pi, (p0, pn) in enumerate(self.pieces):
                    ai, a = self.bank(); bi, b = self.bank()
                    for kc in range(KC):
                        self.mm(a[:, 0:pn], slot[:, kc * 256:kc * 256 + 128], self.xb[:, kc, p0:p0 + pn], kc == 0, kc == KC - 1, [sk, "xb"], [("pb", ai)])
                    for kc in range(KC):
                        self.mm(b[:, 0:pn], slot[:, kc * 256 + 128:kc * 256 + 256], self.xb[:, kc, p0:p0 + pn], kc == 0, kc == KC - 1, [sk, "xb"], [("pb", bi)])
                    s = sa[(jl + pi) % 2]; skk = ("sa", (jl + pi) % 2)
                    self.act(s[:, 0:pn], a[:, 0:pn], AF.Silu, [("pb", ai)], [skk])
                    self.stt(self.br[:, jl, p0:p0 + pn], s[:, 0:pn], 0.5, b[:, 0:pn], ALU.mult, ALU.mult, [skk, ("pb", bi)], [("br", jl)])
            for mgi in range(8):
                slot, sk = self.next_w(f"dn{f}")
                for mc in range(2):
                    m = mgi * 2 + mc
                    for (p0, pn) in self.pieces:
                        bi, b = self.bank()
                        for jl in range(11):
                            self.mm(b[:, 0:pn], slot[:, jl * 256 + mc * 128:jl * 256 + mc * 128 + 128], self.br[:, jl, p0:p0 + pn],
                                    jl == 0, jl == 10, [sk, ("br", jl)], [("pb", bi)])
                        if qd == 0:
                            self.stt(self.xf[:, m, p0:p0 + pn], self.xf[:, m, p0:p0 + pn], ALPHA, b[:, 0:pn], ALU.mult, ALU.add,
                                     [("pb", bi), ("xf", m)], [("xf", m)])
                        else:
                            self.tt(self.xf[:, m, p0:p0 + pn], self.xf[:, m, p0:p0 + pn], b[:, 0:pn], ALU.add, [("pb", bi), ("xf", m)], [("xf", m)])

    def mlstm(self, l, ps):
        self.phase()
        TT = self.TTc; nt = len(self.ttiles); DKS = 96 ** -0.5
        qs32 = [self.fa(NS) for _ in range(4)]; ks32 = [self.fa(NS) for _ in range(4)]
        v32s = self.fa(772)
        ktm_s = self.fa(384); g8_s = self.fa(8); og_s = self.ba(768); vaug_s = self.ba(784)
        mark_f, mark_b = self.fo, self.bo
        qT = [self.ba(TT) for _ in range(4)]; kT = [self.ba(TT) for _ in range(4)]
        ktm = [self.fa(384) for _ in range(4)] + [ktm_s]
        vaug = [self.ba(784) for _ in range(4)] + [vaug_s]
        og = [self.ba(768) for _ in range(4)] + [og_s]
        g8 = [self.fa(8) for _ in range(4)] + [g8_s]
        for ti in range(nt):
            self.memset(vaug[ti], 1.0, [("vaug", ti)])
        self.memset(v32s, 1.0, ["v32s"])
        for nm, dst, d32 in (("mq", qT, qs32), ("mk", kT, ks32)):
            for hp in range(2):
                slot, sk = self.next_w(nm)
                for hh in range(2):
                    h = hp * 2 + hh
                    def ev(pi, p0, pn, bk, psap, h=h, dst=dst, d32=d32, nm=nm):
                        if pi == 0:
                            self.cp(dst[h][0:96, 0:pn], psap[0:96, :], [bk], [(nm, h)], eng="act")
                        else:
                            self.cp(d32[h][0:96, 0:NS], psap[0:96, :], [bk], [(nm + "s", h)])
                    self.fm(slot, sk, 192, hh * 96, 96, KC, self.xb, "xb", ev)
                if nm == "mk":
                    def ev2(ti, t0, tn, bk, psap, hp=hp):
                        self.cp(ktm[ti][0:tn, hp * 192:(hp + 1) * 192], psap, [bk], [("ktm", ti, hp)])
                    self.tmj(slot, sk, 192, 0, 192, ev2)
        import os
        KPROJ = int(os.environ.get("KPROJ", "99"))
        if KPROJ <= 4: return
        for h in range(4):
            slot, sk = self.next_w("mv")
            def ev(ti, t0, tn, bk, psap, h=h):
                self.cp(vaug[ti][0:tn, h * 196:h * 196 + 192], psap, [bk], [("vaug", ti)], eng=os.environ.get("KENG", "act"))
                if tn == NS and "KNOV32" not in os.environ:
                    self.cp(v32s[0:NS, h * 193:h * 193 + 192], psap, [bk], ["v32s"])
            self.tmj(slot, sk, 192, 0, 192, ev)
            if KPROJ <= 5 + h: return
        if KPROJ <= 8: return
        slot, sk = self.next_w("mg")
        def ev(ti, t0, tn, bk, psap):
            self.tt(g8[ti][0:tn, :], psap, self.bif_sb[0:tn, l * 8:(l + 1) * 8], ALU.add, [bk], [("g8", ti)])
        self.tmj(slot, sk, 8, 0, 8, ev)
        if KPROJ <= 9: return
        for h in range(4):
            slot, sk = self.next_w("mo")
            def ev(ti, t0, tn, bk, psap, h=h):
                self.act(og[ti][0:tn, h * 192:(h + 1) * 192], psap, AF.Sigmoid, [bk], [("og", ti)])
            self.tmj(slot, sk, 192, 0, 192, ev)
        import os
        KSUB = int(os.environ.get("KSUB", "9"))
        if KSUB <= 1: return
        sm = [self.fa(48) for _ in range(2)]
        hp_ = [self.fa(192) for _ in range(4)]
        hm = [self.fa(768) for _ in range(2)]
        PT = [self.ba(128) for _ in range(4)]
        ktil = [self.ba(384) for _ in range(2)]
        bn = [self.fa(8) for _ in range(4)]
        flb = self.fa(4)
        WB = dict(hp=hp_, hm=hm, bn=bn)
        gain = self.gain_sb[:, l * 768:(l + 1) * 768]
        def headnorm_g(src_ps, sc_col, tn, h, ti, par, rk, hb=0):
            hpp = WB['hp'][hb]; b6 = WB['bn'][hb]; hm = WB['hm']
            hk = ("hp", hb); bk_ = ("bn", hb)
            self.act(hpp[0:tn, :], src_ps, AF.Copy, rk, [hk], scale=sc_col); yield
            self.P.dve(lambda e: e.bn_stats(out=b6[0:tn, 0:6], in_=hpp[0:tn, :]), [hk], [bk_]); yield
            self.P.dve(lambda e: e.bn_aggr(out=b6[0:tn, 6:8], in_=b6[0:tn, 0:6]), [bk_], [bk_]); yield
            self.ts(b6[0:tn, 7:8], b6[0:tn, 7:8], EPS, None, ALU.add, None, [bk_], [bk_]); yield
            self.act(b6[0:tn, 7:8], b6[0:tn, 7:8], AF.Sqrt, [bk_], [bk_]); yield
            self.P.dve(lambda e: e.reciprocal(out=b6[0:tn, 7:8], in_=b6[0:tn, 7:8]), [bk_], [bk_]); yield
            self.ts(hpp[0:tn, :], hpp[0:tn, :], b6[0:tn, 6:7], b6[0:tn, 7:8], ALU.subtract, ALU.mult, [bk_, hk], [hk]); yield
            self.tt(hpp[0:tn, :], hpp[0:tn, :], gain[0:tn, h * 192:(h + 1) * 192], ALU.mult, [hk], [hk], eng="pool"); yield
            self.tt(hm[par][0:tn, h * 192:(h + 1) * 192], hpp[0:tn, :], og[ti][0:tn, h * 192:(h + 1) * 192], ALU.mult,
                    [hk, ("og", ti)], [("hm", par)]); yield
        def headnorm(*a, **k):
            for _ in headnorm_g(*a, **k):
                pass
        def rr(gens):
            gens = list(gens)
            while gens:
                for g_ in list(gens):
                    try:
                        next(g_)
                    except StopIteration:
                        gens.remove(g_)
        def to_fm(par, t0, tn):
            hm = WB['hm']
            for c in range(6):
                bi, b = self.bank()
                self.tr(b[:, 0:tn], hm[par][0:tn, c * 128:(c + 1) * 128], self.ident[0:tn, 0:tn], [("hm", par)], [("pb", bi)])
                self.cp(self.br[:, c, t0:t0 + tn], b[:, 0:tn], [("pb", bi)], [("br", c)], eng="act")
        if ps == 0:
            for h in range(4):
                self.memset(self.Cst[l][h][:], 0.0, [("Cst", h)])
                self.memset(self.Cbf[l][h][:], 0.0, [("Cbf", h)])
            self.memset(self.mst[l][:], 0.0, ["mst"])
        for ti in range(4):
            t0 = ti * 128; par = ti % 2; s = sm[par]; sk_ = ("sm", par)
            self.act(s[:, 0:4], g8[ti][:, 4:8], AF.Exp, [("g8", ti)], [sk_], scale=-1.0)
            self.act(s[:, 0:4], s[:, 0:4], AF.Ln, [sk_], [sk_], bias=1.0)
            ci, cb_ = self.bank(); toi, tob = self.bank()
            self.mm(cb_[:, 0:4], self.UT[:], s[:, 0:4], True, True, [sk_, "c"], [("pb", ci)])
            self.mm(tob[:, 0:4], self.ones[:], s[:, 0:4], True, True, [sk_, "c"], [("pb", toi)])
            self.act(s[:, 4:8], cb_[:, 0:4], AF.Exp, [("pb", ci)], [sk_], scale=-1.0)
            self.tt(s[:, 16:20], g8[ti][:, 0:4], cb_[:, 0:4], ALU.add, [("g8", ti), ("pb", ci)], [sk_])
            self.act(s[:, 8:12], s[:, 16:20], AF.Exp, [sk_], [sk_])
            self.ts(s[:, 8:12], s[:, 8:12], DKS, None, ALU.mult, None, [sk_], [sk_])
            self.tt(s[:, 24:28], s[:, 16:20], tob[:, 0:4], ALU.subtract, [sk_, ("pb", toi)], [sk_])
            self.act(s[:, 12:16], s[:, 24:28], AF.Exp, [sk_], [sk_])
            self.cp(s[:, 20:24], tob[:, 0:4], [("pb", toi)], [sk_])
            fi, fb = self.bank()
            self.mm(fb[0:96, 0:4], self.onesD[:, 0:96], s[:, 20:24], True, True, [sk_, "c"], [("pb", fi)])
            self.act(flb[0:96, 0:4], fb[0:96, 0:4], AF.Exp, [("pb", fi)], ["flb"], scale=-16.0)
            gi, gb = self.bank(); tti, ttb = self.bank()
            self.tr(gb[0:4, 0:128], s[:, 24:28], self.ident[:], [sk_], [("pb", gi)])
            self.tr(ttb[0:4, 0:128], s[:, 20:24], self.ident[:], [sk_], [("pb", tti)])
            self.P.dve(lambda e, o=s[0:4, 32:33], a=gb[0:4, 0:128]: e.reduce_max(out=o, in_=a, axis=AX.X), [("pb", gi)], [("smx", par)])
            self.cp(s[0:4, 33:34], ttb[0:4, 0:1], [("pb", tti)], [("smx", par)])
            self.stt(self.mst[l][:], self.mst[l][:], s[0:4, 33:34], s[0:4, 32:33], ALU.subtract, ALU.max, [("smx", par), "mst"], ["mst"])
            for h in range(4):
                self.ts(ktil[par][:, h * 96:(h + 1) * 96], ktm[ti][:, h * 96:(h + 1) * 96], s[:, 12 + h:13 + h], DKS, ALU.mult, ALU.mult,
                        [sk_, ("ktm", ti, h // 2)], [("ktil", par)], eng="pool")
            nbs = []
            for h in range(4):
                ai, ab = self.bank()
                self.mm(ab[:, 0:128], kT[h][0:96, t0:t0 + 128], qT[h][0:96, t0:t0 + 128], True, True, [("mk", h), ("mq", h)], [("pb", ai)])
                self.stt(PT[h][:, :], ab[:, 0:128], s[:, 8 + h:9 + h], self.UT[:], ALU.mult, ALU.mult, [("pb", ai), sk_], [("PT", h)])
            for h in range(4):
                ni, nb = self.bank(); nbs.append((ni, nb))
                self.mm(nb[:, 0:194], PT[h][:, :], vaug[ti][:, h * 196:h * 196 + 194], True, False, [("PT", h), ("vaug", ti)], [("pb", ni)])
                self.mm(nb[:, 0:194], qT[h][0:96, t0:t0 + 128], self.Cbf[l][h][:], False, True, [("mq", h), ("Cbf", h)], [("pb", ni)])
            def post(h):
                ni, nb = nbs[h]
                dcol = s[:, 40 + 2 * h:41 + 2 * h]; rcol = s[:, 41 + 2 * h:42 + 2 * h]; dk = ("dn", par, h)
                self.ts(dcol, nb[:, 192:193], s[:, 4 + h:5 + h], None, ALU.mult, None, [("pb", ni), sk_], [dk]); yield
                self.stt(dcol, dcol, -1.0, dcol, ALU.mult, ALU.max, [dk], [dk]); yield
                self.ts(dcol, dcol, 1.0, None, ALU.max, None, [dk], [dk]); yield
                self.P.dve(lambda e, o=rcol, a=dcol: e.reciprocal(out=o, in_=a), [dk], [dk]); yield
                self.tt(rcol, rcol, s[:, 4 + h:5 + h], ALU.mult, [dk, sk_], [dk]); yield
                yield from headnorm_g(nb[:, 0:192], rcol, 128, h, ti, par, [("pb", ni), dk], hb=h)
            rr(post(h) for h in range(4))
            for h in range(4):
                di, db = self.bank()
                self.mm(db[0:96, 0:194], ktil[par][:, h * 96:(h + 1) * 96], vaug[ti][:, h * 196:h * 196 + 194], True, True,
                        [("ktil", par), ("vaug", ti)], [("pb", di)])
                self.stt(self.Cst[l][h][:], self.Cst[l][h][:], flb[0:96, h:h + 1], db[0:96, 0:194], ALU.mult, ALU.add,
                         [("pb", di), "flb", ("Cst", h)], [("Cst", h)])
                self.cp(self.Cbf[l][h][:], self.Cst[l][h][:], [("Cst", h)], [("Cbf", h)], eng="act")
            to_fm(par, t0, 128)
        if KSUB <= 2: return
        if ps == self.npass - 1:
            s = sm[0]
            di4 = self.fa(4); emb = self.fa(4)
            self.ts(di4[0:4, 0:4], self.ident[0:4, 0:4], self.mst[l][:, 0:1], None, ALU.mult, None, ["mst"], ["di4"])
            bi, b = self.bank()
            self.mm(b[0:96, 0:4], self.ones[0:4, 0:96], di4[0:4, 0:4], True, True, ["di4", "c"], [("pb", bi)])
            self.act(emb[0:96, 0:4], b[0:96, 0:4], AF.Exp, [("pb", bi)], ["emb"], scale=-1.0)
            self.dma(self.o_pm[l].rearrange("(h o) -> h o", o=1), self.mst[l][:, 0:1], ["mst"], ["o_pm"])
            for h in range(4):
                co = self.fa(193)
                self.ts(co[0:96, :], self.Cst[l][h][:, 0:193], emb[0:96, h:h + 1], None, ALU.mult, None, [("Cst", h), "emb"], [("co", h)])
                self.dma(self.o_pC[l, h], co[0:96, 0:192], [("co", h)], ["o_pC"])
                self.dma(self.o_pn[l, h].rearrange("(k o) -> k o", o=1), co[0:96, 192:193], [("co", h)], ["o_pn"])
        if ps != 0 or KSUB <= 3:
            return
        self.P.barrier()
        self.fo, self.bo = mark_f, mark_b
        sm = [self.fa(40)]; WB['hp'] = [self.fa(192)]; WB['hm'] = [self.fa(768)]; WB['bn'] = [self.fa(8)]
        ti = 4; t0 = TP; s = sm[0]; sk_ = ("sms",)
        m0 = self.fa(4)
        self.dma(m0[0:NS, :], self.sm[l], (), ["m0"])
        self.act(s[0:NS, 0:4], g8[ti][0:NS, 4:8], AF.Exp, [("g8", ti)], [sk_], scale=-1.0)
        self.act(s[0:NS, 0:4], s[0:NS, 0:4], AF.Ln, [sk_], [sk_], bias=1.0)
        self.tt(s[0:NS, 4:8], m0[0:NS, :], s[0:NS, 0:4], ALU.subtract, ["m0", sk_], [sk_])
        self.tt(s[0:NS, 8:12], s[0:NS, 4:8], g8[ti][0:NS, 0:4], ALU.max, [sk_, ("g8", ti)], [sk_])
        self.dma(self.o_sm[l], s[0:NS, 8:12], [sk_], ["o_sm"])
        self.tt(s[0:NS, 12:16], s[0:NS, 4:8], s[0:NS, 8:12], ALU.subtract, [sk_], [sk_])
        self.act(s[0:NS, 12:16], s[0:NS, 12:16], AF.Exp, [sk_], [sk_])
        self.tt(s[0:NS, 16:20], g8[ti][0:NS, 0:4], s[0:NS, 8:12], ALU.subtract, [sk_, ("g8", ti)], [sk_])
        self.act(s[0:NS, 16:20], s[0:NS, 16:20], AF.Exp, [sk_], [sk_])
        self.act(s[0:NS, 20:24], g8[ti][0:NS, 0:4], AF.Exp, [("g8", ti)], [sk_])
        self.act(s[0:NS, 24:28], s[0:NS, 4:8], AF.Exp, [sk_], [sk_])
        Dm = self.fa(64); dbc = self.fa(64)
        for b_ in range(NS):
            self.ts(Dm[0:NS, b_ * 4:(b_ + 1) * 4], s[0:NS, 12:16], self.ident[0:NS, b_:b_ + 1], None, ALU.mult, None, [sk_], ["Dm"])
        bi, b = self.bank()
        self.mm(b[0:96, 0:64], self.ones[0:NS, 0:96], Dm[0:NS, 0:64], True, True, ["Dm", "c"], [("pb", bi)])
        self.cp(dbc[0:96, 0:64], b[0:96, 0:64], [("pb", bi)], ["dbc"])
        qk = self.fa(4); pr = self.fa(NS)
        for h in range(4):
            self.tt(pr[0:96, 0:NS], qs32[h][0:96, 0:NS], ks32[h][0:96, 0:NS], ALU.mult, [("mqs", h), ("mks", h)], ["pr"])
            bi, b = self.bank()
            self.mm(b[0:NS, 0:1], pr[0:96, 0:NS], self.ones[0:96, 0:1], True, True, ["pr", "c"], [("pb", bi)])
            self.ts(qk[0:NS, h:h + 1], b[0:NS, 0:1], DKS, None, ALU.mult, None, [("pb", bi)], [("qk", h)])
        Cin = self.fa(NS * 193); nin = self.fa(NS); QE = self.fa(NS * NS); KE = self.fa(NS * 96); vt = self.fa(193)
        Cn = [self.fa(193) for _ in range(2)]; nout = self.fa(NS); hsrc = self.fa(193)
        Cin3 = Cin.rearrange("p (b v) -> p b v", b=NS)
        self.memset(QE[0:96, :], 0.0, ["QE"])
        for h in range(4):
            self.dma(Cin3[0:96, :, 0:192], self.sC[l, :, h].rearrange("b k v -> k b v"), (), [("Cin", h)])
            self.dma(nin[0:96, 0:NS], self.snT[l, h], (), [("nin", h)])
            self.cp(Cin3[0:96, :, 192], nin[0:96, 0:NS], [("nin", h)], [("Cin", h)])
            self.cp(QE[0:96, 0:NS * NS:NS + 1], qs32[h][0:96, 0:NS], [("mqs", h)], ["QE"])
            QE3 = QE.rearrange("p (a b) -> p a b", a=NS)
            qi, qb = self.bank()
            for b_ in range(NS):
                self.mm(qb[0:NS, 0:193], QE3[0:96, b_, :], Cin3[0:96, b_, :], b_ == 0, b_ == NS - 1, ["QE", ("Cin", h)], [("pb", qi)])
            qC = self.fa(193)
            self.cp(qC[0:NS, :], qb[0:NS, 0:193], [("pb", qi)], [("qC", h)])
            self.ts(vt[0:NS, :], v32s[0:NS, h * 193:(h + 1) * 193], s[0:NS, 16 + h:17 + h], None, ALU.mult, None, ["v32s", sk_], ["vt"])
            KE3 = KE.rearrange("p (b k) -> p b k", b=NS)
            for b_ in range(NS):
                self.ts(KE3[0:NS, b_, :], ktm[ti][0:NS, h * 96:(h + 1) * 96], self.ident[0:NS, b_:b_ + 1], DKS, ALU.mult, ALU.mult,
                        [("ktm", ti, h // 2)], ["KE"], eng="pool")
            for b_ in range(NS):
                oi, ob = self.bank()
                self.mm(ob[0:96, 0:193], KE3[0:NS, b_, :], vt[0:NS, :], True, True, ["KE", "vt"], [("pb", oi)])
                cn = Cn[b_ % 2]
                self.stt(cn[0:96, :], Cin3[0:96, b_, :], dbc[0:96, b_ * 4 + h:b_ * 4 + h + 1], ob[0:96, 0:193], ALU.mult, ALU.add,
                         [("pb", oi), ("Cin", h), "dbc"], [("Cn", b_ % 2)])
                self.dma(self.o_sC[l, b_, h], cn[0:96, 0:192], [("Cn", b_ % 2)], ["o_sC"])
                self.cp(nout[0:96, b_:b_ + 1], cn[0:96, 192:193], [("Cn", b_ % 2)], ["nout"], eng="act")
            self.dma(self.o_snT[l, h], nout[0:96, 0:NS], ["nout"], ["o_snT"])
            self.tt(s[0:NS, 28:29], s[0:NS, 20 + h:21 + h], qk[0:NS, h:h + 1], ALU.mult, [sk_, ("qk", h)], [("s1",)])
            self.ts(hsrc[0:NS, :], v32s[0:NS, h * 193:(h + 1) * 193], s[0:NS, 28:29], None, ALU.mult, None, ["v32s", ("s1",)], ["hsrc"])
            self.stt(hsrc[0:NS, :], qC[0:NS, :], s[0:NS, 24 + h:25 + h], hsrc[0:NS, :], ALU.mult, ALU.add, [("qC", h), sk_, "hsrc"], ["hsrc"])
            self.stt(s[0:NS, 29:30], hsrc[0:NS, 192:193], -1.0, hsrc[0:NS, 192:193], ALU.mult, ALU.max, ["hsrc"], [("s2",)])
            self.ts(s[0:NS, 29:30], s[0:NS, 29:30], 1.0, None, ALU.max, None, [("s2",)], [("s2",)])
            self.P.dve(lambda e, o=s[0:NS, 30:31], a=s[0:NS, 29:30]: e.reciprocal(out=o, in_=a), [("s2",)], [("s3",)])
            headnorm(hsrc[0:NS, 0:192], s[0:NS, 30:31], NS, h, ti, 0, ["hsrc", ("s3",)])
        to_fm(0, t0, NS)

    def conv(self, l, ps):
        self.phase()
        TT = self.TTc
        W = TT + 2
        cw = lambda j, c: self.convw_sb[:, (l * 3 + j) * 4 + c:(l * 3 + j) * 4 + c + 1]
        if ps == 0:
            self.memset(self.ctail[l][:], 0.0, ["ctail"])
        for cp in range(2):
            buf = {}
            for nm in ("cb", "cc", "ch"):
                slot, sk = self.next_w(nm)
                for mc in range(2):
                    t = self.fa(W); buf[(nm, mc)] = t
                    def ev(pi, p0, pn, bk, psap, t=t, nm=nm, mc=mc):
                        self.cp(t[:, 2 + p0:2 + p0 + pn], psap, [bk], [(nm, cp, mc)], eng="act" if pi == 0 else "dve")
                    self.fm(slot, sk, 256, mc * 128, 128, KC, self.xb, "xb", ev)
            for mc in range(2):
                c = cp * 2 + mc
                cb, cc, ch = buf[("cb", mc)], buf[("cc", mc)], buf[("ch", mc)]
                kk = ("u", cp, mc)
                self.tt(cc[:, 2:2 + TT], cc[:, 2:2 + TT], ch[:, 2:2 + TT], ALU.mult, [("cc", cp, mc), ("ch", cp, mc)], [kk])
                self.cp(cc[:, 0:2], self.ctail[l][:, c, :], ["ctail"], [kk])
                acc = ch
                self.ts(acc[:, 2:2 + TP], cc[:, 0:TP], cw(0, c), None, ALU.mult, None, [kk], [("acc", cp, mc)])
                self.stt(acc[:, 2:2 + TP], cc[:, 1:1 + TP], cw(1, c), acc[:, 2:2 + TP], ALU.mult, ALU.add, [kk, ("acc", cp, mc)], [("acc", cp, mc)])
                self.stt(acc[:, 2:2 + TP], cc[:, 2:2 + TP], cw(2, c), acc[:, 2:2 + TP], ALU.mult, ALU.add, [kk, ("acc", cp, mc)], [("acc", cp, mc)])
                self.tt(self.br[:, 6 + c, 0:TP], acc[:, 2:2 + TP], cb[:, 2:2 + TP], ALU.mult, [("acc", cp, mc), ("cb", cp, mc)], [("br", 6 + c)])
                if ps == 0:
                    st = self.fa(2 * NS)
                    self.dma(st[:, 0:2 * NS].rearrange("p (j b) -> p j b", j=2), self.scT[l, :, c * 128:(c + 1) * 128, :].rearrange("j p b -> p j b"), (), [("st", c)])
                    us = cc[:, 2 + TP:2 + TP + NS]
                    a2 = acc[:, 2 + TP:2 + TP + NS]
                    self.ts(a2, st[:, 0:NS], cw(0, c), None, ALU.mult, None, [("st", c)], [("acc", cp, mc)])
                    self.stt(a2, st[:, NS:2 * NS], cw(1, c), a2, ALU.mult, ALU.add, [("st", c), ("acc", cp, mc)], [("acc", cp, mc)])
                    self.stt(a2, us, cw(2, c), a2, ALU.mult, ALU.add, [kk, ("acc", cp, mc)], [("acc", cp, mc)])
                    self.tt(self.br[:, 6 + c, TP:TP + NS], a2, cb[:, 2 + TP:2 + TP + NS], ALU.mult, [("acc", cp, mc), ("cb", cp, mc)], [("br", 6 + c)])
                    self.dma(self.o_sconvT[l, 0, c * 128:(c + 1) * 128, :], st[:, NS:2 * NS], [("st", c)], ["o_sconv"])
                    self.dma(self.o_sconvT[l, 1, c * 128:(c + 1) * 128, :], us, [kk], ["o_sconv"])
                self.cp(self.ctail[l][:, c, :], cc[:, TP:TP + 2], [kk], ["ctail"], eng="pool")
                if ps == self.npass - 1:
                    self.dma(self.o_pconvT[l, c * 128:(c + 1) * 128, :], cc[:, TP:TP + 2], [kk], ["o_pconv"])

    def attn(self, l, ps):
        self.phase()
        TT = self.TTc
        base = ps * TP
        qst = self.ba(6 * TP).rearrange("p (c t) -> p c t", c=6)
        qs32 = self.fa(6 * NS).rearrange("p (c b) -> p c b", c=6)
        ks32 = self.fa(6 * NS).rearrange("p (c b) -> p c b", c=6)
        vs32 = self.fa(6 * NS).rearrange("p (c b) -> p c b", c=6)
        mark_f, mark_b = self.fo, self.bo
        cosb = self.fa(TT); sinb = self.fa(TT)
        self.dma(cosb[:, 0:TP], self.cosT[:, base:base + TP], (), ["cos"])
        self.dma(sinb[:, 0:TP], self.sinT[:, base:base + TP], (), ["sin"])
        if ps == 0:
            for b_ in range(NS):
                pass
            self.dma(cosb[:, TP:TP + 1], self.cosT[:, SEQ:SEQ + 1], (), ["cos"])
            self.dma(sinb[:, TP:TP + 1], self.sinT[:, SEQ:SEQ + 1], (), ["sin"])
        kst = self.ba(6 * TP).rearrange("p (c t) -> p c t", c=6)
        t1 = [self.fa(TT) for _ in range(2)]; t2 = [self.fa(TT) for _ in range(2)]; kro = [self.fa(TP) for _ in range(2)]
        it = 0
        for nm, st, s32 in (("aq", qst, qs32), ("ak", kst, ks32)):
            for t in range(3):
                slot, sk = self.next_w(nm)
                slot2, sk2 = self.next_w(nm + "s")
                g = t; d = GD[g]; ni = TP // d
                for mc in range(2):
                    c = t * 2 + mc
                    par = it % 2; it += 1
                    banks = {}
                    def ev(pi, p0, pn, bk, psap, par=par):
                        self.tt(t1[par][:, p0:p0 + pn], psap, cosb[:, p0:p0 + pn] if pi == 0 else cosb[:, TP:TP + 1].to_broadcast([128, NS]), ALU.mult,
                                [bk, "cos"], [("t1", par)])
                    def ev2(pi, p0, pn, bk, psap, par=par):
                        self.tt(t2[par][:, p0:p0 + pn], psap, sinb[:, p0:p0 + pn] if pi == 0 else sinb[:, TP:TP + 1].to_broadcast([128, NS]), ALU.mult,
                                [bk, "sin"], [("t2", par)])
                    self.fm(slot, sk, 256, mc * 128, 128, KC, self.xb, "xb", ev)
                    self.fm(slot2, sk2, 256, mc * 128, 128, KC, self.xb, "xb", ev2)
                    stv = st[:, c, :].rearrange("p (r i) -> p i r", r=d)
                    if nm == "ak":
                        self.tt(kro[par][:, 0:TP], t1[par][:, 0:TP], t2[par][:, 0:TP], ALU.add, [("t1", par), ("t2", par)], [("kro", par)], eng="pool")
                        self.cp(stv, kro[par][:, 0:TP].rearrange("p (i r) -> p i r", r=d), [("kro", par)], [("kst", c)], eng="act")
                        W_ = (128, 512, 2048)[g]
                        lo = SEQ - W_
                        a = max(base, lo)
                        if a < base + TP:
                            self.dma(self.o_pk[g][l, mc * 128:(mc + 1) * 128, a - lo:base + TP - lo], kro[par][:, a - base:TP], [("kro", par)], ["o_pk"])
                        self.dma(self.hK[l][g][mc * 128:(mc + 1) * 128, :].rearrange("p (r x) -> p r x", r=d)[:, :, ni * ps:ni * (ps + 1)],
                                 st[:, c, :].rearrange("p (r i) -> p r i", r=d), [("kst", c)], [("hK", g)])
                    else:
                        self.tt(stv, t1[par][:, 0:TP].rearrange("p (i r) -> p i r", r=d), t2[par][:, 0:TP].rearrange("p (i r) -> p i r", r=d), ALU.add,
                                [("t1", par), ("t2", par)], [("qst", c)], eng="pool")
                    if ps == 0:
                        self.tt(s32[:, c, :], t1[par][:, TP:TP + NS], t2[par][:, TP:TP + NS], ALU.add, [("t1", par), ("t2", par)], [(nm + "32", c)])
                        if nm == "ak":
                            self.dma(self.o_skT[l, c * 128:(c + 1) * 128, :], s32[:, c, :], [(nm + "32", c)], ["o_sk"])
        import os
        KATT = int(os.environ.get("KATT", "9"))
        if KATT <= 1: return
        vb = [self.ba(256) for _ in range(2)]; vf = [self.fa(256) for _ in range(2)]
        it = 0
        for t in range(3):
            slot, sk = self.next_w("av")
            g = t
            def ev(ti, t0, tn, bk, psap, g=g):
                nonlocal it
                if tn != 128:
                    return
                par = it % 2; it += 1
                self.cp(vb[par][:, :], psap, [bk], [("vb", par)], eng="act")
                self.dma(self.hV[l][g][base + t0:base + t0 + 128, :], vb[par][:, :], [("vb", par)], [("hV", g)])
                W_ = (128, 512, 2048)[g]; lo = SEQ - W_
                if base + t0 >= lo:
                    self.cp(vf[par][:, :], psap, [bk], [("vf", par)])
                    self.dma(self.o_pv[g][l, base + t0 - lo:base + t0 - lo + 128, :], vf[par][:, :], [("vf", par)], ["o_pv"])
            self.tmj(slot, sk, 256, 0, 256, ev)
            if ps == 0:
                for mc in range(2):
                    c = t * 2 + mc
                    bi, b = self.bank()
                    for kc in range(KC):
                        self.mm(b[:, 0:NS], slot[:, kc * 256 + mc * 128:kc * 256 + mc * 128 + 128], self.xb[:, kc, TP:TP + NS], kc == 0, kc == KC - 1,
                                [sk, "xb"], [("pb", bi)])
                    self.cp(vs32[:, c, :], b[:, 0:NS], [("pb", bi)], [("vs32", c)])
                    self.dma(self.o_svT[l, c * 128:(c + 1) * 128, :], vs32[:, c, :], [("vs32", c)], ["o_sv"])
        if KATT <= 2: return
        self.P.barrier()
        self.fo, self.bo = mark_f, mark_b
        accN = self.fa(4 * TT).rearrange("p (s t) -> p s t", s=4)
        accD = self.fa(4 * TT).rearrange("p (s t) -> p s t", s=4)
        kt = [self.ba(512).rearrange("p (c k) -> p c k", c=2) for _ in range(2)]
        vt = [self.ba(512).rearrange("p (n f) -> p n f", n=2) for _ in range(2)]
        PT = [self.ba(512) for _ in range(4)]
        ex = [self.fa(512) for _ in range(2)]
        it = 0; pti = 0
        for g in range(int(os.environ.get("KAG", "3"))):
            d = GD[g]; ni = TP // d; L_ = SEQ // d
            qn = 128 if g < 2 else 32
            for r in range(d):
                for qb in range(ni // qn):
                    par = it % 2; it += 1
                    i0 = ni * ps + qb * qn
                    if g < 2:
                        k0 = max(0, i0 - 128); nk = i0 + 128 - k0
                    else:
                        k0 = 0; nk = i0 + 32
                    nblk = (nk + 127) // 128
                    self.dma(kt[par][:, :, 0:nk], self.hK[l][g].rearrange("(c p) x -> p c x", c=2)[:, :, r * L_ + k0:r * L_ + k0 + nk], [("hK", g)], [("kt", par)])
                    for n in range(nblk):
                        kk0 = k0 + n * 128; kn = min(128, nk - n * 128)
                        rows = self.hV[l][g].rearrange("(i r) f -> r i f", r=d)[r, kk0:kk0 + kn, :]
                        self.dma(vt[par][0:kn, n, :], rows, [("hV", g)], [("vt", par)])
                    qcol = r * ni + qb * qn
                    pts = []
                    for n in range(nblk):
                        kn = min(128, nk - n * 128)
                        e = ex[pti % 2]; ek = ("ex", pti % 2)
                        for u in range(2):
                            si, sb_ = self.bank()
                            for c2 in range(2):
                                self.mm(sb_[0:kn, c2 * qn:(c2 + 1) * qn], kt[par][64 * u:64 * u + 64, c2, n * 128:n * 128 + kn],
                                        qst[64 * u:64 * u + 64, 2 * g + c2, qcol:qcol + qn], True, True, [("kt", par), ("qst", 2 * g + c2)], [("pb", si)])
                            self.act(e[0:kn, u * 2 * qn:(u + 1) * 2 * qn], sb_[0:kn, 0:2 * qn], AF.Exp, [("pb", si)], [ek], scale=0.125)
                        if g < 2:
                            diag = (n == nblk - 1)
                            mask = (self.UT4 if diag else self.LT4)[0:kn, :]
                        else:
                            mask = self.M2[0:kn, ps, :]
                        p_ = PT[pti % 4]; pk = ("PT", pti % 4); pti += 1
                        self.tt(p_[0:kn, 0:4 * qn], e[0:kn, 0:4 * qn], mask, ALU.mult, [ek], [pk])
                        pts.append((p_, pk, kn, n))
                    ni_, nbk = self.bank(); di_, dbk = self.bank()
                    for hh in range(4):
                        ph = (hh % 2) * 2 + hh // 2
                        for j, (p_, pk, kn, n) in enumerate(pts):
                            self.mm(nbk[0:64, hh * qn:(hh + 1) * qn], vt[par][0:kn, n, hh * 64:(hh + 1) * 64], p_[0:kn, ph * qn:(ph + 1) * qn],
                                    j == 0, j == len(pts) - 1, [("vt", par), pk], [("pb", ni_)])
                        for j, (p_, pk, kn, n) in enumerate(pts):
                            self.mm(dbk[0:64, hh * qn:(hh + 1) * qn], self.onesb[0:kn, 0:64], p_[0:kn, ph * qn:(ph + 1) * qn],
                                    j == 0, j == len(pts) - 1, ["c", pk], [("pb", di_)])
                    tsl = slice(qb * qn * d + r, qb * qn * d + r + (qn - 1) * d + 1, d)
                    nv = nbk[0:64, 0:4 * qn].rearrange("p (s q) -> p s q", s=4)
                    dv = dbk[0:64, 0:4 * qn].rearrange("p (s q) -> p s q", s=4)
                    if g == 0:
                        self.cp(accN[0:64, :, tsl], nv, [("pb", ni_)], ["accN"])
                        self.cp(accD[0:64, :, tsl], dv, [("pb", di_)], ["accD"], eng="act")
                    else:
                        self.tt(accN[0:64, :, tsl], accN[0:64, :, tsl], nv, ALU.add, [("pb", ni_), "accN"], ["accN"])
                        self.tt(accD[0:64, :, tsl], accD[0:64, :, tsl], dv, ALU.add, [("pb", di_), "accD"], ["accD"], eng="pool" if False else "dve")
        if KATT <= 3: return
        if ps == 0:
            ones64 = self.ones
            kc_ = [self.fa(512) for _ in range(2)]
            kT_ = [self.fa(256).rearrange("p (c k) -> p c k", c=2) for _ in range(2)]
            Pm = [self.fa(4) for _ in range(2)]
            it = 0
            sN = self.fa(4 * NS).rearrange("p (s b) -> p s b", s=4); sD = self.fa(4 * NS).rearrange("p (s b) -> p s b", s=4)
            pr = self.fa(NS)
            first = True
            for g in range(3):
                for hh in range(4):
                    c = 2 * g + hh // 2; u = hh % 2
                    self.tt(pr[64 * u:64 * u + 64, 0:NS], qs32[64 * u:64 * u + 64, c, :], ks32[64 * u:64 * u + 64, c, :], ALU.mult, [("aq32", c), ("ak32", c)], ["pr"])
                    bi, b = self.bank()
                    self.mm(b[0:64, 0:NS], ones64[64 * u:64 * u + 64, 0:64], pr[64 * u:64 * u + 64, 0:NS], True, True, ["pr", "c"], [("pb", bi)])
                    pe_ = self.fa(NS)
                    self.act(pe_[0:64, 0:NS], b[0:64, 0:NS], AF.Exp, [("pb", bi)], [("pe", g, hh)], scale=0.125)
                    vi, vbk = self.bank()
                    self.mm(vbk[0:64, 0:NS], self.ident[64 * u:64 * u + 64, 64 * u:64 * u + 64], vs32[64 * u:64 * u + 64, c, :], True, True,
                            [("vs32", c), "c"], [("pb", vi)])
                    if g == 0:
                        self.tt(sN[0:64, hh, :], pe_[0:64, 0:NS], vbk[0:64, 0:NS], ALU.mult, [("pe", g, hh), ("pb", vi)], [("sN", hh)])
                        self.cp(sD[0:64, hh, :], pe_[0:64, 0:NS], [("pe", g, hh)], [("sD", hh)])
                    else:
                        tmpv = self.fa(NS)
                        self.tt(tmpv[0:64, 0:NS], pe_[0:64, 0:NS], vbk[0:64, 0:NS], ALU.mult, [("pe", g, hh), ("pb", vi)], [("tmpv", g, hh)])
                        self.tt(sN[0:64, hh, :], sN[0:64, hh, :], tmpv[0:64, 0:NS], ALU.add, [("tmpv", g, hh), ("sN", hh)], [("sN", hh)])
                        self.tt(sD[0:64, hh, :], sD[0:64, hh, :], pe_[0:64, 0:NS], ALU.add, [("pe", g, hh), ("sD", hh)], [("sD", hh)])
            for g in range(3):
                d = GD[g]
                for b_ in range(NS):
                    par = it % 2; it += 1
                    self.dma(kc_[par][:, :], self.ck[g][l, b_].rearrange("(j r) f -> r j f", r=d)[0, :, :], (), [("kc", par)])
                    for c2 in range(2):
                        ti_, tb = self.bank()
                        self.tr(tb[:, 0:128], kc_[par][:, c2 * 128:(c2 + 1) * 128], self.ident[:], [("kc", par)], [("pb", ti_)])
                        self.cp(kT_[par][:, c2, :], tb[:, 0:128], [("pb", ti_)], [("kT_", par, c2)], eng="act")
                    for u in range(2):
                        si, sb_ = self.bank()
                        for c2 in range(2):
                            self.mm(sb_[:, c2:c2 + 1], kT_[par][64 * u:64 * u + 64, c2, :], qs32[64 * u:64 * u + 64, 2 * g + c2, b_:b_ + 1], True, True,
                                    [("kT_", par, c2), ("aq32", 2 * g + c2)], [("pb", si)])
                        self.act(Pm[par][:, 2 * u:2 * u + 2], sb_[:, 0:2], AF.Exp, [("pb", si)], [("Pm", par)], scale=0.125)
                    ni_, nbk = self.bank()
                    for hh in range(4):
                        ph = (hh % 2) * 2 + hh // 2
                        self.mm(nbk[0:64, hh:hh + 1], kc_[par][:, 256 + hh * 64:256 + (hh + 1) * 64], Pm[par][:, ph:ph + 1], True, True,
                                [("kc", par), ("Pm", par)], [("pb", ni_)])
                        self.mm(nbk[0:64, 4 + hh:5 + hh], self.ones[:, 0:64], Pm[par][:, ph:ph + 1], True, True, ["c", ("Pm", par)], [("pb", ni_)])
                    self.tt(sN[0:64, :, b_], sN[0:64, :, b_], nbk[0:64, 0:4], ALU.add, [("pb", ni_), ("sN", 0), ("sN", 1), ("sN", 2), ("sN", 3)],
                            [("sN", 0), ("sN", 1), ("sN", 2), ("sN", 3)])
                    self.tt(sD[0:64, :, b_], sD[0:64, :, b_], nbk[0:64, 4:8], ALU.add, [("pb", ni_), ("sD", 0), ("sD", 1), ("sD", 2), ("sD", 3)],
                            [("sD", 0), ("sD", 1), ("sD", 2), ("sD", 3)])
            for hh in range(4):
                self.cp(accN[0:64, hh, TP:TP + NS], sN[0:64, hh, :], [("sN", hh)], ["accN"])
                self.cp(accD[0:64, hh, TP:TP + NS], sD[0:64, hh, :], [("sD", hh)], ["accD"])
        for hh in range(4):
            self.P.dve(lambda e, o=accD[0:64, hh, 0:TT], a=accD[0:64, hh, 0:TT]: e.reciprocal(out=o, in_=a), ["accD"], [("rD", hh)])
            self.tt(self.br[0:64, 10 + hh, 0:TT], accN[0:64, hh, 0:TT], accD[0:64, hh, 0:TT], ALU.mult, ["accN", ("rD", hh)], [("br", 10 + hh)])

    def merge(self, l, ps):
        self.phase()
        TT = self.TTc
        if self.dbg is not None and l == 0 and ps == 0:
            dbt = self.fa(14 * NS).rearrange("p (c b) -> p c b", c=14)
            self.cp(dbt, self.br[:, :, TP:TP + NS], [("br", c) for c in range(14)], ["dbt"])
            self.dma(self.dbg.rearrange("(c p) b -> p c b", p=128), dbt, ["dbt"], ["o_dbg"])
        gt = [self.ba(TT) for _ in range(3)]
        acc = [self.fa(TT) for _ in range(2)]
        KB = ((0, 6), (6, 4), (10, 4))
        for m in range(16):
            slotA, skA = self.next_w("gA")
            for b_ in range(2):
                def ev(pi, p0, pn, bk, psap, b_=b_):
                    self.act(gt[b_][:, p0:p0 + pn], psap, AF.Sigmoid, [bk], [("gt", b_)])
                self.fm(slotA, skA, 256, b_ * 128, 128, KC, self.xb, "xb", ev)
            slotB, skB = self.next_w("gB")
            def ev(pi, p0, pn, bk, psap):
                self.act(gt[2][:, p0:p0 + pn], psap, AF.Sigmoid, [bk], [("gt", 2)])
            self.fm(slotB, skB, 128, 0, 128, KC, self.xb, "xb", ev)
            slotU, skU = self.next_w("up")
            a = acc[m % 2]; ak = ("macc", m % 2)
            for bi_, (k0, nk) in enumerate(KB):
                for (p0, pn) in self.pieces:
                    bi, b = self.bank()
                    for j in range(nk):
                        self.mm(b[:, 0:pn], slotU[:, (k0 + j) * 128:(k0 + j + 1) * 128], self.br[:, k0 + j, p0:p0 + pn], j == 0, j == nk - 1,
                                [skU, ("br", k0 + j)], [("pb", bi)])
                    if bi_ == 0:
                        self.tt(a[:, p0:p0 + pn], b[:, 0:pn], gt[0][:, p0:p0 + pn], ALU.mult, [("pb", bi), ("gt", 0)], [ak])
                    else:
                        t_ = gt[bi_]
                        self.tt(t_[:, p0:p0 + pn], b[:, 0:pn], t_[:, p0:p0 + pn], ALU.mult, [("pb", bi), ("gt", bi_)], [("gt", bi_)])
                        if bi_ == 1:
                            self.tt(a[:, p0:p0 + pn], a[:, p0:p0 + pn], t_[:, p0:p0 + pn], ALU.add, [ak, ("gt", bi_)], [ak], eng="pool")
                        else:
                            self.tt(self.mg[:, m, p0:p0 + pn], a[:, p0:p0 + pn], t_[:, p0:p0 + pn], ALU.add, [ak, ("gt", bi_)], [("mg", m)], eng="pool")
        for t in range(8):
            slot, sk = self.next_w("wo")
            for mc in range(2):
                m = t * 2 + mc
                for (p0, pn) in self.pieces:
                    bi, b = self.bank()
                    for kc in range(KC):
                        self.mm(b[:, 0:pn], slot[:, kc * 256 + mc * 128:kc * 256 + mc * 128 + 128], self.mg[:, kc, p0:p0 + pn], kc == 0, kc == KC - 1,
                                [sk, ("mg", kc)], [("pb", bi)])
                    self.stt(self.xf[:, m, p0:p0 + pn], self.xf[:, m, p0:p0 + pn], ALPHA, b[:, 0:pn], ALU.mult, ALU.add, [("pb", bi), ("xf", m)], [("xf", m)])

    def build(self):
        self.declare(); self.alloc()
        P = self.P
        for dst, src in ((self.UT, self.c_UT), (self.ident, self.c_id), (self.UT4, self.c_UT4), (self.LT4, self.c_LT4),
                         (self.lnp_sb, self.lnp), (self.convw_sb, self.convw), (self.gain_sb, self.gain), (self.bif_sb, self.bif)):
            self.dma(dst[:], src, (), ["c"])
        self.dma(self.M2[:], self.c_M2.rearrange("n p x -> p n x"), (), ["c"])
        self.memset(self.onesD[:], 1.0 / D, ["c"]); self.memset(self.ones[:], 1.0, ["c"]); self.memset(self.onesb[:], 1.0, ["c"])
        for ps in range(self.npass):
            self.phase()
            base = ps * TP
            self.TTc = TP + NS if ps == 0 else TP
            self.pieces = [(0, TP)] + ([(TP, NS)] if ps == 0 else [])
            self.ttiles = [(i * 128, 128) for i in range(4)] + ([(TP, NS)] if ps == 0 else [])
            TT = self.TTc
            self.dma(self.xf[:, :, 0:TP], self.xT.rearrange("(c p) t -> p c t", p=128)[:, :, base:base + TP], (), ["xf"])
            if ps == 0:
                self.dma(self.xf[:, :, TP:TP + NS], self.xsT.rearrange("(c p) t -> p c t", p=128), (), ["xf"])
            for kc in range(KC):
                self.cp(self.xb[:, kc, 0:TT], self.xf[:, kc, 0:TT], ["xf"], [("xb", kc)], eng="act" if kc % 2 else "dve")
            import os
            STOP = int(os.environ.get("KSTOP", "9"))
            for l in range(self.nl):
                self.ffn(l, 0)
                if STOP <= 0: break
                self.layernorm(l, 0)
                if STOP <= 1: break
                self.mlstm(l, ps)
                if STOP <= 2: break
                self.conv(l, ps)
                if STOP <= 3: break
                self.P.dve(lambda e: e.memset(self.br[64:128, 10:14, :], 0.0), (), [("br", 10), ("br", 11), ("br", 12), ("br", 13)])
                self.attn(l, ps)
                if STOP <= 4: break
                self.merge(l, ps)
                self.layernorm(l, 1)
                self.ffn(l, 1)
                self.layernorm(l, 2)
            self.phase()
            self.dma(self.o_yT.rearrange("(c p) t -> p c t", p=128)[:, :, base:base + TP], self.xf[:, :, 0:TP], ["xf"], ["o_y"])
            if ps == 0:
                self.dma(self.o_ysT.rearrange("(c p) t -> p c t", p=128), self.xf[:, :, TP:TP + NS], ["xf"], ["o_ys"])
        self.P.barrier()
        P.emit()


_CACHE = {}


def _prep_core(c, inp, wts, consts):
    b = c % 4
    s0, s1 = c * NS, (c + 1) * NS
    m = dict(consts)
    m["wts"] = wts
    m["xT"] = np.ascontiguousarray(inp["x_prompt"][b].T)
    m["xsT"] = np.ascontiguousarray(inp["x_sample"][s0:s1, 0].T)
    g = inp["ln_g"]; bb = inp["ln_b"]
    lnp = np.stack([g, bb]).reshape(2, NL, 3, KC, 128)
    m["lnp"] = np.ascontiguousarray(lnp.transpose(4, 0, 1, 2, 3).reshape(128, -1))
    cw = inp["conv_w"].reshape(NL, 3, 4, 128)
    m["convw"] = np.ascontiguousarray(cw.transpose(3, 0, 1, 2).reshape(128, -1))
    m["gain"] = np.ascontiguousarray(np.broadcast_to(inp["mlstm_norm_g"].reshape(1, -1), (128, NL * 768)))
    m["bif"] = np.ascontiguousarray(np.broadcast_to(inp["b_gate_if"].reshape(1, -1), (128, NL * 8)))
    m["sC"] = np.ascontiguousarray(inp["state_mlstm_C"][:, s0:s1])
    m["snT"] = np.ascontiguousarray(inp["state_mlstm_n"][:, s0:s1].transpose(0, 2, 3, 1))
    m["sm"] = np.ascontiguousarray(inp["state_mlstm_m"][:, s0:s1])
    m["scT"] = np.ascontiguousarray(inp["state_conv"][:, s0:s1].transpose(0, 2, 3, 1))
    for gi, nm in enumerate(("cache_attn_kv_w128", "cache_attn_kv_w512", "cache_attn_kv_w2048")):
        a = inp[nm][:, s0:s1]
        m[f"ck{gi}"] = np.ascontiguousarray(a.reshape(NL, NS, a.shape[2], 512))
    return m


def build_program(npass=NPASS, nl=NL):
    nc = bass.Bass("TRN2", target_bir_lowering=False)
    k = K(nc, npass, nl)
    k.build()
    return nc, k


def assemble(res, ncores=8):
    f = np.float32
    y_prompt = np.zeros((4, SEQ, D), f); y_sample = np.zeros((128, 1, D), f)
    p_C = np.zeros((NL, 4, 4, 96, 192), f); p_n = np.zeros((NL, 4, 4, 96), f); p_m = np.zeros((NL, 4, 4), f)
    p_conv = np.zeros((NL, 4, 2, 512), f)
    p_kv = [np.zeros((NL, 4, w, 2, 4, 64), f) for w in (128, 512, 2048)]
    s_C = np.zeros((NL, 128, 4, 96, 192), f); s_n = np.zeros((NL, 128, 4, 96), f); s_m = np.zeros((NL, 128, 4), f)
    s_conv = np.zeros((NL, 128, 2, 512), f)
    s_kv = [np.zeros((NL, 128, 1, 2, 4, 64), f) for _ in range(3)]
    for c in range(ncores):
        r = res[c]
        s0, s1 = c * NS, (c + 1) * NS
        if c < 4:
            b = c
            y_prompt[b] = r["o_yT"].T
            p_C[:, b] = r["o_pC"]; p_n[:, b] = r["o_pn"]; p_m[:, b] = r["o_pm"]
            p_conv[:, b] = r["o_pconvT"].transpose(0, 2, 1)
            for g in range(3):
                w = (128, 512, 2048)[g]
                p_kv[g][:, b, :, 0] = r[f"o_pk{g}"].transpose(0, 2, 1).reshape(NL, w, 4, 64)
                p_kv[g][:, b, :, 1] = r[f"o_pv{g}"].reshape(NL, w, 4, 64)
        y_sample[s0:s1, 0] = r["o_ysT"].T
        s_C[:, s0:s1] = r["o_sC"]; s_n[:, s0:s1] = r["o_snT"].transpose(0, 3, 1, 2); s_m[:, s0:s1] = r["o_sm"]
        s_conv[:, s0:s1] = r["o_sconvT"].transpose(0, 3, 1, 2)
        kT = r["o_skT"]; vT = r["o_svT"]
        for g in range(3):
            s_kv[g][:, s0:s1, 0, 0] = kT[:, g * 256:(g + 1) * 256].transpose(0, 2, 1).reshape(NL, NS, 4, 64)
            s_kv[g][:, s0:s1, 0, 1] = vT[:, g * 256:(g + 1) * 256].transpose(0, 2, 1).reshape(NL, NS, 4, 64)
    return (y_prompt, y_sample, p_C, p_n, p_m, p_conv, p_kv[0], p_kv[1], p_kv[2],
            s_C, s_n, s_m, s_conv, s_kv[0], s_kv[1], s_kv[2])


def kernel(**inp):
    inp = {k: np.asarray(v) for k, v in inp.items()}
    wts = np.stack([host_weights(l, inp["w_in"], inp["w_up_mlstm"], inp["w_up_conv"], inp["w_up_attn"], inp["w_o"],
                                 inp["w_ffn_in"], inp["w_ffn_out"]) for l in range(NL)])
    consts = host_consts()
    nc, _ = build_program()
    in_maps = [_prep_core(c, inp, wts, consts) for c in range(8)]
    res = run_bass_kernel_spmd(nc, in_maps, core_ids=list(range(8)))
    return assemble(res.results, 8)
```

```python
import contextlib
import numpy as np
import concourse.bass as bass
import concourse.mybir as mybir

F32 = mybir.dt.float32
BF16 = mybir.dt.bfloat16
AF = mybir.ActivationFunctionType
ALU = mybir.AluOpType
AX = mybir.AxisListType

ENGS = ("pe", "act", "dve", "pool", "sp")
NDMA_SEMS = 8


class Op:
    __slots__ = ("eng", "fn", "deps", "is_dma", "inc", "sem", "val", "qwait")

    def __init__(self, eng, fn, is_dma):
        self.eng = eng
        self.fn = fn
        self.is_dma = is_dma
        self.deps = []
        self.inc = False
        self.sem = None
        self.val = 0
        self.qwait = None


class Prog:
    def __init__(self, nc):
        self.nc = nc
        self.ops = []
        self.last_w = {}
        self.readers = {}
        self.stack = contextlib.ExitStack()
        self.n_sb = 0

    def sb(self, shape, dt, name=None):
        self.n_sb += 1
        return self.stack.enter_context(self.nc.sbuf_tensor(name or f"sb{self.n_sb}", list(shape), dt))

    def ps(self, shape, dt=F32, name=None):
        self.n_sb += 1
        return self.stack.enter_context(self.nc.psum_tensor(name or f"ps{self.n_sb}", list(shape), dt))

    def _add(self, eng, fn, reads, writes, is_dma):
        op = Op(eng, fn, is_dma)
        idx = len(self.ops)
        writes = list(writes) + [k for k in reads if isinstance(k, tuple) and k and k[0] == "pb"]
        deps = set()
        for k in reads:
            w = self.last_w.get(k)
            if w is not None:
                deps.add(w)
        for k in writes:
            w = self.last_w.get(k)
            if w is not None:
                deps.add(w)
            for r in self.readers.get(k, ()):
                deps.add(r)
        for k in reads:
            self.readers.setdefault(k, []).append(idx)
        for k in writes:
            self.last_w[k] = idx
            self.readers[k] = []
        deps.discard(idx)
        op.deps = sorted(deps)
        self.ops.append(op)
        return idx

    def pe(self, fn, reads=(), writes=()):
        return self._add("pe", fn, reads, writes, False)

    def act(self, fn, reads=(), writes=()):
        return self._add("act", fn, reads, writes, False)

    def dve(self, fn, reads=(), writes=()):
        return self._add("dve", fn, reads, writes, False)

    def pool(self, fn, reads=(), writes=()):
        return self._add("pool", fn, reads, writes, False)

    def dma(self, fn, reads=(), writes=(), q="sp"):
        return self._add(q, fn, reads, writes, True)

    def barrier(self):
        last = {}
        dmas = {"sp": [], "pool": []}
        for i, op in enumerate(self.ops):
            if op.fn is None:
                continue
            if op.is_dma:
                dmas[op.eng].append(i)
            else:
                last[op.eng] = i
        deps = sorted(set(list(last.values()) + dmas["sp"][-NDMA_SEMS:] + dmas["pool"][-NDMA_SEMS:]))
        for e in ENGS:
            op = Op(e, None, False)
            op.deps = list(deps)
            self.ops.append(op)
        self.last_w = {}
        self.readers = {}

    def emit(self):
        nc = self.nc
        ops = self.ops
        for op in ops:
            for d in op.deps:
                p = ops[d]
                if p.eng == "pe" and op.eng == "pe" and not p.is_dma and op.fn is not None:
                    continue
                p.inc = True
        sems = {}
        for e in ("pe", "act", "dve", "pool"):
            sems[e] = self.stack.enter_context(nc.semaphore(f"s_{e}"))
        dsems = {}
        for q in ("sp", "pool"):
            dsems[q] = [self.stack.enter_context(nc.semaphore(f"d_{q}{i}")) for i in range(NDMA_SEMS)]
        cnt = {e: 0 for e in sems}
        dcnt = {"sp": 0, "pool": 0}
        for op in ops:
            if op.is_dma:
                i = dcnt[op.eng]
                dcnt[op.eng] += 1
                op.sem = dsems[op.eng][i % NDMA_SEMS]
                op.val = 16 * (i // NDMA_SEMS + 1)
                op.qwait = 16 * (i // NDMA_SEMS)
                op.inc = True
            elif op.inc:
                cnt[op.eng] += 1
                op.sem = sems[op.eng]
                op.val = cnt[op.eng]
        self.final_d = {q: [(dsems[q][i], 16 * ((dcnt[q] - 1 - i) // NDMA_SEMS + 1)) for i in range(min(NDMA_SEMS, dcnt[q]))]
                        for q in dsems}
        by_eng = {e: [] for e in ENGS}
        for op in ops:
            by_eng[op.eng].append(op)
        self.n_wait = 0

        def run(eng_name, eng):
            waited = {}
            for op in by_eng[eng_name]:
                need = {}
                for d in op.deps:
                    p = ops[d]
                    if not p.inc:
                        continue
                    if p.eng == "pe" and eng_name == "pe" and not p.is_dma and op.fn is not None:
                        continue
                    key = id(p.sem)
                    if waited.get(key, 0) >= p.val:
                        continue
                    if key not in need or need[key][1] < p.val:
                        need[key] = (p.sem, p.val)
                if op.is_dma and op.qwait > 0:
                    key = id(op.sem)
                    if waited.get(key, 0) < op.qwait:
                        if key not in need or need[key][1] < op.qwait:
                            need[key] = (op.sem, op.qwait)
                for key, (s, v) in need.items():
                    eng.wait_ge(s, v)
                    waited[key] = v
                    self.n_wait += 1
                if op.fn is None:
                    continue
                ins = op.fn(eng)
                if op.inc:
                    ins.then_inc(op.sem, 16 if op.is_dma else 1)
            if eng_name == "sp":
                for q in ("sp", "pool"):
                    for s, v in self.final_d[q]:
                        eng.wait_ge(s, v)

        with nc.Block() as block:
            @block.tensor
            def _(e):
                run("pe", e)

            @block.scalar
            def _(e):
                run("act", e)

            @block.vector
            def _(e):
                run("dve", e)

            @block.gpsimd
            def _(e):
                run("pool", e)

            @block.sync
            def _(e):
                run("sp", e)
        self.stack.close()

from concourse.bass_utils import run_bass_kernel_spmd

D = 2048; KC = 16; NL = 2; TP = 512; NPASS = 4; NS = 16; DFF = 5632; SEQ = 2048
ALPHA = float((2 * NL) ** 0.25)
EPS = 1e-5
OFF = dict(mq=0, mk=384, mv=768, mi=1536, mf=1540, mo=1544, cb=2312, cc=2824, ch=3336,
           aq=3848, ak=4616, av=5384, gt=6152)
GD = (1, 4, 16)
SLOT_ELEMS = 4096
NSLOT = 5


def layer_plan():
    plan = []

    def add(name, src, k0, kc, cols):
        cols = np.asarray(cols)
        assert kc * len(cols) <= SLOT_ELEMS
        plan.append((name, src, k0, kc, cols))

    def ffn(f):
        for qd in range(4):
            for j in range(11 * qd, 11 * qd + 11):
                add(f"up{f}", f"ffi{f}", 0, 16, np.r_[j * 128:(j + 1) * 128, DFF + j * 128:DFF + (j + 1) * 128])
            for mg in range(8):
                add(f"dn{f}", f"ffo{f}", 11 * qd * 128, 11, np.r_[mg * 256:(mg + 1) * 256])

    ffn(0)
    for hp in range(2):
        add("mq", "win", 0, 16, OFF["mq"] + np.r_[hp * 192:(hp + 1) * 192])
    for hp in range(2):
        add("mk", "win", 0, 16, OFF["mk"] + np.r_[hp * 192:(hp + 1) * 192])
    for h in range(4):
        add("mv", "win", 0, 16, OFF["mv"] + np.r_[h * 192:(h + 1) * 192])
    add("mg", "win", 0, 16, OFF["mi"] + np.r_[0:8])
    for h in range(4):
        add("mo", "win", 0, 16, OFF["mo"] + np.r_[h * 192:(h + 1) * 192])
    for cp in range(2):
        for nm in ("cb", "cc", "ch"):
            add(nm, "win", 0, 16, OFF[nm] + np.r_[cp * 256:(cp + 1) * 256])
    for nm in ("aq", "ak"):
        for t in range(3):
            c = np.r_[t * 256:(t + 1) * 256]
            add(nm, "win", 0, 16, OFF[nm] + c)
            add(nm + "s", "win", 0, 16, OFF[nm] + (c // 64) * 64 + (c % 64 + 32) % 64)
    for t in range(3):
        add("av", "win", 0, 16, OFF["av"] + np.r_[t * 256:(t + 1) * 256])
    for m in range(16):
        g0 = OFF["gt"] + m * 128
        add("gA", "win", 0, 16, np.r_[g0:g0 + 128, g0 + 2048:g0 + 2048 + 128])
        add("gB", "win", 0, 16, np.r_[g0 + 4096:g0 + 4096 + 128])
        add("up", "wup", 0, 14, np.r_[m * 128:(m + 1) * 128])
    for t in range(8):
        add("wo", "wo", 0, 16, np.r_[t * 256:(t + 1) * 256])
    ffn(1)
    offs = []
    o = 0
    for (_, _, _, kc, cols) in plan:
        offs.append(o)
        o += 128 * kc * len(cols)
    return plan, offs, o


PLAN, POFF, PTOT = layer_plan()


def host_weights(l, w_in, w_up_mlstm, w_up_conv, w_up_attn, w_o, w_ffn_in, w_ffn_out):
    wa = np.zeros((512, D), np.float32)
    for s in range(4):
        wa[s * 128:s * 128 + 64] = w_up_attn[l][s * 64:(s + 1) * 64]
    src = dict(ffi0=w_ffn_in[l, 0], ffi1=w_ffn_in[l, 1], ffo0=w_ffn_out[l, 0], ffo1=w_ffn_out[l, 1],
               win=w_in[l], wo=w_o[l], wup=np.concatenate([w_up_mlstm[l], w_up_conv[l], wa], axis=0))
    out = np.empty((PTOT,), np.float32)
    for (name, s, k0, kc, cols), o in zip(PLAN, POFF):
        W = src[s]
        n = len(cols)
        if n > 1 and np.all(np.diff(cols) == 1):
            blk = W[k0:k0 + kc * 128, cols[0]:cols[0] + n]
        else:
            blk = W[k0:k0 + kc * 128][:, cols]
        out[o:o + 128 * kc * n] = blk.reshape(kc, 128, n).transpose(1, 0, 2).reshape(-1)
    return out


def host_consts():
    c = {}
    half = 32
    inv = (10000.0 ** (-(2.0 * np.arange(half, dtype=np.float32)) / 64)).astype(np.float32)
    pos = np.arange(SEQ + 1, dtype=np.float32)
    ang = (pos[None, :] * np.tile(inv, 4)[:, None]).astype(np.float32)
    sgn = np.where((np.arange(128) % 64) < 32, -1.0, 1.0).astype(np.float32)[:, None]
    c["cosT"] = np.cos(ang).astype(np.float32)
    c["sinT"] = (np.sin(ang) * sgn).astype(np.float32)
    k = np.arange(128)
    UT = (k[:, None] <= k[None, :]).astype(np.float32)
    LT = (k[:, None] >= k[None, :]).astype(np.float32)
    c["UT"] = UT
    c["ident"] = np.eye(128, dtype=np.float32)
    c["UT4"] = np.tile(UT, (1, 4)).astype(np.float32)
    c["LT4"] = np.tile(LT, (1, 4)).astype(np.float32)
    m2 = np.zeros((NPASS, 128, 128), np.float32)
    for p in range(NPASS):
        blk = (k[:, None] <= (np.arange(32)[None, :] + 32 * p)).astype(np.float32)
        m2[p] = np.tile(blk, (1, 4))
    c["M2"] = m2
    return c


class K:
    def __init__(self, nc, npass=NPASS, nl=NL):
        self.nc = nc
        self.P = Prog(nc)
        self.npass = npass
        self.nl = nl
        self.bank_i = 0
        self.wnext = 0
        self.wissued = 0

    def declare(self):
        nc = self.nc
        di = lambda n, s: nc.dram_tensor(n, list(s), F32, kind="ExternalInput").ap()
        do = lambda n, s: nc.dram_tensor(n, list(s), F32, kind="ExternalOutput").ap()
        self.wts = di("wts", [self.nl, PTOT])
        self.xT = di("xT", [D, SEQ])
        self.xsT = di("xsT", [D, NS])
        self.cosT = di("cosT", [128, SEQ + 1]); self.sinT = di("sinT", [128, SEQ + 1])
        self.c_UT = di("UT", [128, 128]); self.c_id = di("ident", [128, 128])
        self.c_UT4 = di("UT4", [128, 512]); self.c_LT4 = di("LT4", [128, 512]); self.c_M2 = di("M2", [NPASS, 128, 128])
        self.lnp = di("lnp", [128, 2 * NL * 3 * KC])
        self.convw = di("convw", [128, NL * 3 * 4])
        self.gain = di("gain", [128, NL * 768])
        self.bif = di("bif", [128, NL * 8])
        self.sC = di("sC", [NL, NS, 4, 96, 192])
        self.snT = di("snT", [NL, 4, 96, NS])
        self.sm = di("sm", [NL, NS, 4])
        self.scT = di("scT", [NL, 2, 512, NS])
        self.ck = [di(f"ck{g}", [NL, NS, (128, 512, 2048)[g], 512]) for g in range(3)]
        self.o_yT = do("o_yT", [D, SEQ]); self.o_ysT = do("o_ysT", [D, NS])
        self.o_pC = do("o_pC", [NL, 4, 96, 192]); self.o_pn = do("o_pn", [NL, 4, 96]); self.o_pm = do("o_pm", [NL, 4])
        self.o_pconvT = do("o_pconvT", [NL, 512, 2])
        self.o_pk = [do(f"o_pk{g}", [NL, 256, (128, 512, 2048)[g]]) for g in range(3)]
        self.o_pv = [do(f"o_pv{g}", [NL, (128, 512, 2048)[g], 256]) for g in range(3)]
        self.o_sC = do("o_sC", [NL, NS, 4, 96, 192]); self.o_snT = do("o_snT", [NL, 4, 96, NS]); self.o_sm = do("o_sm", [NL, NS, 4])
        self.o_sconvT = do("o_sconvT", [NL, 2, 512, NS])
        self.o_skT = do("o_skT", [NL, 768, NS]); self.o_svT = do("o_svT", [NL, 768, NS])
        import os
        self.dbg = do("o_dbg", [14 * 128, NS]) if "KDBG" in os.environ else None
        self.hK = [[nc.dram_tensor(f"hK{l}_{g}", [256, SEQ], BF16, kind="Internal").ap() for g in range(3)] for l in range(NL)]
        self.hV = [[nc.dram_tensor(f"hV{l}_{g}", [SEQ, 256], BF16, kind="Internal").ap() for g in range(3)] for l in range(NL)]

    def alloc(self):
        P = self.P
        TT = TP + NS
        self.TT = TT
        self.xf = P.sb([128, KC, TT], F32, "xf")
        self.xb = P.sb([128, KC, TT], BF16, "xb")
        self.br = P.sb([128, 14, TT], BF16, "br")
        self.mg = P.sb([128, KC, TT], BF16, "mg")
        self.ws = [P.sb([128, SLOT_ELEMS], BF16, f"ws{i}") for i in range(NSLOT)]
        self.pb = [P.ps([128, 512], F32, f"pb{i}") for i in range(8)]
        self.AF = P.sb([128, 8912], F32, "arenaF")
        self.AB = P.sb([128, 13264], BF16, "arenaB")
        self.UT = P.sb([128, 128], F32, "cUT"); self.ident = P.sb([128, 128], F32, "cid")
        self.UT4 = P.sb([128, 512], F32, "cUT4"); self.LT4 = P.sb([128, 512], F32, "cLT4"); self.M2 = P.sb([128, NPASS, 128], F32, "cM2")
        self.onesD = P.sb([128, 128], F32, "onesD"); self.ones = P.sb([128, 128], F32, "ones1"); self.onesb = P.sb([128, 64], BF16, "onesb")
        self.lnp_sb = P.sb([128, 2 * NL * 3 * KC], F32, "lnp_s")
        self.convw_sb = P.sb([128, NL * 12], F32, "convw_s")
        self.gain_sb = P.sb([128, NL * 768], F32, "gain_s")
        self.bif_sb = P.sb([128, NL * 8], F32, "bif_s")
        self.Cst = [[P.sb([96, 194], F32, f"Cst{l}_{h}") for h in range(4)] for l in range(NL)]
        self.Cbf = [[P.sb([96, 194], BF16, f"Cbf{l}_{h}") for h in range(4)] for l in range(NL)]
        self.mst = [P.sb([4, 1], F32, f"mst{l}") for l in range(NL)]
        self.ctail = [P.sb([128, 4, 2], F32, f"ctail{l}") for l in range(NL)]
        self.fo = 0; self.bo = 0

    def phase(self):
        self.P.barrier()
        self.fo = 0; self.bo = 0

    def fa(self, n):
        a = self.AF[:, self.fo:self.fo + n]; self.fo += n; self.fmax = max(getattr(self, 'fmax', 0), self.fo)
        assert self.fo <= 8912, self.fo
        return a

    def ba(self, n):
        a = self.AB[:, self.bo:self.bo + n]; self.bo += n; self.bmax = max(getattr(self, 'bmax', 0), self.bo)
        assert self.bo <= 13264, self.bo
        return a

    def bank(self):
        i = self.bank_i % 8
        self.bank_i += 1
        return i, self.pb[i]

    def mm(self, out, lhsT, rhs, start, stop, r, w):
        self.P.pe(lambda e: e.matmul(out, lhsT=lhsT, rhs=rhs, start=start, stop=stop), r, w)

    def tr(self, out, in_, ident, r, w):
        self.P.pe(lambda e: e.transpose(out, in_, ident), r, w)

    def act(self, out, in_, func, r, w, bias=None, scale=None):
        kw = {}
        if bias is not None: kw["bias"] = bias
        if scale is not None: kw["scale"] = scale
        self.P.act(lambda e: e.activation(out=out, in_=in_, func=func, **kw), r, w)

    def tt(self, out, in0, in1, op, r, w, eng="dve"):
        getattr(self.P, eng)(lambda e: e.tensor_tensor(out=out, in0=in0, in1=in1, op=op), r, w)

    def ts(self, out, in0, s1, s2, op0, op1, r, w, eng="dve"):
        if s2 is None:
            getattr(self.P, eng)(lambda e: e.tensor_scalar(out=out, in0=in0, scalar1=s1, scalar2=None, op0=op0), r, w)
        else:
            getattr(self.P, eng)(lambda e: e.tensor_scalar(out=out, in0=in0, scalar1=s1, scalar2=s2, op0=op0, op1=op1), r, w)

    def stt(self, out, in0, scalar, in1, op0, op1, r, w, eng="dve"):
        getattr(self.P, eng)(lambda e: e.scalar_tensor_tensor(out=out, in0=in0, scalar=scalar, in1=in1, op0=op0, op1=op1), r, w)

    def cp(self, out, in_, r, w, eng="dve"):
        if eng == "act":
            self.P.act(lambda e: e.copy(out=out, in_=in_), r, w)
        else:
            getattr(self.P, eng)(lambda e: e.tensor_copy(out=out, in_=in_), r, w)

    def memset(self, ap, v, w, eng="dve"):
        getattr(self.P, eng)(lambda e: e.memset(ap, v), (), w)

    def dma(self, out, in_, r, w, q="sp"):
        self.P.dma(lambda e: e.dma_start(out=out, in_=in_, allow_slow_non_contiguous=True), r, w, q=q)

    def _issue(self, g):
        per = len(PLAN)
        li = (g // per) % self.nl
        idx = g % per
        name, src, k0, kc, cols = PLAN[idx]
        n = kc * len(cols)
        s = g % NSLOT
        src_ap = self.wts[li, POFF[idx]:POFF[idx] + 128 * n].rearrange("(p n) -> p n", p=128)
        self.dma(self.ws[s][:, 0:n], src_ap, (), [("ws", s)], q="pool")

    def next_w(self, expect):
        g = self.wnext
        import os
        total = self.npass * self.nl * len(PLAN)
        lim = {0: 76, 1: 76, 2: 76 + 13, 3: 76 + 19, 4: 76 + 34}.get(int(os.environ.get("KSTOP", "9")))
        if lim is not None:
            total = lim
        if "KPROJ" in os.environ:
            total = 76 + int(os.environ["KPROJ"])
        if os.environ.get("KATT") == "1":
            total = 76 + 19 + 12
        while self.wissued < min(total, g + NSLOT - 1):
            self._issue(self.wissued)
            self.wissued += 1
        name = PLAN[g % len(PLAN)][0]
        assert name == expect, (name, expect, g)
        self.wnext += 1
        s = g % NSLOT
        return self.ws[s], ("ws", s)

    def fm(self, slot, skey, stride, c0, M, nk, src, srckey, evac):
        for pi, (p0, pn) in enumerate(self.pieces):
            bi, b = self.bank()
            for kc in range(nk):
                self.mm(b[0:M, 0:pn], slot[:, kc * stride + c0:kc * stride + c0 + M], src[:, kc, p0:p0 + pn],
                        kc == 0, kc == nk - 1, [skey, srckey], [("pb", bi)])
            evac(pi, p0, pn, ("pb", bi), b[0:M, 0:pn])

    def tmj(self, slot, skey, stride, c0, N, evac):
        for ti, (t0, tn) in enumerate(self.ttiles):
            bi, b = self.bank()
            for kc in range(KC):
                self.mm(b[0:tn, 0:N], self.xb[:, kc, t0:t0 + tn], slot[:, kc * stride + c0:kc * stride + c0 + N],
                        kc == 0, kc == KC - 1, [skey, "xb"], [("pb", bi)])
            evac(ti, t0, tn, ("pb", bi), b[0:tn, 0:N])

    def layernorm(self, l, i):
        self.phase()
        TT = self.TTc
        sq = [self.fa(512) for _ in range(2)]
        mean = self.fa(TT); rstd = self.fa(TT); tmp = [self.fa(TT) for _ in range(4)]
        gcol = lambda kc: self.lnp_sb[:, ((0 * NL + l) * 3 + i) * KC + kc:((0 * NL + l) * 3 + i) * KC + kc + 1]
        bcol = lambda kc: self.lnp_sb[:, ((1 * NL + l) * 3 + i) * KC + kc:((1 * NL + l) * 3 + i) * KC + kc + 1]
        for (p0, pn) in self.pieces:
            b1i, b1 = self.bank(); b2i, b2 = self.bank()
            for kc in range(KC):
                s = sq[kc % 2]
                self.act(s[:, 0:pn], self.xf[:, kc, p0:p0 + pn], AF.Square, ["xf"], [("sq", kc % 2)])
                self.mm(b1[:, 0:pn], self.onesD[:], self.xf[:, kc, p0:p0 + pn], kc == 0, kc == KC - 1, ["xf", "c"], [("pb", b1i)])
                self.mm(b2[:, 0:pn], self.onesD[:], s[:, 0:pn], kc == 0, kc == KC - 1, [("sq", kc % 2), "c"], [("pb", b2i)])
            self.cp(mean[:, p0:p0 + pn], b1[:, 0:pn], [("pb", b1i)], ["mean"])
            self.tt(tmp[0][:, p0:p0 + pn], mean[:, p0:p0 + pn], mean[:, p0:p0 + pn], ALU.mult, ["mean"], [("lt", 0)])
            self.tt(tmp[1][:, p0:p0 + pn], b2[:, 0:pn], tmp[0][:, p0:p0 + pn], ALU.subtract, [("pb", b2i), ("lt", 0)], [("lt", 1)])
            self.ts(tmp[1][:, p0:p0 + pn], tmp[1][:, p0:p0 + pn], EPS, None, ALU.add, None, [("lt", 1)], [("lt", 1)])
            self.act(tmp[0][:, p0:p0 + pn], tmp[1][:, p0:p0 + pn], AF.Sqrt, [("lt", 1)], [("lt", 0)])
            self.P.dve(lambda e, o=rstd[:, p0:p0 + pn], a=tmp[0][:, p0:p0 + pn]: e.reciprocal(out=o, in_=a), [("lt", 0)], ["rstd"])
        for kc in range(KC):
            t = tmp[kc % 4]; lk = ("lt", kc % 4)
            self.tt(t[:, 0:TT], self.xf[:, kc, 0:TT], mean[:, 0:TT], ALU.subtract, ["xf", "mean"], [lk])
            self.tt(t[:, 0:TT], t[:, 0:TT], rstd[:, 0:TT], ALU.mult, [lk, "rstd"], [lk], eng="pool")
            self.act(self.xf[:, kc, 0:TT], t[:, 0:TT], AF.Identity, [lk], [("xfo", kc)], bias=bcol(kc), scale=gcol(kc))
            self.act(self.xb[:, kc, 0:TT], t[:, 0:TT], AF.Identity, [lk], [("xbo", kc)], bias=bcol(kc), scale=gcol(kc))
        self.phase()

    def ffn(self, l, f):
        TT = self.TTc
        self.phase()
        sa = [self.fa(512) for _ in range(2)]
        for qd in range(4):
            for jl in range(11):
                slot, sk = self.next_w(f"up{f}")
                for pi, (p0, pn) in enumerate(self.pieces):
                    ai, a = self.bank(); bi, b = self.bank()
                    for kc in range(KC):
                        self.mm(a[:, 0:pn], slot[:, kc * 256:kc * 256 + 128], self.xb[:, kc, p0:p0 + pn], kc == 0, kc == KC - 1, [sk, "xb"], [("pb", ai)])
                    for kc in range(KC):
                        self.mm(b[:, 0:pn], slot[:, kc * 256 + 128:kc * 256 + 256], self.xb[:, kc, p0:p0 + pn], kc == 0, kc == KC - 1, [sk, "xb"], [("pb", bi)])
                    s = sa[(jl + pi) % 2]; skk = ("sa", (jl + pi) % 2)
                    self.act(s[:, 0:pn], a[:, 0:pn], AF.Silu, [("pb", ai)], [skk])
                    self.stt(self.br[:, jl, p0:p0 + pn], s[:, 0:pn], 0.5, b[:, 0:pn], ALU.mult, ALU.mult, [skk, ("pb", bi)], [("br", jl)])
            for mgi in range(8):
                slot, sk = self.next_w(f"dn{f}")
                for mc in range(2):
                    m = mgi * 2 + mc
                    for (p0, pn) in self.pieces:
                        bi, b = self.bank()
                        for jl in range(11):
                            self.mm(b[:, 0:pn], slot[:, jl * 256 + mc * 128:jl * 256 + mc * 128 + 128], self.br[:, jl, p0:p0 + pn],
                                    jl == 0, jl == 10, [sk, ("br", jl)], [("pb", bi)])
                        if qd == 0:
                            self.stt(self.xf[:, m, p0:p0 + pn], self.xf[:, m, p0:p0 + pn], ALPHA, b[:, 0:pn], ALU.mult, ALU.add,
                                     [("pb", bi), ("xf", m)], [("xf", m)])
                        else:
                            self.tt(self.xf[:, m, p0:p0 + pn], self.xf[:, m, p0:p0 + pn], b[:, 0:pn], ALU.add, [("pb", bi), ("xf", m)], [("xf", m)])

    def mlstm(self, l, ps):
        self.phase()
        TT = self.TTc; nt = len(self.ttiles); DKS = 96 ** -0.5
        qs32 = [self.fa(NS) for _ in range(4)]; ks32 = [self.fa(NS) for _ in range(4)]
        v32s = self.fa(772)
        ktm_s = self.fa(384); g8_s = self.fa(8); og_s = self.ba(768); vaug_s = self.ba(784)
        mark_f, mark_b = self.fo, self.bo
        qT = [self.ba(TT) for _ in range(4)]; kT = [self.ba(TT) for _ in range(4)]
        ktm = [self.fa(384) for _ in range(4)] + [ktm_s]
        vaug = [self.ba(784) for _ in range(4)] + [vaug_s]
        og = [self.ba(768) for _ in range(4)] + [og_s]
        g8 = [self.fa(8) for _ in range(4)] + [g8_s]
        for ti in range(nt):
            self.memset(vaug[ti], 1.0, [("vaug", ti)])
        self.memset(v32s, 1.0, ["v32s"])
        for nm, dst, d32 in (("mq", qT, qs32), ("mk", kT, ks32)):
            for hp in range(2):
                slot, sk = self.next_w(nm)
                for hh in range(2):
                    h = hp * 2 + hh
                    def ev(pi, p0, pn, bk, psap, h=h, dst=dst, d32=d32, nm=nm):
                        if pi == 0:
                            self.cp(dst[h][0:96, 0:pn], psap[0:96, :], [bk], [(nm, h)], eng="act")
                        else:
                            self.cp(d32[h][0:96, 0:NS], psap[0:96, :], [bk], [(nm + "s", h)])
                    self.fm(slot, sk, 192, hh * 96, 96, KC, self.xb, "xb", ev)
                if nm == "mk":
                    def ev2(ti, t0, tn, bk, psap, hp=hp):
                        self.cp(ktm[ti][0:tn, hp * 192:(hp + 1) * 192], psap, [bk], [("ktm", ti, hp)])
                    self.tmj(slot, sk, 192, 0, 192, ev2)
        import os
        KPROJ = int(os.environ.get("KPROJ", "99"))
        if KPROJ <= 4: return
        for h in range(4):
            slot, sk = self.next_w("mv")
            def ev(ti, t0, tn, bk, psap, h=h):
                self.cp(vaug[ti][0:tn, h * 196:h * 196 + 192], psap, [bk], [("vaug", ti)], eng=os.environ.get("KENG", "act"))
                if tn == NS and "KNOV32" not in os.environ:
                    self.cp(v32s[0:NS, h * 193:h * 193 + 192], psap, [bk], ["v32s"])
            self.tmj(slot, sk, 192, 0, 192, ev)
            if KPROJ <= 5 + h: return
        if KPROJ <= 8: return
        slot, sk = self.next_w("mg")
        def ev(ti, t0, tn, bk, psap):
            self.tt(g8[ti][0:tn, :], psap, self.bif_sb[0:tn, l * 8:(l + 1) * 8], ALU.add, [bk], [("g8", ti)])
        self.tmj(slot, sk, 8, 0, 8, ev)
        if KPROJ <= 9: return
        for h in range(4):
            slot, sk = self.next_w("mo")
            def ev(ti, t0, tn, bk, psap, h=h):
                self.act(og[ti][0:tn, h * 192:(h + 1) * 192], psap, AF.Sigmoid, [bk], [("og", ti)])
            self.tmj(slot, sk, 192, 0, 192, ev)
        import os
        KSUB = int(os.environ.get("KSUB", "9"))
        if KSUB <= 1: return
        sm = [self.fa(48) for _ in range(2)]
        hp_ = [self.fa(192) for _ in range(4)]
        hm = [self.fa(768) for _ in range(2)]
        PT = [self.ba(128) for _ in range(4)]
        ktil = [self.ba(384) for _ in range(2)]
        bn = [self.fa(8) for _ in range(4)]
        flb = self.fa(4)
        WB = dict(hp=hp_, hm=hm, bn=bn)
        gain = self.gain_sb[:, l * 768:(l + 1) * 768]
        def headnorm_g(src_ps, sc_col, tn, h, ti, par, rk, hb=0):
            hpp = WB['hp'][hb]; b6 = WB['bn'][hb]; hm = WB['hm']
            hk = ("hp", hb); bk_ = ("bn", hb)
            self.act(hpp[0:tn, :], src_ps, AF.Copy, rk, [hk], scale=sc_col); yield
            self.P.dve(lambda e: e.bn_stats(out=b6[0:tn, 0:6], in_=hpp[0:tn, :]), [hk], [bk_]); yield
            self.P.dve(lambda e: e.bn_aggr(out=b6[0:tn, 6:8], in_=b6[0:tn, 0:6]), [bk_], [bk_]); yield
            self.ts(b6[0:tn, 7:8], b6[0:tn, 7:8], EPS, None, ALU.add, None, [bk_], [bk_]); yield
            self.act(b6[0:tn, 7:8], b6[0:tn, 7:8], AF.Sqrt, [bk_], [bk_]); yield
            self.P.dve(lambda e: e.reciprocal(out=b6[0:tn, 7:8], in_=b6[0:tn, 7:8]), [bk_], [bk_]); yield
            self.ts(hpp[0:tn, :], hpp[0:tn, :], b6[0:tn, 6:7], b6[0:tn, 7:8], ALU.subtract, ALU.mult, [bk_, hk], [hk]); yield
            self.tt(hpp[0:tn, :], hpp[0:tn, :], gain[0:tn, h * 192:(h + 1) * 192], ALU.mult, [hk], [hk], eng="pool"); yield
            self.tt(hm[par][0:tn, h * 192:(h + 1) * 192], hpp[0:tn, :], og[ti][0:tn, h * 192:(h + 1) * 192], ALU.mult,
                    [hk, ("og", ti)], [("hm", par)]); yield
        def headnorm(*a, **k):
            for _ in headnorm_g(*a, **k):
                pass
        def rr(gens):
            gens = list(gens)
            while gens:
                for g_ in list(gens):
                    try:
                        next(g_)
                    except StopIteration:
                        gens.remove(g_)
        def to_fm(par, t0, tn):
            hm = WB['hm']
            for c in range(6):
                bi, b = self.bank()
                self.tr(b[:, 0:tn], hm[par][0:tn, c * 128:(c + 1) * 128], self.ident[0:tn, 0:tn], [("hm", par)], [("pb", bi)])
                self.cp(self.br[:, c, t0:t0 + tn], b[:, 0:tn], [("pb", bi)], [("br", c)], eng="act")
        if ps == 0:
            for h in range(4):
                self.memset(self.Cst[l][h][:], 0.0, [("Cst", h)])
                self.memset(self.Cbf[l][h][:], 0.0, [("Cbf", h)])
            self.memset(self.mst[l][:], 0.0, ["mst"])
        for ti in range(4):
            t0 = ti * 128; par = ti % 2; s = sm[par]; sk_ = ("sm", par)
            self.act(s[:, 0:4], g8[ti][:, 4:8], AF.Exp, [("g8", ti)], [sk_], scale=-1.0)
            self.act(s[:, 0:4], s[:, 0:4], AF.Ln, [sk_], [sk_], bias=1.0)
            ci, cb_ = self.bank(); toi, tob = self.bank()
            self.mm(cb_[:, 0:4], self.UT[:], s[:, 0:4], True, True, [sk_, "c"], [("pb", ci)])
            self.mm(tob[:, 0:4], self.ones[:], s[:, 0:4], True, True, [sk_, "c"], [("pb", toi)])
            self.act(s[:, 4:8], cb_[:, 0:4], AF.Exp, [("pb", ci)], [sk_], scale=-1.0)
            self.tt(s[:, 16:20], g8[ti][:, 0:4], cb_[:, 0:4], ALU.add, [("g8", ti), ("pb", ci)], [sk_])
            self.act(s[:, 8:12], s[:, 16:20], AF.Exp, [sk_], [sk_])
            self.ts(s[:, 8:12], s[:, 8:12], DKS, None, ALU.mult, None, [sk_], [sk_])
            self.tt(s[:, 24:28], s[:, 16:20], tob[:, 0:4], ALU.subtract, [sk_, ("pb", toi)], [sk_])
            self.act(s[:, 12:16], s[:, 24:28], AF.Exp, [sk_], [sk_])
            self.cp(s[:, 20:24], tob[:, 0:4], [("pb", toi)], [sk_])
            fi, fb = self.bank()
            self.mm(fb[0:96, 0:4], self.onesD[:, 0:96], s[:, 20:24], True, True, [sk_, "c"], [("pb", fi)])
            self.act(flb[0:96, 0:4], fb[0:96, 0:4], AF.Exp, [("pb", fi)], ["flb"], scale=-16.0)
            gi, gb = self.bank(); tti, ttb = self.bank()
            self.tr(gb[0:4, 0:128], s[:, 24:28], self.ident[:], [sk_], [("pb", gi)])
            self.tr(ttb[0:4, 0:128], s[:, 20:24], self.ident[:], [sk_], [("pb", tti)])
            self.P.dve(lambda e, o=s[0:4, 32:33], a=gb[0:4, 0:128]: e.reduce_max(out=o, in_=a, axis=AX.X), [("pb", gi)], [("smx", par)])
            self.cp(s[0:4, 33:34], ttb[0:4, 0:1], [("pb", tti)], [("smx", par)])
            self.stt(self.mst[l][:], self.mst[l][:], s[0:4, 33:34], s[0:4, 32:33], ALU.subtract, ALU.max, [("smx", par), "mst"], ["mst"])
            for h in range(4):
                self.ts(ktil[par][:, h * 96:(h + 1) * 96], ktm[ti][:, h * 96:(h + 1) * 96], s[:, 12 + h:13 + h], DKS, ALU.mult, ALU.mult,
                        [sk_, ("ktm", ti, h // 2)], [("ktil", par)], eng="pool")
            nbs = []
            for h in range(4):
                ai, ab = self.bank()
                self.mm(ab[:, 0:128], kT[h][0:96, t0:t0 + 128], qT[h][0:96, t0:t0 + 128], True, True, [("mk", h), ("mq", h)], [("pb", ai)])
                self.stt(PT[h][:, :], ab[:, 0:128], s[:, 8 + h:9 + h], self.UT[:], ALU.mult, ALU.mult, [("pb", ai), sk_], [("PT", h)])
            for h in range(4):
                ni, nb = self.bank(); nbs.append((ni, nb))
                self.mm(nb[:, 0:194], PT[h][:, :], vaug[ti][:, h * 196:h * 196 + 194], True, False, [("PT", h), ("vaug", ti)], [("pb", ni)])
                self.mm(nb[:, 0:194], qT[h][0:96, t0:t0 + 128], self.Cbf[l][h][:], False, True, [("mq", h), ("Cbf", h)], [("pb", ni)])
            def post(h):
                ni, nb = nbs[h]
                dcol = s[:, 40 + 2 * h:41 + 2 * h]; rcol = s[:, 41 + 2 * h:42 + 2 * h]; dk = ("dn", par, h)
                self.ts(dcol, nb[:, 192:193], s[:, 4 + h:5 + h], None, ALU.mult, None, [("pb", ni), sk_], [dk]); yield
                self.stt(dcol, dcol, -1.0, dcol, ALU.mult, ALU.max, [dk], [dk]); yield
                self.ts(dcol, dcol, 1.0, None, ALU.max, None, [dk], [dk]); yield
                self.P.dve(lambda e, o=rcol, a=dcol: e.reciprocal(out=o, in_=a), [dk], [dk]); yield
                self.tt(rcol, rcol, s[:, 4 + h:5 + h], ALU.mult, [dk, sk_], [dk]); yield
                yield from headnorm_g(nb[:, 0:192], rcol, 128, h, ti, par, [("pb", ni), dk], hb=h)
            rr(post(h) for h in range(4))
            for h in range(4):
                di, db = self.bank()
                self.mm(db[0:96, 0:194], ktil[par][:, h * 96:(h + 1) * 96], vaug[ti][:, h * 196:h * 196 + 194], True, True,
                        [("ktil", par), ("vaug", ti)], [("pb", di)])
                self.stt(self.Cst[l][h][:], self.Cst[l][h][:], flb[0:96, h:h + 1], db[0:96, 0:194], ALU.mult, ALU.add,
                         [("pb", di), "flb", ("Cst", h)], [("Cst", h)])
                self.cp(self.Cbf[l][h][:], self.Cst[l][h][:], [("Cst", h)], [("Cbf", h)], eng="act")
            to_fm(par, t0, 128)
        if KSUB <= 2: return
        if ps == self.npass - 1:
            s = sm[0]
            di4 = self.fa(4); emb = self.fa(4)
            self.ts(di4[0:4, 0:4], self.ident[0:4, 0:4], self.mst[l][:, 0:1], None, ALU.mult, None, ["mst"], ["di4"])
            bi, b = self.bank()
            self.mm(b[0:96, 0:4], self.ones[0:4, 0:96], di4[0:4, 0:4], True, True, ["di4", "c"], [("pb", bi)])
            self.act(emb[0:96, 0:4], b[0:96, 0:4], AF.Exp, [("pb", bi)], ["emb"], scale=-1.0)
            self.dma(self.o_pm[l].rearrange("(h o) -> h o", o=1), self.mst[l][:, 0:1], ["mst"], ["o_pm"])
            for h in range(4):
                co = self.fa(193)
                self.ts(co[0:96, :], self.Cst[l][h][:, 0:193], emb[0:96, h:h + 1], None, ALU.mult, None, [("Cst", h), "emb"], [("co", h)])
                self.dma(self.o_pC[l, h], co[0:96, 0:192], [("co", h)], ["o_pC"])
                self.dma(self.o_pn[l, h].rearrange("(k o) -> k o", o=1), co[0:96, 192:193], [("co", h)], ["o_pn"])
        if ps != 0 or KSUB <= 3:
            return
        self.P.barrier()
        self.fo, self.bo = mark_f, mark_b
        sm = [self.fa(40)]; WB['hp'] = [self.fa(192)]; WB['hm'] = [self.fa(768)]; WB['bn'] = [self.fa(8)]
        ti = 4; t0 = TP; s = sm[0]; sk_ = ("sms",)
        m0 = self.fa(4)
        self.dma(m0[0:NS, :], self.sm[l], (), ["m0"])
        self.act(s[0:NS, 0:4], g8[ti][0:NS, 4:8], AF.Exp, [("g8", ti)], [sk_], scale=-1.0)
        self.act(s[0:NS, 0:4], s[0:NS, 0:4], AF.Ln, [sk_], [sk_], bias=1.0)
        self.tt(s[0:NS, 4:8], m0[0:NS, :], s[0:NS, 0:4], ALU.subtract, ["m0", sk_], [sk_])
        self.tt(s[0:NS, 8:12], s[0:NS, 4:8], g8[ti][0:NS, 0:4], ALU.max, [sk_, ("g8", ti)], [sk_])
        self.dma(self.o_sm[l], s[0:NS, 8:12], [sk_], ["o_sm"])
        self.tt(s[0:NS, 12:16], s[0:NS, 4:8], s[0:NS, 8:12], ALU.subtract, [sk_], [sk_])
        self.act(s[0:NS, 12:16], s[0:NS, 12:16], AF.Exp, [sk_], [sk_])
        self.tt(s[0:NS, 16:20], g8[ti][0:NS, 0:4], s[0:NS, 8:12], ALU.subtract, [sk_, ("g8", ti)], [sk_])
        self.act(s[0:NS, 16:20], s[0:NS, 16:20], AF.Exp, [sk_], [sk_])
        self.act(s[0:NS, 20:24], g8[ti][0:NS, 0:4], AF.Exp, [("g8", ti)], [sk_])
        self.act(s[0:NS, 24:28], s[0:NS, 4:8], AF.Exp, [sk_], [sk_])
        Dm = self.fa(64); dbc = self.fa(64)
        for b_ in range(NS):
            self.ts(Dm[0:NS, b_ * 4:(b_ + 1) * 4], s[0:NS, 12:16], self.ident[0:NS, b_:b_ + 1], None, ALU.mult, None, [sk_], ["Dm"])
        bi, b = self.bank()
        self.mm(b[0:96, 0:64], self.ones[0:NS, 0:96], Dm[0:NS, 0:64], True, True, ["Dm", "c"], [("pb", bi)])
        self.cp(dbc[0:96, 0:64], b[0:96, 0:64], [("pb", bi)], ["dbc"])
        qk = self.fa(4); pr = self.fa(NS)
        for h in range(4):
            self.tt(pr[0:96, 0:NS], qs32[h][0:96, 0:NS], ks32[h][0:96, 0:NS], ALU.mult, [("mqs", h), ("mks", h)], ["pr"])
            bi, b = self.bank()
            self.mm(b[0:NS, 0:1], pr[0:96, 0:NS], self.ones[0:96, 0:1], True, True, ["pr", "c"], [("pb", bi)])
            self.ts(qk[0:NS, h:h + 1], b[0:NS, 0:1], DKS, None, ALU.mult, None, [("pb", bi)], [("qk", h)])
        Cin = self.fa(NS * 193); nin = self.fa(NS); QE = self.fa(NS * NS); KE = self.fa(NS * 96); vt = self.fa(193)
        Cn = [self.fa(193) for _ in range(2)]; nout = self.fa(NS); hsrc = self.fa(193)
        Cin3 = Cin.rearrange("p (b v) -> p b v", b=NS)
        self.memset(QE[0:96, :], 0.0, ["QE"])
        for h in range(4):
            self.dma(Cin3[0:96, :, 0:192], self.sC[l, :, h].rearrange("b k v -> k b v"), (), [("Cin", h)])
            self.dma(nin[0:96, 0:NS], self.snT[l, h], (), [("nin", h)])
            self.cp(Cin3[0:96, :, 192], nin[0:96, 0:NS], [("nin", h)], [("Cin", h)])
            self.cp(QE[0:96, 0:NS * NS:NS + 1], qs32[h][0:96, 0:NS], [("mqs", h)], ["QE"])
            QE3 = QE.rearrange("p (a b) -> p a b", a=NS)
            qi, qb = self.bank()
            for b_ in range(NS):
                self.mm(qb[0:NS, 0:193], QE3[0:96, b_, :], Cin3[0:96, b_, :], b_ == 0, b_ == NS - 1, ["QE", ("Cin", h)], [("pb", qi)])
            qC = self.fa(193)
            self.cp(qC[0:NS, :], qb[0:NS, 0:193], [("pb", qi)], [("qC", h)])
            self.ts(vt[0:NS, :], v32s[0:NS, h * 193:(h + 1) * 193], s[0:NS, 16 + h:17 + h], None, ALU.mult, None, ["v32s", sk_], ["vt"])
            KE3 = KE.rearrange("p (b k) -> p b k", b=NS)
            for b_ in range(NS):
                self.ts(KE3[0:NS, b_, :], ktm[ti][0:NS, h * 96:(h + 1) * 96], self.ident[0:NS, b_:b_ + 1], DKS, ALU.mult, ALU.mult,
                        [("ktm", ti, h // 2)], ["KE"], eng="pool")
            for b_ in range(NS):
                oi, ob = self.bank()
                self.mm(ob[0:96, 0:193], KE3[0:NS, b_, :], vt[0:NS, :], True, True, ["KE", "vt"], [("pb", oi)])
                cn = Cn[b_ % 2]
                self.stt(cn[0:96, :], Cin3[0:96, b_, :], dbc[0:96, b_ * 4 + h:b_ * 4 + h + 1], ob[0:96, 0:193], ALU.mult, ALU.add,
                         [("pb", oi), ("Cin", h), "dbc"], [("Cn", b_ % 2)])
                self.dma(self.o_sC[l, b_, h], cn[0:96, 0:192], [("Cn", b_ % 2)], ["o_sC"])
                self.cp(nout[0:96, b_:b_ + 1], cn[0:96, 192:193], [("Cn", b_ % 2)], ["nout"], eng="act")
            self.dma(self.o_snT[l, h], nout[0:96, 0:NS], ["nout"], ["o_snT"])
            self.tt(s[0:NS, 28:29], s[0:NS, 20 + h:21 + h], qk[0:NS, h:h + 1], ALU.mult, [sk_, ("qk", h)], [("s1",)])
            self.ts(hsrc[0:NS, :], v32s[0:NS, h * 193:(h + 1) * 193], s[0:NS, 28:29], None, ALU.mult, None, ["v32s", ("s1",)], ["hsrc"])
            self.stt(hsrc[0:NS, :], qC[0:NS, :], s[0:NS, 24 + h:25 + h], hsrc[0:NS, :], ALU.mult, ALU.add, [("qC", h), sk_, "hsrc"], ["hsrc"])
            self.stt(s[0:NS, 29:30], hsrc[0:NS, 192:193], -1.0, hsrc[0:NS, 192:193], ALU.mult, ALU.max, ["hsrc"], [("s2",)])
            self.ts(s[0:NS, 29:30], s[0:NS, 29:30], 1.0, None, ALU.max, None, [("s2",)], [("s2",)])
            self.P.dve(lambda e, o=s[0:NS, 30:31], a=s[0:NS, 29:30]: e.reciprocal(out=o, in_=a), [("s2",)], [("s3",)])
            headnorm(hsrc[0:NS, 0:192], s[0:NS, 30:31], NS, h, ti, 0, ["hsrc", ("s3",)])
        to_fm(0, t0, NS)

    def conv(self, l, ps):
        self.phase()
        TT = self.TTc
        W = TT + 2
        cw = lambda j, c: self.convw_sb[:, (l * 3 + j) * 4 + c:(l * 3 + j) * 4 + c + 1]
        if ps == 0:
            self.memset(self.ctail[l][:], 0.0, ["ctail"])
        for cp in range(2):
            buf = {}
            for nm in ("cb", "cc", "ch"):
                slot, sk = self.next_w(nm)
                for mc in range(2):
                    t = self.fa(W); buf[(nm, mc)] = t
                    def ev(pi, p0, pn, bk, psap, t=t, nm=nm, mc=mc):
                        self.cp(t[:, 2 + p0:2 + p0 + pn], psap, [bk], [(nm, cp, mc)], eng="act" if pi == 0 else "dve")
                    self.fm(slot, sk, 256, mc * 128, 128, KC, self.xb, "xb", ev)
            for mc in range(2):
                c = cp * 2 + mc
                cb, cc, ch = buf[("cb", mc)], buf[("cc", mc)], buf[("ch", mc)]
                kk = ("u", cp, mc)
                self.tt(cc[:, 2:2 + TT], cc[:, 2:2 + TT], ch[:, 2:2 + TT], ALU.mult, [("cc", cp, mc), ("ch", cp, mc)], [kk])
                self.cp(cc[:, 0:2], self.ctail[l][:, c, :], ["ctail"], [kk])
                acc = ch
                self.ts(acc[:, 2:2 + TP], cc[:, 0:TP], cw(0, c), None, ALU.mult, None, [kk], [("acc", cp, mc)])
                self.stt(acc[:, 2:2 + TP], cc[:, 1:1 + TP], cw(1, c), acc[:, 2:2 + TP], ALU.mult, ALU.add, [kk, ("acc", cp, mc)], [("acc", cp, mc)])
                self.stt(acc[:, 2:2 + TP], cc[:, 2:2 + TP], cw(2, c), acc[:, 2:2 + TP], ALU.mult, ALU.add, [kk, ("acc", cp, mc)], [("acc", cp, mc)])
                self.tt(self.br[:, 6 + c, 0:TP], acc[:, 2:2 + TP], cb[:, 2:2 + TP], ALU.mult, [("acc", cp, mc), ("cb", cp, mc)], [("br", 6 + c)])
                if ps == 0:
                    st = self.fa(2 * NS)
                    self.dma(st[:, 0:2 * NS].rearrange("p (j b) -> p j b", j=2), self.scT[l, :, c * 128:(c + 1) * 128, :].rearrange("j p b -> p j b"), (), [("st", c)])
                    us = cc[:, 2 + TP:2 + TP + NS]
                    a2 = acc[:, 2 + TP:2 + TP + NS]
                    self.ts(a2, st[:, 0:NS], cw(0, c), None, ALU.mult, None, [("st", c)], [("acc", cp, mc)])
                    self.stt(a2, st[:, NS:2 * NS], cw(1, c), a2, ALU.mult, ALU.add, [("st", c), ("acc", cp, mc)], [("acc", cp, mc)])
                    self.stt(a2, us, cw(2, c), a2, ALU.mult, ALU.add, [kk, ("acc", cp, mc)], [("acc", cp, mc)])
                    self.tt(self.br[:, 6 + c, TP:TP + NS], a2, cb[:, 2 + TP:2 + TP + NS], ALU.mult, [("acc", cp, mc), ("cb", cp, mc)], [("br", 6 + c)])
                    self.dma(self.o_sconvT[l, 0, c * 128:(c + 1) * 128, :], st[:, NS:2 * NS], [("st", c)], ["o_sconv"])
                    self.dma(self.o_sconvT[l, 1, c * 128:(c + 1) * 128, :], us, [kk], ["o_sconv"])
                self.cp(self.ctail[l][:, c, :], cc[:, TP:TP + 2], [kk], ["ctail"], eng="pool")
                if ps == self.npass - 1:
                    self.dma(self.o_pconvT[l, c * 128:(c + 1) * 128, :], cc[:, TP:TP + 2], [kk], ["o_pconv"])

    def attn(self, l, ps):
        self.phase()
        TT = self.TTc
        base = ps * TP
        qst = self.ba(6 * TP).rearrange("p (c t) -> p c t", c=6)
        qs32 = self.fa(6 * NS).rearrange("p (c b) -> p c b", c=6)
        ks32 = self.fa(6 * NS).rearrange("p (c b) -> p c b", c=6)
        vs32 = self.fa(6 * NS).rearrange("p (c b) -> p c b", c=6)
        mark_f, mark_b = self.fo, self.bo
        cosb = self.fa(TT); sinb = self.fa(TT)
        self.dma(cosb[:, 0:TP], self.cosT[:, base:base + TP], (), ["cos"])
        self.dma(sinb[:, 0:TP], self.sinT[:, base:base + TP], (), ["sin"])
        if ps == 0:
            for b_ in range(NS):
                pass
            self.dma(cosb[:, TP:TP + 1], self.cosT[:, SEQ:SEQ + 1], (), ["cos"])
            self.dma(sinb[:, TP:TP + 1], self.sinT[:, SEQ:SEQ + 1], (), ["sin"])
        kst = self.ba(6 * TP).rearrange("p (c t) -> p c t", c=6)
        t1 = [self.fa(TT) for _ in range(2)]; t2 = [self.fa(TT) for _ in range(2)]; kro = [self.fa(TP) for _ in range(2)]
        it = 0
        for nm, st, s32 in (("aq", qst, qs32), ("ak", kst, ks32)):
            for t in range(3):
                slot, sk = self.next_w(nm)
                slot2, sk2 = self.next_w(nm + "s")
                g = t; d = GD[g]; ni = TP // d
                for mc in range(2):
                    c = t * 2 + mc
                    par = it % 2; it += 1
                    banks = {}
                    def ev(pi, p0, pn, bk, psap, par=par):
                        self.tt(t1[par][:, p0:p0 + pn], psap, cosb[:, p0:p0 + pn] if pi == 0 else cosb[:, TP:TP + 1].to_broadcast([128, NS]), ALU.mult,
                                [bk, "cos"], [("t1", par)])
                    def ev2(pi, p0, pn, bk, psap, par=par):
                        self.tt(t2[par][:, p0:p0 + pn], psap, sinb[:, p0:p0 + pn] if pi == 0 else sinb[:, TP:TP + 1].to_broadcast([128, NS]), ALU.mult,
                                [bk, "sin"], [("t2", par)])
                    self.fm(slot, sk, 256, mc * 128, 128, KC, self.xb, "xb", ev)
                    self.fm(slot2, sk2, 256, mc * 128, 128, KC, self.xb, "xb", ev2)
                    stv = st[:, c, :].rearrange("p (r i) -> p i r", r=d)
                    if nm == "ak":
                        self.tt(kro[par][:, 0:TP], t1[par][:, 0:TP], t2[par][:, 0:TP], ALU.add, [("t1", par), ("t2", par)], [("kro", par)], eng="pool")
                        self.cp(stv, kro[par][:, 0:TP].rearrange("p (i r) -> p i r", r=d), [("kro", par)], [("kst", c)], eng="act")
                        W_ = (128, 512, 2048)[g]
                        lo = SEQ - W_
                        a = max(base, lo)
                        if a < base + TP:
                            self.dma(self.o_pk[g][l, mc * 128:(mc + 1) * 128, a - lo:base + TP - lo], kro[par][:, a - base:TP], [("kro", par)], ["o_pk"])
                        self.dma(self.hK[l][g][mc * 128:(mc + 1) * 128, :].rearrange("p (r x) -> p r x", r=d)[:, :, ni * ps:ni * (ps + 1)],
                                 st[:, c, :].rearrange("p (r i) -> p r i", r=d), [("kst", c)], [("hK", g)])
                    else:
                        self.tt(stv, t1[par][:, 0:TP].rearrange("p (i r) -> p i r", r=d), t2[par][:, 0:TP].rearrange("p (i r) -> p i r", r=d), ALU.add,
                                [("t1", par), ("t2", par)], [("qst", c)], eng="pool")
                    if ps == 0:
                        self.tt(s32[:, c, :], t1[par][:, TP:TP + NS], t2[par][:, TP:TP + NS], ALU.add, [("t1", par), ("t2", par)], [(nm + "32", c)])
                        if nm == "ak":
                            self.dma(self.o_skT[l, c * 128:(c + 1) * 128, :], s32[:, c, :], [(nm + "32", c)], ["o_sk"])
        import os
        KATT = int(os.environ.get("KATT", "9"))
        if KATT <= 1: return
        vb = [self.ba(256) for _ in range(2)]; vf = [self.fa(256) for _ in range(2)]
        it = 0
        for t in range(3):
            slot, sk = self.next_w("av")
            g = t
            def ev(ti, t0, tn, bk, psap, g=g):
                nonlocal it
                if tn != 128:
                    return
                par = it % 2; it += 1
                self.cp(vb[par][:, :], psap, [bk], [("vb", par)], eng="act")
                self.dma(self.hV[l][g][base + t0:base + t0 + 128, :], vb[par][:, :], [("vb", par)], [("hV", g)])
                W_ = (128, 512, 2048)[g]; lo = SEQ - W_
                if base + t0 >= lo:
                    self.cp(vf[par][:, :], psap, [bk], [("vf", par)])
                    self.dma(self.o_pv[g][l, base + t0 - lo:base + t0 - lo + 128, :], vf[par][:, :], [("vf", par)], ["o_pv"])
            self.tmj(slot, sk, 256, 0, 256, ev)
            if ps == 0:
                for mc in range(2):
                    c = t * 2 + mc
                    bi, b = self.bank()
                    for kc in range(KC):
                        self.mm(b[:, 0:NS], slot[:, kc * 256 + mc * 128:kc * 256 + mc * 128 + 128], self.xb[:, kc, TP:TP + NS], kc == 0, kc == KC - 1,
                                [sk, "xb"], [("pb", bi)])
                    self.cp(vs32[:, c, :], b[:, 0:NS], [("pb", bi)], [("vs32", c)])
                    self.dma(self.o_svT[l, c * 128:(c + 1) * 128, :], vs32[:, c, :], [("vs32", c)], ["o_sv"])
        if KATT <= 2: return
        self.P.barrier()
        self.fo, self.bo = mark_f, mark_b
        accN = self.fa(4 * TT).rearrange("p (s t) -> p s t", s=4)
        accD = self.fa(4 * TT).rearrange("p (s t) -> p s t", s=4)
        kt = [self.ba(512).rearrange("p (c k) -> p c k", c=2) for _ in range(2)]
        vt = [self.ba(512).rearrange("p (n f) -> p n f", n=2) for _ in range(2)]
        PT = [self.ba(512) for _ in range(4)]
        ex = [self.fa(512) for _ in range(2)]
        it = 0; pti = 0
        for g in range(int(os.environ.get("KAG", "3"))):
            d = GD[g]; ni = TP // d; L_ = SEQ // d
            qn = 128 if g < 2 else 32
            for r in range(d):
                for qb in range(ni // qn):
                    par = it % 2; it += 1
                    i0 = ni * ps + qb * qn
                    if g < 2:
                        k0 = max(0, i0 - 128); nk = i0 + 128 - k0
                    else:
                        k0 = 0; nk = i0 + 32
                    nblk = (nk + 127) // 128
                    self.dma(kt[par][:, :, 0:nk], self.hK[l][g].rearrange("(c p) x -> p c x", c=2)[:, :, r * L_ + k0:r * L_ + k0 + nk], [("hK", g)], [("kt", par)])
                    for n in range(nblk):
                        kk0 = k0 + n * 128; kn = min(128, nk - n * 128)
                        rows = self.hV[l][g].rearrange("(i r) f -> r i f", r=d)[r, kk0:kk0 + kn, :]
                        self.dma(vt[par][0:kn, n, :], rows, [("hV", g)], [("vt", par)])
                    qcol = r * ni + qb * qn
                    pts = []
                    for n in range(nblk):
                        kn = min(128, nk - n * 128)
                        e = ex[pti % 2]; ek = ("ex", pti % 2)
                        for u in range(2):
                            si, sb_ = self.bank()
                            for c2 in range(2):
                                self.mm(sb_[0:kn, c2 * qn:(c2 + 1) * qn], kt[par][64 * u:64 * u + 64, c2, n * 128:n * 128 + kn],
                                        qst[64 * u:64 * u + 64, 2 * g + c2, qcol:qcol + qn], True, True, [("kt", par), ("qst", 2 * g + c2)], [("pb", si)])
                            self.act(e[0:kn, u * 2 * qn:(u + 1) * 2 * qn], sb_[0:kn, 0:2 * qn], AF.Exp, [("pb", si)], [ek], scale=0.125)
                        if g < 2:
                            diag = (n == nblk - 1)
                            mask = (self.UT4 if diag else self.LT4)[0:kn, :]
                        else:
                            mask = self.M2[0:kn, ps, :]
                        p_ = PT[pti % 4]; pk = ("PT", pti % 4); pti += 1
                        self.tt(p_[0:kn, 0:4 * qn], e[0:kn, 0:4 * qn], mask, ALU.mult, [ek], [pk])
                        pts.append((p_, pk, kn, n))
                    ni_, nbk = self.bank(); di_, dbk = self.bank()
                    for hh in range(4):
                        ph = (hh % 2) * 2 + hh // 2
                        for j, (p_, pk, kn, n) in enumerate(pts):
                            self.mm(nbk[0:64, hh * qn:(hh + 1) * qn], vt[par][0:kn, n, hh * 64:(hh + 1) * 64], p_[0:kn, ph * qn:(ph + 1) * qn],
                                    j == 0, j == len(pts) - 1, [("vt", par), pk], [("pb", ni_)])
                        for j, (p_, pk, kn, n) in enumerate(pts):
                            self.mm(dbk[0:64, hh * qn:(hh + 1) * qn], self.onesb[0:kn, 0:64], p_[0:kn, ph * qn:(ph + 1) * qn],
                                    j == 0, j == len(pts) - 1, ["c", pk], [("pb", di_)])
                    tsl = slice(qb * qn * d + r, qb * qn * d + r + (qn - 1) * d + 1, d)
                    nv = nbk[0:64, 0:4 * qn].rearrange("p (s q) -> p s q", s=4)
                    dv = dbk[0:64, 0:4 * qn].rearrange("p (s q) -> p s q", s=4)
                    if g == 0:
                        self.cp(accN[0:64, :, tsl], nv, [("pb", ni_)], ["accN"])
                        self.cp(accD[0:64, :, tsl], dv, [("pb", di_)], ["accD"], eng="act")
                    else:
                        self.tt(accN[0:64, :, tsl], accN[0:64, :, tsl], nv, ALU.add, [("pb", ni_), "accN"], ["accN"])
                        self.tt(accD[0:64, :, tsl], accD[0:64, :, tsl], dv, ALU.add, [("pb", di_), "accD"], ["accD"], eng="pool" if False else "dve")
        if KATT <= 3: return
        if ps == 0:
            ones64 = self.ones
            kc_ = [self.fa(512) for _ in range(2)]
            kT_ = [self.fa(256).rearrange("p (c k) -> p c k", c=2) for _ in range(2)]
            Pm = [self.fa(4) for _ in range(2)]
            it = 0
            sN = self.fa(4 * NS).rearrange("p (s b) -> p s b", s=4); sD = self.fa(4 * NS).rearrange("p (s b) -> p s b", s=4)
            pr = self.fa(NS)
            first = True
            for g in range(3):
                for hh in range(4):
                    c = 2 * g + hh // 2; u = hh % 2
                    self.tt(pr[64 * u:64 * u + 64, 0:NS], qs32[64 * u:64 * u + 64, c, :], ks32[64 * u:64 * u + 64, c, :], ALU.mult, [("aq32", c), ("ak32", c)], ["pr"])
                    bi, b = self.bank()
                    self.mm(b[0:64, 0:NS], ones64[64 * u:64 * u + 64, 0:64], pr[64 * u:64 * u + 64, 0:NS], True, True, ["pr", "c"], [("pb", bi)])
                    pe_ = self.fa(NS)
                    self.act(pe_[0:64, 0:NS], b[0:64, 0:NS], AF.Exp, [("pb", bi)], [("pe", g, hh)], scale=0.125)
                    vi, vbk = self.bank()
                    self.mm(vbk[0:64, 0:NS], self.ident[64 * u:64 * u + 64, 64 * u:64 * u + 64], vs32[64 * u:64 * u + 64, c, :], True, True,
                            [("vs32", c), "c"], [("pb", vi)])
                    if g == 0:
                        self.tt(sN[0:64, hh, :], pe_[0:64, 0:NS], vbk[0:64, 0:NS], ALU.mult, [("pe", g, hh), ("pb", vi)], [("sN", hh)])
                        self.cp(sD[0:64, hh, :], pe_[0:64, 0:NS], [("pe", g, hh)], [("sD", hh)])
                    else:
                        tmpv = self.fa(NS)
                        self.tt(tmpv[0:64, 0:NS], pe_[0:64, 0:NS], vbk[0:64, 0:NS], ALU.mult, [("pe", g, hh), ("pb", vi)], [("tmpv", g, hh)])
                        self.tt(sN[0:64, hh, :], sN[0:64, hh, :], tmpv[0:64, 0:NS], ALU.add, [("tmpv", g, hh), ("sN", hh)], [("sN", hh)])
                        self.tt(sD[0:64, hh, :], sD[0:64, hh, :], pe_[0:64, 0:NS], ALU.add, [("pe", g, hh), ("sD", hh)], [("sD", hh)])
            for g in range(3):
                d = GD[g]
                for b_ in range(NS):
                    par = it % 2; it += 1
                    self.dma(kc_[par][:, :], self.ck[g][l, b_].rearrange("(j r) f -> r j f", r=d)[0, :, :], (), [("kc", par)])
                    for c2 in range(2):
                        ti_, tb = self.bank()
                        self.tr(tb[:, 0:128], kc_[par][:, c2 * 128:(c2 + 1) * 128], self.ident[:], [("kc", par)], [("pb", ti_)])
                        self.cp(kT_[par][:, c2, :], tb[:, 0:128], [("pb", ti_)], [("kT_", par, c2)], eng="act")
                    for u in range(2):
                        si, sb_ = self.bank()
                        for c2 in range(2):
                            self.mm(sb_[:, c2:c2 + 1], kT_[par][64 * u:64 * u + 64, c2, :], qs32[64 * u:64 * u + 64, 2 * g + c2, b_:b_ + 1], True, True,
                                    [("kT_", par, c2), ("aq32", 2 * g + c2)], [("pb", si)])
                        self.act(Pm[par][:, 2 * u:2 * u + 2], sb_[:, 0:2], AF.Exp, [("pb", si)], [("Pm", par)], scale=0.125)
                    ni_, nbk = self.bank()
                    for hh in range(4):
                        ph = (hh % 2) * 2 + hh // 2
                        self.mm(nbk[0:64, hh:hh + 1], kc_[par][:, 256 + hh * 64:256 + (hh + 1) * 64], Pm[par][:, ph:ph + 1], True, True,
                                [("kc", par), ("Pm", par)], [("pb", ni_)])
                        self.mm(nbk[0:64, 4 + hh:5 + hh], self.ones[:, 0:64], Pm[par][:, ph:ph + 1], True, True, ["c", ("Pm", par)], [("pb", ni_)])
                    self.tt(sN[0:64, :, b_], sN[0:64, :, b_], nbk[0:64, 0:4], ALU.add, [("pb", ni_), ("sN", 0), ("sN", 1), ("sN", 2), ("sN", 3)],
                            [("sN", 0), ("sN", 1), ("sN", 2), ("sN", 3)])
                    self.tt(sD[0:64, :, b_], sD[0:64, :, b_], nbk[0:64, 4:8], ALU.add, [("pb", ni_), ("sD", 0), ("sD", 1), ("sD", 2), ("sD", 3)],
                            [("sD", 0), ("sD", 1), ("sD", 2), ("sD", 3)])
            for hh in range(4):
                self.cp(accN[0:64, hh, TP:TP + NS], sN[0:64, hh, :], [("sN", hh)], ["accN"])
                self.cp(accD[0:64, hh, TP:TP + NS], sD[0:64, hh, :], [("sD", hh)], ["accD"])
        for hh in range(4):
            self.P.dve(lambda e, o=accD[0:64, hh, 0:TT], a=accD[0:64, hh, 0:TT]: e.reciprocal(out=o, in_=a), ["accD"], [("rD", hh)])
            self.tt(self.br[0:64, 10 + hh, 0:TT], accN[0:64, hh, 0:TT], accD[0:64, hh, 0:TT], ALU.mult, ["accN", ("rD", hh)], [("br", 10 + hh)])

    def merge(self, l, ps):
        self.phase()
        TT = self.TTc
        if self.dbg is not None and l == 0 and ps == 0:
            dbt = self.fa(14 * NS).rearrange("p (c b) -> p c b", c=14)
            self.cp(dbt, self.br[:, :, TP:TP + NS], [("br", c) for c in range(14)], ["dbt"])
            self.dma(self.dbg.rearrange("(c p) b -> p c b", p=128), dbt, ["dbt"], ["o_dbg"])
        gt = [self.ba(TT) for _ in range(3)]
        acc = [self.fa(TT) for _ in range(2)]
        KB = ((0, 6), (6, 4), (10, 4))
        for m in range(16):
            slotA, skA = self.next_w("gA")
            for b_ in range(2):
                def ev(pi, p0, pn, bk, psap, b_=b_):
                    self.act(gt[b_][:, p0:p0 + pn], psap, AF.Sigmoid, [bk], [("gt", b_)])
                self.fm(slotA, skA, 256, b_ * 128, 128, KC, self.xb, "xb", ev)
            slotB, skB = self.next_w("gB")
            def ev(pi, p0, pn, bk, psap):
                self.act(gt[2][:, p0:p0 + pn], psap, AF.Sigmoid, [bk], [("gt", 2)])
            self.fm(slotB, skB, 128, 0, 128, KC, self.xb, "xb", ev)
            slotU, skU = self.next_w("up")
            a = acc[m % 2]; ak = ("macc", m % 2)
            for bi_, (k0, nk) in enumerate(KB):
                for (p0, pn) in self.pieces:
                    bi, b = self.bank()
                    for j in range(nk):
                        self.mm(b[:, 0:pn], slotU[:, (k0 + j) * 128:(k0 + j + 1) * 128], self.br[:, k0 + j, p0:p0 + pn], j == 0, j == nk - 1,
                                [skU, ("br", k0 + j)], [("pb", bi)])
                    if bi_ == 0:
                        self.tt(a[:, p0:p0 + pn], b[:, 0:pn], gt[0][:, p0:p0 + pn], ALU.mult, [("pb", bi), ("gt", 0)], [ak])
                    else:
                        t_ = gt[bi_]
                        self.tt(t_[:, p0:p0 + pn], b[:, 0:pn], t_[:, p0:p0 + pn], ALU.mult, [("pb", bi), ("gt", bi_)], [("gt", bi_)])
                        if bi_ == 1:
                            self.tt(a[:, p0:p0 + pn], a[:, p0:p0 + pn], t_[:, p0:p0 + pn], ALU.add, [ak, ("gt", bi_)], [ak], eng="pool")
                        else:
                            self.tt(self.mg[:, m, p0:p0 + pn], a[:, p0:p0 + pn], t_[:, p0:p0 + pn], ALU.add, [ak, ("gt", bi_)], [("mg", m)], eng="pool")
        for t in range(8):
            slot, sk = self.next_w("wo")
            for mc in range(2):
                m = t * 2 + mc
                for (p0, pn) in self.pieces:
                    bi, b = self.bank()
                    for kc in range(KC):
                        self.mm(b[:, 0:pn], slot[:, kc * 256 + mc * 128:kc * 256 + mc * 128 + 128], self.mg[:, kc, p0:p0 + pn], kc == 0, kc == KC - 1,
                                [sk, ("mg", kc)], [("pb", bi)])
                    self.stt(self.xf[:, m, p0:p0 + pn], self.xf[:, m, p0:p0 + pn], ALPHA, b[:, 0:pn], ALU.mult, ALU.add, [("pb", bi), ("xf", m)], [("xf", m)])

    def build(self):
        self.declare(); self.alloc()
        P = self.P
        for dst, src in ((self.UT, self.c_UT), (self.ident, self.c_id), (self.UT4, self.c_UT4), (self.LT4, self.c_LT4),
                         (self.lnp_sb, self.lnp), (self.convw_sb, self.convw), (self.gain_sb, self.gain), (self.bif_sb, self.bif)):
            self.dma(dst[:], src, (), ["c"])
        self.dma(self.M2[:], self.c_M2.rearrange("n p x -> p n x"), (), ["c"])
        self.memset(self.onesD[:], 1.0 / D, ["c"]); self.memset(self.ones[:], 1.0, ["c"]); self.memset(self.onesb[:], 1.0, ["c"])
        for ps in range(self.npass):
            self.phase()
            base = ps * TP
            self.TTc = TP + NS if ps == 0 else TP
            self.pieces = [(0, TP)] + ([(TP, NS)] if ps == 0 else [])
            self.ttiles = [(i * 128, 128) for i in range(4)] + ([(TP, NS)] if ps == 0 else [])
            TT = self.TTc
            self.dma(self.xf[:, :, 0:TP], self.xT.rearrange("(c p) t -> p c t", p=128)[:, :, base:base + TP], (), ["xf"])
            if ps == 0:
                self.dma(self.xf[:, :, TP:TP + NS], self.xsT.rearrange("(c p) t -> p c t", p=128), (), ["xf"])
            for kc in range(KC):
                self.cp(self.xb[:, kc, 0:TT], self.xf[:, kc, 0:TT], ["xf"], [("xb", kc)], eng="act" if kc % 2 else "dve")
            import os
            STOP = int(os.environ.get("KSTOP", "9"))
            for l in range(self.nl):
                self.ffn(l, 0)
                if STOP <= 0: break
                self.layernorm(l, 0)
                if STOP <= 1: break
                self.mlstm(l, ps)
                if STOP <= 2: break
                self.conv(l, ps)
                if STOP <= 3: break
                self.P.dve(lambda e: e.memset(self.br[64:128, 10:14, :], 0.0), (), [("br", 10), ("br", 11), ("br", 12), ("br", 13)])
                self.attn(l, ps)
                if STOP <= 4: break
                self.merge(l, ps)
                self.layernorm(l, 1)
                self.ffn(l, 1)
                self.layernorm(l, 2)
            self.phase()
            self.dma(self.o_yT.rearrange("(c p) t -> p c t", p=128)[:, :, base:base + TP], self.xf[:, :, 0:TP], ["xf"], ["o_y"])
            if ps == 0:
                self.dma(self.o_ysT.rearrange("(c p) t -> p c t", p=128), self.xf[:, :, TP:TP + NS], ["xf"], ["o_ys"])
        self.P.barrier()
        P.emit()


_CACHE = {}


def _prep_core(c, inp, wts, consts):
    b = c % 4
    s0, s1 = c * NS, (c + 1) * NS
    m = dict(consts)
    m["wts"] = wts
    m["xT"] = np.ascontiguousarray(inp["x_prompt"][b].T)
    m["xsT"] = np.ascontiguousarray(inp["x_sample"][s0:s1, 0].T)
    g = inp["ln_g"]; bb = inp["ln_b"]
    lnp = np.stack([g, bb]).reshape(2, NL, 3, KC, 128)
    m["lnp"] = np.ascontiguousarray(lnp.transpose(4, 0, 1, 2, 3).reshape(128, -1))
    cw = inp["conv_w"].reshape(NL, 3, 4, 128)
    m["convw"] = np.ascontiguousarray(cw.transpose(3, 0, 1, 2).reshape(128, -1))
    m["gain"] = np.ascontiguousarray(np.broadcast_to(inp["mlstm_norm_g"].reshape(1, -1), (128, NL * 768)))
    m["bif"] = np.ascontiguousarray(np.broadcast_to(inp["b_gate_if"].reshape(1, -1), (128, NL * 8)))
    m["sC"] = np.ascontiguousarray(inp["state_mlstm_C"][:, s0:s1])
    m["snT"] = np.ascontiguousarray(inp["state_mlstm_n"][:, s0:s1].transpose(0, 2, 3, 1))
    m["sm"] = np.ascontiguousarray(inp["state_mlstm_m"][:, s0:s1])
    m["scT"] = np.ascontiguousarray(inp["state_conv"][:, s0:s1].transpose(0, 2, 3, 1))
    for gi, nm in enumerate(("cache_attn_kv_w128", "cache_attn_kv_w512", "cache_attn_kv_w2048")):
        a = inp[nm][:, s0:s1]
        m[f"ck{gi}"] = np.ascontiguousarray(a.reshape(NL, NS, a.shape[2], 512))
    return m


def build_program(npass=NPASS, nl=NL):
    nc = bass.Bass("TRN2", target_bir_lowering=False)
    k = K(nc, npass, nl)
    k.build()
    return nc, k


def assemble(res, ncores=8):
    f = np.float32
    y_prompt = np.zeros((4, SEQ, D), f); y_sample = np.zeros((128, 1, D), f)
    p_C = np.zeros((NL, 4, 4, 96, 192), f); p_n = np.zeros((NL, 4, 4, 96), f); p_m = np.zeros((NL, 4, 4), f)
    p_conv = np.zeros((NL, 4, 2, 512), f)
    p_kv = [np.zeros((NL, 4, w, 2, 4, 64), f) for w in (128, 512, 2048)]
    s_C = np.zeros((NL, 128, 4, 96, 192), f); s_n = np.zeros((NL, 128, 4, 96), f); s_m = np.zeros((NL, 128, 4), f)
    s_conv = np.zeros((NL, 128, 2, 512), f)
    s_kv = [np.zeros((NL, 128, 1, 2, 4, 64), f) for _ in range(3)]
    for c in range(ncores):
        r = res[c]
        s0, s1 = c * NS, (c + 1) * NS
        if c < 4:
            b = c
            y_prompt[b] = r["o_yT"].T
            p_C[:, b] = r["o_pC"]; p_n[:, b] = r["o_pn"]; p_m[:, b] = r["o_pm"]
            p_conv[:, b] = r["o_pconvT"].transpose(0, 2, 1)
            for g in range(3):
                w = (128, 512, 2048)[g]
                p_kv[g][:, b, :, 0] = r[f"o_pk{g}"].transpose(0, 2, 1).reshape(NL, w, 4, 64)
                p_kv[g][:, b, :, 1] = r[f"o_pv{g}"].reshape(NL, w, 4, 64)
        y_sample[s0:s1, 0] = r["o_ysT"].T
        s_C[:, s0:s1] = r["o_sC"]; s_n[:, s0:s1] = r["o_snT"].transpose(0, 3, 1, 2); s_m[:, s0:s1] = r["o_sm"]
        s_conv[:, s0:s1] = r["o_sconvT"].transpose(0, 3, 1, 2)
        kT = r["o_skT"]; vT = r["o_svT"]
        for g in range(3):
            s_kv[g][:, s0:s1, 0, 0] = kT[:, g * 256:(g + 1) * 256].transpose(0, 2, 1).reshape(NL, NS, 4, 64)
            s_kv[g][:, s0:s1, 0, 1] = vT[:, g * 256:(g + 1) * 256].transpose(0, 2, 1).reshape(NL, NS, 4, 64)
    return (y_prompt, y_sample, p_C, p_n, p_m, p_conv, p_kv[0], p_kv[1], p_kv[2],
            s_C, s_n, s_m, s_conv, s_kv[0], s_kv[1], s_kv[2])


def kernel(**inp):
    inp = {k: np.asarray(v) for k, v in inp.items()}
    wts = np.stack([host_weights(l, inp["w_in"], inp["w_up_mlstm"], inp["w_up_conv"], inp["w_up_attn"], inp["w_o"],
                                 inp["w_ffn_in"], inp["w_ffn_out"]) for l in range(NL)])
    consts = host_consts()
    nc, _ = build_program()
    in_maps = [_prep_core(c, inp, wts, consts) for c in range(8)]
    res = run_bass_kernel_spmd(nc, in_maps, core_ids=list(range(8)))
    return assemble(res.results, 8)
```
